# Optimizing a Trainium2 kernel written in Bass

```python
import math
import jax, jax.numpy as jnp
from jax import lax
import numpy as np

D_MODEL = 1024
BATCH = 16
SEQ = 2048
DEPTH = 4

HEAD_DIM = 64
N_BRANCHES = 3
LRU_W = D_MODEL
LRU_BLOCKS = D_MODEL // HEAD_DIM
LRU_BLOCK_W = LRU_W // LRU_BLOCKS
CONV_WIDTH = 4
LRU_C = 8.0
SWA_HEADS = D_MODEL // HEAD_DIM
SWA_KV_HEADS = SWA_HEADS // 4
SWA_GROUP = SWA_HEADS // SWA_KV_HEADS
SWA_WINDOW = 128
DIL_HEADS = D_MODEL // HEAD_DIM
DIL_CONFIGS = ((128, 1), (512, 4), (2048, 16))
FF_HIDDEN = int(math.ceil(8 * D_MODEL / 3 / 256) * 256)
DEEPNORM_ALPHA = (2.0 * DEPTH) ** 0.25
DEEPNORM_BETA = (8.0 * DEPTH) ** -0.25
LN_EPS = 1e-5
NEG_INF = -1e30
IN_WIDTHS = (LRU_W, LRU_W,
             SWA_HEADS * HEAD_DIM, SWA_KV_HEADS * HEAD_DIM, SWA_KV_HEADS * HEAD_DIM,
             DIL_HEADS * HEAD_DIM, DIL_HEADS * HEAD_DIM, DIL_HEADS * HEAD_DIM,
             N_BRANCHES * D_MODEL)
IN_WIDTH = sum(IN_WIDTHS)
BRANCH_W = D_MODEL

kernel_name = "hybrid_rglru_swa_sink_dilated_deepnorm"


def layer_norm(x, g, b):
    xf = x.astype(jnp.float32)
    mu = jnp.mean(xf, axis=-1, keepdims=True)
    var = jnp.mean(jnp.square(xf - mu), axis=-1, keepdims=True)
    y = (xf - mu) * lax.rsqrt(var + LN_EPS)
    return (y * g.astype(jnp.float32) + b.astype(jnp.float32)).astype(x.dtype)


def banded_window_attention(q, k, v, window, sink=None):
    bsz, L, hk, g, dh = q.shape
    blk = window
    nb = -(-L // blk)
    pad = nb * blk - L
    if pad:
        q = jnp.pad(q, ((0, 0), (0, pad), (0, 0), (0, 0), (0, 0)))
        k = jnp.pad(k, ((0, 0), (0, pad), (0, 0), (0, 0)))
        v = jnp.pad(v, ((0, 0), (0, pad), (0, 0), (0, 0)))
    qb = q.reshape(bsz, nb, blk, hk, g, dh)
    kb = k.reshape(bsz, nb, blk, hk, dh)
    vb = v.reshape(bsz, nb, blk, hk, dh)

    def with_prev(t):
        prev = jnp.pad(t[:, :-1], ((0, 0), (1, 0), (0, 0), (0, 0), (0, 0)))
        return jnp.concatenate([prev, t], axis=2)

    kk = with_prev(kb)
    vv = with_prev(vb)
    s = jnp.einsum('bnqhgd,bnkhd->bhgnqk', qb, kk,
                   preferred_element_type=jnp.float32) * (dh ** -0.5)
    qi = jnp.arange(blk)[:, None]
    kj = jnp.arange(2 * blk)[None, :]
    rel = qi + blk - kj
    key_exists = (jnp.arange(nb)[:, None, None] > 0) | (kj[None] >= blk)
    mask = (rel >= 0)[None] & (rel <= window)[None] & key_exists
    s = jnp.where(mask, s, NEG_INF)
    m = jnp.max(s, axis=-1)
    if sink is not None:
        sink_b = sink.astype(jnp.float32).reshape(hk, g, 1, 1)
        m = jnp.maximum(m, sink_b)
    p = jnp.exp(s - m[..., None])
    denom = jnp.sum(p, axis=-1)
    if sink is not None:
        denom = denom + jnp.exp(sink_b - m)
    o = jnp.einsum('bhgnqk,bnkhd->bnqhgd', p.astype(vv.dtype), vv,
                   preferred_element_type=jnp.float32)
    den_t = jnp.transpose(denom, (0, 3, 4, 1, 2))
    lse_t = jnp.transpose(m + jnp.log(denom), (0, 3, 4, 1, 2))
    o = (o / den_t[..., None]).reshape(bsz, nb * blk, hk, g, dh)[:, :L]
    lse = lse_t.reshape(bsz, nb * blk, hk, g)[:, :L]
    return o.astype(q.dtype), lse


def dilated_attention(q, k, v):
    bsz, S, H, dh = q.shape
    outs, lses = [], []
    for window, dil in DIL_CONFIGS:
        Ls = S // dil

        def to_sub(t):
            return t.reshape(bsz, Ls, dil, H, dh).transpose(0, 2, 1, 3, 4).reshape(bsz * dil, Ls, H, dh)

        o, lse = banded_window_attention(to_sub(q)[:, :, :, None], to_sub(k), to_sub(v), window // dil)
        o = o[:, :, :, 0].reshape(bsz, dil, Ls, H, dh).transpose(0, 2, 1, 3, 4).reshape(bsz, S, H, dh)
        lse = lse[:, :, :, 0].reshape(bsz, dil, Ls, H).transpose(0, 2, 1, 3).reshape(bsz, S, H)
        outs.append(o)
        lses.append(lse)
    w = jax.nn.softmax(jnp.stack(lses, axis=0), axis=0)
    o = jnp.sum(w[..., None] * jnp.stack(outs, axis=0).astype(jnp.float32), axis=0)
    return o.astype(q.dtype)


def rglru_branch(xr, gate_in, conv_w, conv_b, w_rg, b_rg, w_ig, b_ig, lru_lambda):
    bsz, S, W = xr.shape
    xp = jnp.pad(xr, ((0, 0), (CONV_WIDTH - 1, 0), (0, 0)))
    xc = sum(xp[:, j:j + S] * conv_w[j] for j in range(CONV_WIDTH)) + conv_b
    xh = xc.reshape(bsz, S, LRU_BLOCKS, LRU_BLOCK_W)
    r = jax.nn.sigmoid(jnp.einsum('bshi,hij->bshj', xh, w_rg).reshape(bsz, S, W) + b_rg)
    i = jax.nn.sigmoid(jnp.einsum('bshi,hij->bshj', xh, w_ig).reshape(bsz, S, W) + b_ig)
    log_a = -LRU_C * r.astype(jnp.float32) * jax.nn.softplus(-lru_lambda.astype(jnp.float32))
    a = jnp.exp(log_a)
    mult = jnp.sqrt(-jnp.expm1(2.0 * log_a))
    b = mult * (i * xc).astype(jnp.float32)

    def combine(e1, e2):
        a1, b1 = e1
        a2, b2 = e2
        return a1 * a2, a2 * b1 + b2

    _, h = lax.associative_scan(combine, (a, b), axis=1)
    return (h.astype(xr.dtype) * jax.nn.gelu(gate_in))


def hybrid_mixer(x, w_in, conv_w, conv_b, w_rg, b_rg, w_ig, b_ig, lru_lambda, sinks, w_branch, w_out):
    bsz, S, D = x.shape
    proj = x @ w_in
    split_at = list(np.cumsum(IN_WIDTHS)[:-1])
    lru_x, lru_gate, qb, kb, vb, qc, kc, vc, gates = jnp.split(proj, split_at, axis=-1)
    y_a = rglru_branch(lru_x, lru_gate, conv_w, conv_b, w_rg, b_rg, w_ig, b_ig, lru_lambda)
    o_b, _ = banded_window_attention(qb.reshape(bsz, S, SWA_KV_HEADS, SWA_GROUP, HEAD_DIM),
                                     kb.reshape(bsz, S, SWA_KV_HEADS, HEAD_DIM),
                                     vb.reshape(bsz, S, SWA_KV_HEADS, HEAD_DIM),
                                     SWA_WINDOW, sinks)
    y_b = o_b.reshape(bsz, S, SWA_HEADS * HEAD_DIM)
    y_c = dilated_attention(qc.reshape(bsz, S, DIL_HEADS, HEAD_DIM),
                            kc.reshape(bsz, S, DIL_HEADS, HEAD_DIM),
                            vc.reshape(bsz, S, DIL_HEADS, HEAD_DIM)).reshape(bsz, S, DIL_HEADS * HEAD_DIM)
    ys = jnp.stack([y_a, y_b, y_c], axis=2)
    branch = jnp.einsum('bsnc,ncd->bsnd', ys, w_branch)
    merged = jnp.sum(jax.nn.sigmoid(gates.reshape(bsz, S, N_BRANCHES, D)) * branch, axis=2)
    return merged @ w_out


def swiglu(x, w_ffn_in, w_ffn_out):
    h1, h3 = jnp.split(x @ w_ffn_in, 2, axis=-1)
    return (jax.nn.silu(h1) * h3) @ w_ffn_out


def setup_inputs(seed: int = 0) -> dict:
    key = jax.random.key(seed)
    ks = jax.random.split(key, 20)
    f32 = jnp.float32

    def nrm(k, shape, scale):
        return jax.random.normal(k, shape, f32) * scale

    u = jax.random.uniform(ks[7], (DEPTH, LRU_W), f32, 0.9, 0.999)
    return {
        "x": jax.random.normal(ks[0], (BATCH, SEQ, D_MODEL), f32),
        "w_in": nrm(ks[1], (DEPTH, D_MODEL, IN_WIDTH), D_MODEL ** -0.5),
        "conv_w": nrm(ks[2], (DEPTH, CONV_WIDTH, LRU_W), CONV_WIDTH ** -0.5),
        "conv_b": nrm(ks[3], (DEPTH, LRU_W), 0.02),
        "w_rg": nrm(ks[4], (DEPTH, LRU_BLOCKS, LRU_BLOCK_W, LRU_BLOCK_W), LRU_BLOCK_W ** -0.5),
        "b_rg": nrm(ks[5], (DEPTH, LRU_W), 0.02),
        "w_ig": nrm(ks[6], (DEPTH, LRU_BLOCKS, LRU_BLOCK_W, LRU_BLOCK_W), LRU_BLOCK_W ** -0.5),
        "b_ig": nrm(ks[8], (DEPTH, LRU_W), 0.02),
        "lru_lambda": jnp.log(u / (1.0 - u)),
        "sinks": nrm(ks[9], (DEPTH, SWA_HEADS), 0.5),
        "w_branch": nrm(ks[10], (DEPTH, N_BRANCHES, BRANCH_W, D_MODEL), DEEPNORM_BETA * BRANCH_W ** -0.5),
        "w_out": nrm(ks[11], (DEPTH, D_MODEL, D_MODEL), DEEPNORM_BETA * D_MODEL ** -0.5),
        "ln1_g": 1.0 + nrm(ks[12], (DEPTH, D_MODEL), 0.02),
        "ln1_b": nrm(ks[13], (DEPTH, D_MODEL), 0.02),
        "w_ffn_in": nrm(ks[14], (DEPTH, D_MODEL, 2 * FF_HIDDEN), DEEPNORM_BETA * D_MODEL ** -0.5),
        "w_ffn_out": nrm(ks[15], (DEPTH, FF_HIDDEN, D_MODEL), DEEPNORM_BETA * FF_HIDDEN ** -0.5),
        "ln2_g": 1.0 + nrm(ks[16], (DEPTH, D_MODEL), 0.02),
        "ln2_b": nrm(ks[17], (DEPTH, D_MODEL), 0.02),
    }


def reference(x, w_in, conv_w, conv_b, w_rg, b_rg, w_ig, b_ig, lru_lambda, sinks, w_branch, w_out,
              ln1_g, ln1_b, w_ffn_in, w_ffn_out, ln2_g, ln2_b):
    for l in range(DEPTH):
        mix = hybrid_mixer(x, w_in[l], conv_w[l], conv_b[l], w_rg[l], b_rg[l], w_ig[l], b_ig[l],
                           lru_lambda[l], sinks[l], w_branch[l], w_out[l])
        x = layer_norm(DEEPNORM_ALPHA * x + mix, ln1_g[l], ln1_b[l])
        ffn = swiglu(x, w_ffn_in[l], w_ffn_out[l])
        x = layer_norm(DEEPNORM_ALPHA * x + ffn, ln2_g[l], ln2_b[l])
    return x
```

```python
import numpy as np
import concourse.bass as bass
import concourse.mybir as mybir
from concourse.bass_utils import run_bass_kernel_spmd

F32 = mybir.dt.float32
BF16 = mybir.dt.bfloat16
AF = mybir.ActivationFunctionType
ALU = mybir.AluOpType

D = 1024
T = 2048
DEPTH = 4
KC = 8
TT = 256
NTL = T // TT
FFH = 2816
NJ = FFH // 128
NSEQ = 2
NCORES = 8
ALPHA = float((2.0 * DEPTH) ** 0.25)
LN_EPS = 1e-5
NS = 3
NTMP = 12
TMPW = 264
NPC = 120

FUSED = False


def _chunked(wsub):
    n = wsub.shape[1]
    return np.ascontiguousarray(wsub.reshape(KC, 128, n).transpose(1, 0, 2).reshape(128, KC * n))


def _tile_table():
    names = []
    for kv in range(4):
        names.append((f"swa{kv}", 8 * 448))
    for g in range(8):
        names.append((f"dil{g}", 8 * 384))
    names.append(("bd", 2048))
    for cp in range(4):
        names.append((f"lru{cp}", 4096))
    for c in range(8):
        names.append((f"gate{c}", 3072))
        names.append((f"br{c}", 3072))
    for o in range(2):
        names.append((f"out{o}", 4096))
    for jp in range(11):
        names.append((f"ffi{jp}", 4096))
    for c in range(8):
        names.append((f"ffo{c}", 2816))
    table = {}
    off = 0
    for n, s in names:
        nb = (s + 511) // 512
        table[n] = (off, s, nb)
        off += nb
    return table, off


TILES, BLOB_LEN = _tile_table()


def _build_blob(l, w_in, w_rg, w_ig, w_branch, w_out, w_ffn_in, w_ffn_out):
    blob = np.zeros((BLOB_LEN, 128, 512), np.float32)
    wi = w_in[l]

    def put(name, arr):
        off, size, nb = TILES[name]
        assert arr.shape == (128, size), (name, arr.shape, size)
        for b in range(nb):
            w = min(512, size - 512 * b)
            blob[off + b, :, 0:w] = arr[:, 512 * b:512 * b + w]

    for kv in range(4):
        cols = np.concatenate([
            np.arange(2048 + 256 * kv, 2048 + 256 * kv + 256),
            np.arange(3072 + 64 * kv, 3072 + 64 * kv + 64),
            np.arange(3072 + 64 * kv, 3072 + 64 * kv + 64),
            np.arange(3328 + 64 * kv, 3328 + 64 * kv + 64)])
        put(f"swa{kv}", _chunked(wi[:, cols]))
    for g in range(8):
        cols = np.concatenate([np.arange(3584 + 128 * g, 3584 + 128 * g + 128),
                               np.arange(4608 + 128 * g, 4608 + 128 * g + 128),
                               np.arange(5632 + 128 * g, 5632 + 128 * g + 128)])
        put(f"dil{g}", _chunked(wi[:, cols]))
    bd = np.zeros((128, 2, 8, 128), np.float32)
    for gi, wg in enumerate((w_rg[l], w_ig[l])):
        for c in range(8):
            for b in range(2):
                bd[64 * b:64 * b + 64, gi, c, 64 * b:64 * b + 64] = wg[2 * c + b]
    put("bd", bd.reshape(128, 2048))
    for cp in range(4):
        cols = []
        for c in (2 * cp, 2 * cp + 1):
            cols.append(np.arange(128 * c, 128 * c + 128))
            cols.append(np.arange(1024 + 128 * c, 1024 + 128 * c + 128))
        put(f"lru{cp}", _chunked(wi[:, np.concatenate(cols)]))
    for c in range(8):
        cols = np.concatenate([np.arange(6656 + 1024 * n + 128 * c, 6656 + 1024 * n + 128 * c + 128) for n in range(3)])
        put(f"gate{c}", _chunked(wi[:, cols]))
        br = np.concatenate([w_branch[l, n][:, 128 * c:128 * c + 128] for n in range(3)], axis=1)
        put(f"br{c}", _chunked(br))
    for o in range(2):
        put(f"out{o}", _chunked(w_out[l][:, 512 * o:512 * o + 512]))
    wf = w_ffn_in[l]
    for jp in range(11):
        cols = []
        for j in (2 * jp, 2 * jp + 1):
            cols.append(np.arange(128 * j, 128 * j + 128))
            cols.append(np.arange(FFH + 128 * j, FFH + 128 * j + 128))
        put(f"ffi{jp}", _chunked(wf[:, np.concatenate(cols)]))
    wo = w_ffn_out[l]
    for c in range(8):
        sub = wo[:, 128 * c:128 * c + 128].reshape(NJ, 128, 128).transpose(1, 0, 2).reshape(128, NJ * 128)
        put(f"ffo{c}", np.ascontiguousarray(sub))
    return blob


def _build_pcol(l, conv_w, conv_b, b_rg, b_ig, lru_lambda, sinks, ln1_g, ln1_b, ln2_g, ln2_b):
    pc = np.zeros((128, NPC), np.float32)

    def col(v):
        return v.reshape(8, 128).T

    for j in range(4):
        pc[:, 8 * j:8 * j + 8] = col(conv_w[l, j])
    pc[:, 32:40] = col(conv_b[l])
    pc[:, 40:48] = col(b_rg[l])
    pc[:, 48:56] = col(b_ig[l])
    pc[:, 56:64] = col(lru_lambda[l])
    pc[:, 64:72] = col(ln1_g[l])
    pc[:, 72:80] = col(ln1_b[l])
    pc[:, 80:88] = col(ln2_g[l])
    pc[:, 88:96] = col(ln2_b[l])
    pc[:, 104:120] = np.broadcast_to(sinks[l][None, :], (128, 16))
    return pc


def _build_masks():
    ki = np.arange(128)[:, None]
    j = np.arange(19 * 128)[None, :]
    dist = 128 * (j // 128 - 3) + (j % 128) - ki
    m = ((dist >= 0) & (dist <= 128)).astype(np.float32)
    m += ((dist >= 0) & (dist % 4 == 0) & (dist <= 512)).astype(np.float32)
    m += ((dist >= 0) & (dist % 16 == 0) & (dist <= 2048)).astype(np.float32)
    j2 = np.arange(8 * 128)[None, :]
    dist2 = 128 * (j2 // 128 - 3) + (j2 % 128) - ki
    m2 = ((dist2 >= 0) & (dist2 <= 128)).astype(np.float32)
    mp = np.zeros((128, 2560), np.float32)
    mp[:, :19 * 128] = m
    mb = np.ascontiguousarray(mp.reshape(128, 5, 512).transpose(1, 0, 2))
    m2b = np.ascontiguousarray(m2.reshape(128, 2, 512).transpose(1, 0, 2))
    return mb, m2b


class Atom:
    __slots__ = ("w", "r")

    def __init__(self):
        self.w = None
        self.r = []


class Op:
    __slots__ = ("eng", "fn", "deps", "sig", "sigval", "isdma", "semkey", "semval")


class Prog:
    ENGS = ("pe", "act", "dve", "pool", "sp")

    def __init__(self):
        self.ops = {e: [] for e in self.ENGS}
        self.dma_count = {}

    def _add(self, eng, fn, reads, writes, isdma=False, semkey=None):
        op = Op()
        op.eng = eng
        op.fn = fn
        op.sig = False
        op.sigval = 0
        op.isdma = isdma
        op.semkey = semkey
        op.semval = 0
        deps = {}
        for a in reads:
            w = a.w
            if w is None:
                continue
            if w.isdma or isdma or w.eng != eng or eng != "pe":
                deps[id(w)] = w
        for a in writes:
            w = a.w
            if w is not None and (w.isdma or isdma or w.eng != eng):
                deps[id(w)] = w
            for r in a.r:
                if r.isdma or isdma or r.eng != eng:
                    deps[id(r)] = r
        op.deps = list(deps.values())
        for d in op.deps:
            if not d.isdma:
                d.sig = True
        for a in reads:
            a.r.append(op)
        for a in writes:
            a.w = op
            a.r = []
        if isdma:
            n = self.dma_count.get(semkey, 0) + 16
            self.dma_count[semkey] = n
            op.semval = n
        self.ops[eng].append(op)
        return op

    def op(self, eng, fn, reads=(), writes=()):
        return self._add(eng, fn, reads, writes)

    def dma(self, queue, fn, semkey, reads=(), writes=()):
        return self._add(queue, fn, reads, writes, isdma=True, semkey=semkey)


def _flat(*items):
    out = []
    for it in items:
        if isinstance(it, Atom):
            out.append(it)
        else:
            out.extend(_flat(*it))
    return out


def build_program(n_layers, last_flags):
    nc = bass.Bass("TRN2", target_bir_lowering=False)
    P = Prog()

    xT = nc.dram_tensor("xT", [NSEQ, D, T], F32, kind="ExternalInput").ap()
    blob = nc.dram_tensor("blob", [n_layers, BLOB_LEN, 128, 512], F32, kind="ExternalInput").ap()
    pcol = nc.dram_tensor("pcol", [128, n_layers * NPC], F32, kind="ExternalInput").ap()
    maskd_d = nc.dram_tensor("maskd", [5, 128, 512], F32, kind="ExternalInput").ap()
    masks_d = nc.dram_tensor("masks", [2, 128, 512], F32, kind="ExternalInput").ap()
    oT = nc.dram_tensor("oT", [NSEQ, D, T], F32, kind="ExternalOutput").ap()
    import os as _os2
    KDUMP = _os2.environ.get("KDUMP", "") == "1"
    import itertools as _it
    _dbgc = _it.count()
    if KDUMP:
        dbg_yb = nc.dram_tensor("dbg_yb", [128, KC, T], BF16, kind="ExternalOutput").ap()
        dbg_yc = nc.dram_tensor("dbg_yc", [128, KC, T], BF16, kind="ExternalOutput").ap()
        dbg_ya = nc.dram_tensor("dbg_ya", [128, KC, T], BF16, kind="ExternalOutput").ap()
        dbg_mg = nc.dram_tensor("dbg_mg", [128, KC, T], BF16, kind="ExternalOutput").ap()
        dbg_xh = nc.dram_tensor("dbg_xh", [128, KC, T], BF16, kind="ExternalOutput").ap()
        dbg_x1 = nc.dram_tensor("dbg_x1", [128, KC, T], BF16, kind="ExternalOutput").ap()

    from contextlib import ExitStack
    es = ExitStack()

    def sb(name, shape, dt):
        return es.enter_context(nc.sbuf_tensor(name, shape, dt))

    XH = sb("XH", [128, KC, T], BF16)
    XL = sb("XL", [128, KC, T], BF16)
    YB = sb("YB", [128, KC, T], BF16)
    YC = sb("YC", [128, KC, T], BF16)
    WS = [sb(f"WS{i}", [128, 4096], BF16) for i in range(NS)]
    MASKD = sb("MASKD", [128, 2560], BF16)
    MASKS = sb("MASKS", [128, 8 * 128], BF16)
    BD = sb("BD", [128, 2048], BF16)
    SCR = sb("SCR", [128, 10240], BF16)
    TMP = sb("TMP", [128, NTMP * TMPW], F32)
    TB = sb("TB", [128, 4 * 256], BF16)
    HALF = sb("HALF", [128, 256], F32)
    NHALF = sb("NHALF", [128, 256], F32)
    PCOL = sb("PCOL", [128, n_layers * NPC], F32)
    DER = sb("DER", [128, 64], F32)
    ONES32 = sb("ONES32", [128, 128], F32)
    CARRY = sb("CARRY", [128, 8, 3], F32)
    HC = sb("HC", [128, 8], F32)
    PS = [es.enter_context(nc.psum_tensor(f"PS{i}", [128, 512], F32)) for i in range(8)]


    def scr_bf(off_el, n_el):
        return SCR[:, off_el:off_el + n_el]

    scr_atoms = [Atom() for _ in range(20)]

    def scr_at(off_el, n_el):
        return scr_atoms[off_el // 512:(off_el + n_el + 511) // 512]

    XA = [[Atom() for _ in range(NTL)] for _ in range(KC)]
    YBA = [[Atom() for _ in range(NTL)] for _ in range(KC)]
    YCA = [[Atom() for _ in range(NTL)] for _ in range(KC)]
    WSB = [[Atom() for _ in range(8)] for _ in range(NS)]
    WSA = WSB
    TMPA = [Atom() for _ in range(NTMP)]
    TBA = [Atom() for _ in range(4)]
    PSA = [Atom() for _ in range(8)]
    CONSTA = Atom()
    MASKA = [Atom() for _ in range(5)]
    MASKA2 = [Atom() for _ in range(2)]
    BDB = [Atom() for _ in range(4)]
    BDA = BDB
    DERA = Atom()
    CARRYA = [Atom() for _ in range(8)]
    HCA = [Atom() for _ in range(8)]

    def xa(cs, t0, t1):
        return [XA[c][i] for c in cs for i in range(t0 // TT, (t1 + TT - 1) // TT)]

    def ya(A, cs, t0, t1):
        return [A[c][i] for c in cs for i in range(t0 // TT, (t1 + TT - 1) // TT)]

    def bank_atoms(b):
        return [PSA[b]]

    def PSH(hb):
        return PS[hb][:, 0:256]

    st = {"w": 0, "tmp": 0, "tb": 0, "hb": 0}

    def tmp():
        i = st["tmp"] % (NTMP - 2)
        st["tmp"] += 1
        return TMP[:, i * TMPW:(i + 1) * TMPW], TMPA[i]

    def tmp_fixed(i):
        return TMP[:, i * TMPW:(i + 1) * TMPW], TMPA[i]

    def tmpb():
        i = st["tb"] % 4
        st["tb"] += 1
        return TB[:, i * 256:(i + 1) * 256], TBA[i]

    def next_hb():
        i = st["hb"] % 6
        st["hb"] += 1
        return i

    def atmp(a):
        return TMP[:, 2 * a * TMPW:2 * a * TMPW + 512], [TMPA[2 * a], TMPA[2 * a + 1]]

    def pcv(l, k):
        return PCOL[:, l * NPC + k:l * NPC + k + 1]

    def cast_dma(dst2d, src_blk, nblk, key, atoms):
        grp = []
        for b in range(nblk):
            d_ = dst2d[:, 512 * b:512 * b + 512]
            s_ = src_blk(b)
            grp.append(P.dma("pool", lambda e, d_=d_, s_=s_: e.dma_start(out=d_, in_=s_), key, reads=[],
                             writes=([atoms[b % len(atoms)]] if atoms else [])))
        tot = P.dma_count[key]
        for o_ in grp:
            o_.semval = tot

    def load_w(l, name):
        s = st["w"] % NS
        st["w"] += 1
        off, size, nb = TILES[name]
        cast_dma(WS[s], lambda b: blob[l, off + b, :, :], nb, f"ws{s}", WSB[s])
        return s

    import os as _os
    SK = _os.environ.get("KSKIP", "")
    if "m" not in SK:
      cast_dma(MASKD, lambda b: maskd_d[b, :, :], 5, "const", MASKA)
    if "n" not in SK:
      cast_dma(MASKS, lambda b: masks_d[b, :, :], 2, "const", MASKA2)
    if "p" not in SK:
      P.dma("sp", lambda e: e.dma_start(out=PCOL[:], in_=pcol[:, :]), "consth", writes=[CONSTA])
    if "z" not in SK:
      P.op("dve", lambda e: e.memset(ONES32[:], 1.0), writes=[CONSTA])
      P.op("dve", lambda e: e.memset(HALF[:], 0.5), writes=[CONSTA])
      P.op("dve", lambda e: e.memset(NHALF[:], -0.5), writes=[CONSTA])

    xs_ctr = [0]

    def load_x(s):
        for c in range(KC):
            for q in range(4):
                a = xs_ctr[0] % 2
                xs_ctr[0] += 1
                stg, stga = atmp(a)
                src = xT[s, c * 128:(c + 1) * 128, q * 512:(q + 1) * 512]
                P.dma("sp", lambda e, stg=stg, src=src: e.dma_start(out=stg, in_=src), f"xs{a}",
                      reads=[], writes=stga)
                hi = XH[:, c, q * 512:(q + 1) * 512]
                lo = XL[:, c, q * 512:(q + 1) * 512]
                xat = xa([c], q * 512, (q + 1) * 512)
                P.op("act", lambda e, hi=hi, stg=stg: e.activation(out=hi, in_=stg, func=AF.Copy),
                     reads=stga, writes=xat)
                P.op("dve", lambda e, lo=lo, hi=hi, stg=stg: e.tensor_tensor(out=lo, in0=stg, in1=hi, op=ALU.subtract),
                     reads=stga + xat, writes=xat)

    def layer_consts(l):
        lam = PCOL[:, l * NPC + 56:l * NPC + 64]
        P.op("act", lambda e: e.activation(out=DER[:, 0:8], in_=lam, func=AF.Exp, scale=-1.0),
             reads=[CONSTA], writes=[DERA])
        P.op("act", lambda e: e.activation(out=DER[:, 8:16], in_=DER[:, 0:8], func=AF.Ln, bias=1.0),
             reads=[DERA], writes=[DERA])
        P.op("act", lambda e: e.activation(out=DER[:, 48:64], in_=PCOL[:, l * NPC + 104:l * NPC + 120], func=AF.Exp),
             reads=[CONSTA], writes=[DERA])
        P.op("dve", lambda e: e.tensor_scalar(out=DER[:, 16:24], in0=DER[:, 8:16], scalar1=-8.0, scalar2=None, op0=ALU.mult),
             reads=[DERA], writes=[DERA])
        P.op("dve", lambda e: e.tensor_scalar(out=DER[:, 24:32], in0=DER[:, 8:16], scalar1=-4.0, scalar2=None, op0=ALU.mult),
             reads=[DERA], writes=[DERA])
        P.op("dve", lambda e: e.tensor_scalar(out=DER[:, 32:48], in0=PCOL[:, l * NPC + 40:l * NPC + 56], scalar1=0.5,
                                              scalar2=None, op0=ALU.mult),
             reads=[DERA, CONSTA], writes=[DERA])
        off, size, nb = TILES["bd"]
        cast_dma(BD, lambda b: blob[l, off + b, :, :], nb, "bd", BDB)

    SBANKS = [0, 1, 2]
    OBANKS = [3, 4]
    PBANKS = [5, 6, 7]
    pa = {"s": 0, "o": 0, "p": 0, "pt": 0, "at": 0}

    def attention_units(heads, mask, mask_w, kb_lo_fn, ptile_off):
        units = []
        for h in heads:
            for Qi in range(4):
                lo = kb_lo_fn(Qi)
                hi = 4 * Qi + 3
                for kb in range(lo, hi + 1):
                    units.append((h, Qi, kb, kb == lo, kb == hi))
        n = len(units)
        LA = 2
        info = [None] * n
        for i in range(n + LA):
            if i < n:
                h, Qi, kb, first, last = units[i]
                sbk = SBANKS[pa["s"] % 3]
                pa["s"] += 1
                pt = pa["pt"] % 4
                pa["pt"] += 1
                Pt = scr_bf(ptile_off + pt * 512, 512)
                Pa = scr_at(ptile_off + pt * 512, 512)
                kT = h["kT"](kb)
                qT = h["qT"](Qi)
                P.op("pe", lambda e, sbk=sbk, kT=kT, qT=qT: e.matmul(PS[sbk][:, :], lhsT=kT, rhs=qT, start=True, stop=True),
                     reads=h["k_atoms"](kb) + h["q_atoms"](Qi), writes=bank_atoms(sbk))
                P.op("act", lambda e, sbk=sbk, Pt=Pt: e.activation(out=Pt, in_=PS[sbk][:, :], func=AF.Exp),
                     reads=bank_atoms(sbk), writes=Pa)
                j0 = (4 * Qi - kb + 3) * 128
                assert 0 <= j0 and j0 + 512 <= mask_w
                mk = mask[:, j0:j0 + 512]
                meng = "dve" if (i % 3) != 2 else "pool"
                P.op(meng, lambda e, Pt=Pt, mk=mk: e.tensor_tensor(out=Pt, in0=Pt, in1=mk, op=ALU.mult),
                     reads=Pa + MASKA + MASKA2, writes=Pa)
                info[i] = (Pt, Pa)
            k = i - LA
            if k >= 0:
                h, Qi, kb, first, last = units[k]
                Pt, Pa = info[k]
                if first:
                    pa["o"] += 1
                ob = OBANKS[pa["o"] % 2]
                v1 = h["v1"](kb)
                P.op("pe", lambda e, ob=ob, v1=v1, Pt=Pt, first=first, last=last:
                     e.matmul(PS[ob][:, :], lhsT=v1, rhs=Pt, start=first, stop=last),
                     reads=Pa + h["v_atoms"](kb), writes=bank_atoms(ob))
                if last:
                    a = pa["at"] % 2
                    pa["at"] += 1
                    ta, taa = atmp(a)
                    sink = h["sink"]
                    if sink is not None:
                        P.op("act", lambda e, ob=ob, ta=ta, sink=sink: e.activation(
                            out=ta[64:128, :], in_=PS[ob][64:128, :], func=AF.Ln, bias=sink),
                            reads=bank_atoms(ob) + [DERA], writes=taa)
                    else:
                        P.op("act", lambda e, ob=ob, ta=ta: e.activation(
                            out=ta[64:128, :], in_=PS[ob][64:128, :], func=AF.Ln),
                            reads=bank_atoms(ob), writes=taa)
                    P.op("act", lambda e, ta=ta: e.activation(out=ta[64:128, :], in_=ta[64:128, :], func=AF.Exp, scale=-1.0),
                         reads=taa, writes=taa)
                    yo = h["out"](Qi)
                    P.op("dve", lambda e, yo=yo, ob=ob, ta=ta: e.tensor_tensor(
                        out=yo, in0=PS[ob][0:64, :], in1=ta[64:128, :], op=ALU.mult),
                        reads=bank_atoms(ob) + taa, writes=h["out_atoms"](Qi))

    def proj_fm(s_slot, wv, ncols0, dst_fn, dst_atoms_fn, scale, eng):
        for Qi in range(4):
            bk = PBANKS[pa["p"] % 3]
            pa["p"] += 1

            def mm(e, bk=bk, Qi=Qi):
                r = None
                for kc in range(KC):
                    r = e.matmul(PS[bk][:, :], lhsT=wv[:, kc, ncols0:ncols0 + 128], rhs=XH[:, kc, Qi * 512:(Qi + 1) * 512],
                                 start=(kc == 0), stop=(kc == KC - 1))
                return r
            P.op("pe", mm, reads=WSA[s_slot] + xa(range(KC), Qi * 512, (Qi + 1) * 512), writes=bank_atoms(bk))
            dst = dst_fn(Qi)
            if eng == "act":
                P.op("act", lambda e, dst=dst, bk=bk: e.activation(out=dst, in_=PS[bk][:, :], func=AF.Copy, scale=scale),
                     reads=bank_atoms(bk), writes=dst_atoms_fn(Qi))
            else:
                P.op("dve", lambda e, dst=dst, bk=bk: e.tensor_copy(out=dst, in_=PS[bk][:, :]),
                     reads=bank_atoms(bk), writes=dst_atoms_fn(Qi))

    def phase_a_dil(l, g):
        s = load_w(l, f"dil{g}")
        wv = WS[s][:, 0:3072].rearrange("p (k n) -> p k n", n=384)
        QO, KO, VO, PO = 0, 2048, 4096, 8192
        proj_fm(s, wv, 0, lambda Qi: scr_bf(QO + Qi * 512, 512), lambda Qi: scr_at(QO + Qi * 512, 512), 0.125, "act")
        proj_fm(s, wv, 128, lambda Qi: scr_bf(KO + Qi * 512, 512), lambda Qi: scr_at(KO + Qi * 512, 512), 1.0, "dve")
        V1 = scr_bf(VO, 4096).rearrange("p (t h d) -> p t h d", h=2, d=128)
        V1a = scr_at(VO, 4096)
        P.op("dve", lambda e: e.memset(V1[:, :, :, 64:128], 1.0), writes=V1a)
        for tg in range(4):
            bk = PBANKS[pa["p"] % 3]
            pa["p"] += 1

            def mm(e, bk=bk, tg=tg):
                r = None
                for tb in range(4 * tg, 4 * tg + 4):
                    for kc in range(KC):
                        r = e.matmul(PS[bk][:, (tb % 4) * 128:(tb % 4) * 128 + 128], lhsT=XH[:, kc, tb * 128:(tb + 1) * 128],
                                     rhs=wv[:, kc, 256:384], start=(kc == 0), stop=(kc == KC - 1))
                return r
            P.op("pe", mm, reads=WSA[s] + xa(range(KC), tg * 512, (tg + 1) * 512), writes=bank_atoms(bk))
            dst = V1[:, 4 * tg:4 * tg + 4, :, 0:64]
            src = PS[bk][:, :].rearrange("p (t h d) -> p t h d", h=2, d=64)
            P.op("act", lambda e, dst=dst, src=src: e.activation(out=dst, in_=src, func=AF.Copy),
                 reads=bank_atoms(bk), writes=scr_at(VO + tg * 1024, 1024))
        heads = []
        for j in range(2):
            pb = 64 * j
            heads.append(dict(
                qT=lambda Qi, pb=pb: SCR[pb:pb + 64, QO + Qi * 512:QO + (Qi + 1) * 512],
                q_atoms=lambda Qi: scr_at(QO + Qi * 512, 512),
                kT=lambda kb, pb=pb: SCR[pb:pb + 64, KO + kb * 128:KO + (kb + 1) * 128],
                k_atoms=lambda kb: scr_at(KO + kb * 128, 128),
                v1=lambda kb, j=j: V1[:, kb, j, :],
                v_atoms=lambda kb: scr_at(VO + kb * 256, 256),
                sink=None,
                out=lambda Qi, pb=pb: YC[pb:pb + 64, g, Qi * 512:(Qi + 1) * 512],
                out_atoms=lambda Qi: ya(YCA, [g], Qi * 512, (Qi + 1) * 512)))
        attention_units(heads, MASKD, 19 * 128, lambda Qi: 0, PO)

    def phase_a_swa(l, kv):
        s = load_w(l, f"swa{kv}")
        wv = WS[s][:, 0:3584].rearrange("p (k n) -> p k n", n=448)
        QO, KO, VO, PO = 0, 4096, 6144, 8192
        for ch in range(2):
            proj_fm(s, wv, 128 * ch, lambda Qi, ch=ch: scr_bf(QO + ch * 2048 + Qi * 512, 512),
                    lambda Qi, ch=ch: scr_at(QO + ch * 2048 + Qi * 512, 512), 0.125, "act")
        proj_fm(s, wv, 256, lambda Qi: scr_bf(KO + Qi * 512, 512), lambda Qi: scr_at(KO + Qi * 512, 512), 1.0, "dve")
        V1 = scr_bf(VO, 2048).rearrange("p (t d) -> p t d", d=128)
        V1a = scr_at(VO, 2048)
        P.op("dve", lambda e: e.memset(V1[:, :, 64:128], 1.0), writes=V1a)
        for tg in range(2):
            bk = PBANKS[pa["p"] % 3]
            pa["p"] += 1

            def mm(e, bk=bk, tg=tg):
                r = None
                for tb in range(8 * tg, 8 * tg + 8):
                    for kc in range(KC):
                        r = e.matmul(PS[bk][:, (tb % 8) * 64:(tb % 8) * 64 + 64], lhsT=XH[:, kc, tb * 128:(tb + 1) * 128],
                                     rhs=wv[:, kc, 384:448], start=(kc == 0), stop=(kc == KC - 1))
                return r
            P.op("pe", mm, reads=WSA[s] + xa(range(KC), tg * 1024, (tg + 1) * 1024), writes=bank_atoms(bk))
            dst = V1[:, 8 * tg:8 * tg + 8, 0:64]
            src = PS[bk][:, :].rearrange("p (t d) -> p t d", d=64)
            P.op("act", lambda e, dst=dst, src=src: e.activation(out=dst, in_=src, func=AF.Copy),
                 reads=bank_atoms(bk), writes=scr_at(VO + tg * 1024, 1024))
        heads = []
        for j in range(4):
            pb = 64 * (j % 2)
            ch = j // 2
            hidx = 4 * kv + j
            cidx = 2 * kv + ch
            heads.append(dict(
                qT=lambda Qi, pb=pb, ch=ch: SCR[pb:pb + 64, QO + ch * 2048 + Qi * 512:QO + ch * 2048 + (Qi + 1) * 512],
                q_atoms=lambda Qi, ch=ch: scr_at(QO + ch * 2048 + Qi * 512, 512),
                kT=lambda kb, pb=pb: SCR[pb:pb + 64, KO + kb * 128:KO + (kb + 1) * 128],
                k_atoms=lambda kb: scr_at(KO + kb * 128, 128),
                v1=lambda kb: V1[:, kb, :],
                v_atoms=lambda kb: scr_at(VO + kb * 128, 128),
                sink=DER[64:128, 48 + hidx:48 + hidx + 1],
                out=lambda Qi, pb=pb, cidx=cidx: YB[pb:pb + 64, cidx, Qi * 512:(Qi + 1) * 512],
                out_atoms=lambda Qi, cidx=cidx: ya(YBA, [cidx], Qi * 512, (Qi + 1) * 512)))
        attention_units(heads, MASKS, 8 * 128, lambda Qi: max(0, 4 * Qi - 1), PO)

    HHO, YAO, MGO, OSO = 0, 5632, 7680, 9728
    Z1O, Z2O = 0, 5632

    def scr_f32(off_el, ncol):
        return SCR[:, off_el:off_el + 2 * ncol].bitcast(F32)

    SBANK1, SBANK2 = 6, 7

    def mm_group(hb, lhs_fn, rhs_fn, nk):
        def mm(e):
            r = None
            for k in range(nk):
                r = e.matmul(PSH(hb), lhsT=lhs_fn(k), rhs=rhs_fn(k), start=(k == 0), stop=(k == nk - 1))
            return r
        return mm

    def layer_norm_tile(l, i, zoff, gk, bk_, is_last_out, s):
        t0 = i * TT
        za = scr_at(zoff, 4096)
        mean, meana = tmp_fixed(NTMP - 2)
        P.op("act", lambda e: e.activation(out=mean[:, 0:256], in_=PS[SBANK1][:, 0:256], func=AF.Copy, scale=1.0 / D),
             reads=bank_atoms(SBANK1), writes=[meana])
        msq, msqa = tmp()
        P.op("dve", lambda e: e.scalar_tensor_tensor(out=msq[:, 0:256], in0=mean[:, 0:256], scalar=-1.0, in1=mean[:, 0:256],
                                                     op0=ALU.mult, op1=ALU.mult),
             reads=[meana], writes=[msqa])
        var, vara = tmp()
        P.op("dve", lambda e: e.scalar_tensor_tensor(out=var[:, 0:256], in0=PS[SBANK2][:, 0:256], scalar=1.0 / D,
                                                     in1=msq[:, 0:256], op0=ALU.mult, op1=ALU.add),
             reads=bank_atoms(SBANK2) + [msqa], writes=[vara])
        P.op("pool", lambda e: e.tensor_scalar(out=var[:, 0:256], in0=var[:, 0:256], scalar1=LN_EPS, scalar2=None, op0=ALU.add),
             reads=[vara], writes=[vara])
        rstd, rstda = tmp_fixed(NTMP - 1)
        P.op("pool", lambda e: e.tensor_tensor(out=rstd[:, 0:256], in0=var[:, 0:256], in1=NHALF[:, :], op=ALU.pow),
             reads=[vara, CONSTA], writes=[rstda])
        for c in range(KC):
            z = scr_f32(zoff + c * 512, 256)
            zc = scr_at(zoff + c * 512, 512)
            u, ua = tmp()
            P.op("dve", lambda e, u=u, z=z: e.tensor_tensor(out=u[:, 0:256], in0=z, in1=mean[:, 0:256], op=ALU.subtract),
                 reads=zc + [meana], writes=[ua])
            P.op("dve", lambda e, u=u: e.tensor_tensor(out=u[:, 0:256], in0=u[:, 0:256], in1=rstd[:, 0:256], op=ALU.mult),
                 reads=[ua, rstda], writes=[ua])
            gcol = pcv(l, gk + c)
            bcol = pcv(l, bk_ + c)
            if is_last_out:
                v = scr_f32(OSO, 256)
                va = scr_at(OSO, 512)
            else:
                v, va1 = tmp()
                v = v[:, 0:256]
                va = [va1]
            P.op("act", lambda e, v=v, u=u, gcol=gcol, bcol=bcol: e.activation(out=v, in_=u[:, 0:256], func=AF.Identity,
                                                                              scale=gcol, bias=bcol),
                 reads=[ua, CONSTA], writes=va)
            if is_last_out:
                dst = oT[s, c * 128:(c + 1) * 128, t0:t0 + TT]
                P.dma("sp", lambda e, v=v, dst=dst: e.dma_start(out=dst, in_=v), "os0", reads=va, writes=[])
            else:
                hi = XH[:, c, t0:t0 + TT]
                lo = XL[:, c, t0:t0 + TT]
                xat = xa([c], t0, t0 + TT)
                P.op("act", lambda e, hi=hi, v=v: e.activation(out=hi, in_=v, func=AF.Copy), reads=va, writes=xat)
                P.op("pool", lambda e, lo=lo, hi=hi, v=v: e.tensor_tensor(out=lo, in0=v, in1=hi, op=ALU.subtract),
                     reads=va + xat, writes=xat)

    def residual_and_stats(i, c, hz, zoff):
        t0 = i * TT
        xf, xfa = tmp()
        xat = xa([c], t0, t0 + TT)
        P.op("pool", lambda e, xf=xf: e.tensor_tensor(out=xf[:, 0:256], in0=XH[:, c, t0:t0 + TT], in1=XL[:, c, t0:t0 + TT],
                                                      op=ALU.add),
             reads=xat, writes=[xfa])
        z = scr_f32(zoff + c * 512, 256)
        zc = scr_at(zoff + c * 512, 512)
        P.op("dve", lambda e, xf=xf, z=z: e.scalar_tensor_tensor(out=z, in0=xf[:, 0:256], scalar=ALPHA, in1=PSH(hz),
                                                                 op0=ALU.mult, op1=ALU.add),
             reads=[xfa, PSA[hz]], writes=zc)
        sq, sqa = tmp()
        P.op("act", lambda e, sq=sq, z=z: e.activation(out=sq[:, 0:256], in_=z, func=AF.Square), reads=zc, writes=[sqa])
        P.op("pe", lambda e, z=z: e.matmul(PS[SBANK1][:, 0:256], lhsT=ONES32[:, :], rhs=z, start=(c == 0), stop=(c == KC - 1)),
             reads=zc + [CONSTA], writes=bank_atoms(SBANK1))
        P.op("pe", lambda e, sq=sq: e.matmul(PS[SBANK2][:, 0:256], lhsT=ONES32[:, :], rhs=sq[:, 0:256], start=(c == 0),
                                             stop=(c == KC - 1)),
             reads=[sqa, CONSTA], writes=bank_atoms(SBANK2))

    def phase_c_tile(l, i, is_last_out, s):
        t0 = i * TT
        XHt = lambda kc: XH[:, kc, t0:t0 + TT]
        xall = xa(range(KC), t0, t0 + TT)
        YAv = scr_bf(YAO, 2048).rearrange("p (c t) -> p c t", t=256)
        MGv = scr_bf(MGO, 2048).rearrange("p (c t) -> p c t", t=256)
        HHv = scr_bf(HHO, 5632).rearrange("p (j t) -> p j t", t=256)
        BDv = BD[:, :].rearrange("p (g c n) -> p g c n", g=2, c=8)
        for c in range(KC):
            if c % 2 == 0:
                sl = load_w(l, f"lru{c // 2}")
                w4 = WS[sl][:, 0:4096].rearrange("p (k n) -> p k n", n=512)
            base = (c % 2) * 256
            hx = next_hb()
            P.op("pe", mm_group(hx, lambda k, w4=w4, base=base: w4[:, k, base:base + 128], XHt, KC),
                 reads=WSA[sl] + xall, writes=[PSA[hx]])
            hg = next_hb()
            P.op("pe", mm_group(hg, lambda k, w4=w4, base=base: w4[:, k, base + 128:base + 256], XHt, KC),
                 reads=WSA[sl] + xall, writes=[PSA[hg]])
            lx, lxa = tmp()
            if i == 0:
                P.op("dve", lambda e, lx=lx: e.memset(lx[:, 0:3], 0.0), writes=[lxa])
            else:
                P.op("act", lambda e, lx=lx, c=c: e.activation(out=lx[:, 0:3], in_=CARRY[:, c, :], func=AF.Copy),
                     reads=[CARRYA[c]], writes=[lxa])
            P.op("act", lambda e, lx=lx, hx=hx: e.activation(out=lx[:, 3:259], in_=PSH(hx), func=AF.Copy),
                 reads=[PSA[hx]], writes=[lxa])
            P.op("act", lambda e, lx=lx, c=c: e.activation(out=CARRY[:, c, :], in_=lx[:, 256:259], func=AF.Copy),
                 reads=[lxa], writes=[CARRYA[c]])
            xc, xca = tmp()
            P.op("dve", lambda e, xc=xc, lx=lx, c=c: e.tensor_scalar(out=xc[:, 0:256], in0=lx[:, 3:259], scalar1=pcv(l, 24 + c),
                                                                     scalar2=pcv(l, 32 + c), op0=ALU.mult, op1=ALU.add),
                 reads=[lxa, CONSTA], writes=[xca])
            for jj in (2, 1, 0):
                P.op("dve", lambda e, xc=xc, lx=lx, c=c, jj=jj: e.scalar_tensor_tensor(
                    out=xc[:, 0:256], in0=lx[:, jj:jj + 256], scalar=pcv(l, 8 * jj + c), in1=xc[:, 0:256],
                    op0=ALU.mult, op1=ALU.add),
                    reads=[lxa, xca, CONSTA], writes=[xca])
            xcb, xcba = tmpb()
            P.op("act", lambda e, xcb=xcb, xc=xc: e.activation(out=xcb, in_=xc[:, 0:256], func=AF.Copy),
                 reads=[xca], writes=[xcba])
            hr = next_hb()
            P.op("pe", lambda e, hr=hr, c=c, xcb=xcb: e.matmul(PSH(hr), lhsT=BDv[:, 0, c, :], rhs=xcb, start=True, stop=True),
                 reads=BDA + [xcba], writes=[PSA[hr]])
            hi_ = next_hb()
            P.op("pe", lambda e, hi_=hi_, c=c, xcb=xcb: e.matmul(PSH(hi_), lhsT=BDv[:, 1, c, :], rhs=xcb, start=True, stop=True),
                 reads=BDA + [xcba], writes=[PSA[hi_]])
            tr, tra = tmp()
            P.op("act", lambda e, tr=tr, hr=hr, c=c: e.activation(out=tr[:, 0:256], in_=PSH(hr), func=AF.Tanh, scale=0.5,
                                                                 bias=DER[:, 32 + c:33 + c]),
                 reads=[PSA[hr], DERA], writes=[tra])
            ti, tia = tmp()
            P.op("act", lambda e, ti=ti, hi_=hi_, c=c: e.activation(out=ti[:, 0:256], in_=PSH(hi_), func=AF.Tanh, scale=0.5,
                                                                   bias=DER[:, 40 + c:41 + c]),
                 reads=[PSA[hi_], DERA], writes=[tia])
            av, ava = tmp()
            P.op("act", lambda e, av=av, tr=tr, c=c: e.activation(out=av[:, 0:256], in_=tr[:, 0:256], func=AF.Exp,
                                                                 scale=DER[:, 24 + c:25 + c], bias=DER[:, 24 + c:25 + c]),
                 reads=[tra, DERA], writes=[ava])
            a2, a2a = tmp()
            P.op("act", lambda e, a2=a2, tr=tr, c=c: e.activation(out=a2[:, 0:256], in_=tr[:, 0:256], func=AF.Exp,
                                                                 scale=DER[:, 16 + c:17 + c], bias=DER[:, 16 + c:17 + c]),
                 reads=[tra, DERA], writes=[a2a])
            P.op("pool", lambda e, a2=a2: e.tensor_scalar(out=a2[:, 0:256], in0=a2[:, 0:256], scalar1=-1.0, scalar2=1.0,
                                                          op0=ALU.mult, op1=ALU.add),
                 reads=[a2a], writes=[a2a])
            P.op("pool", lambda e, a2=a2: e.tensor_tensor(out=a2[:, 0:256], in0=a2[:, 0:256], in1=HALF[:, :], op=ALU.pow),
                 reads=[a2a, CONSTA], writes=[a2a])
            P.op("dve", lambda e, ti=ti, xc=xc: e.scalar_tensor_tensor(out=ti[:, 0:256], in0=ti[:, 0:256], scalar=1.0,
                                                                       in1=xc[:, 0:256], op0=ALU.add, op1=ALU.mult),
                 reads=[tia, xca], writes=[tia])
            P.op("dve", lambda e, ti=ti, a2=a2: e.scalar_tensor_tensor(out=ti[:, 0:256], in0=ti[:, 0:256], scalar=0.5,
                                                                       in1=a2[:, 0:256], op0=ALU.mult, op1=ALU.mult),
                 reads=[tia, a2a], writes=[tia])
            hh, hha = tmp()
            if i == 0:
                P.op("dve", lambda e, hh=hh, av=av, ti=ti: e.tensor_tensor_scan(
                    out=hh[:, 0:256], data0=av[:, 0:256], data1=ti[:, 0:256], initial=0.0, op0=ALU.mult, op1=ALU.add),
                    reads=[ava, tia], writes=[hha])
            else:
                P.op("dve", lambda e, hh=hh, av=av, ti=ti, c=c: e.tensor_tensor_scan(
                    out=hh[:, 0:256], data0=av[:, 0:256], data1=ti[:, 0:256], initial=HC[:, c:c + 1], op0=ALU.mult,
                    op1=ALU.add),
                    reads=[ava, tia, HCA[c]], writes=[hha])
            P.op("act", lambda e, hh=hh, c=c: e.activation(out=HC[:, c:c + 1], in_=hh[:, 255:256], func=AF.Copy),
                 reads=[hha], writes=[HCA[c]])
            xg, xga = tmp()
            P.op("act", lambda e, xg=xg, hg=hg: e.activation(out=xg[:, 0:256], in_=PSH(hg), func=AF.Copy),
                 reads=[PSA[hg]], writes=[xga])
            x2, x2a = tmp()
            P.op("pool", lambda e, x2=x2, xg=xg: e.tensor_tensor(out=x2[:, 0:256], in0=xg[:, 0:256], in1=xg[:, 0:256], op=ALU.mult),
                 reads=[xga], writes=[x2a])
            P.op("pool", lambda e, x2=x2: e.tensor_scalar(out=x2[:, 0:256], in0=x2[:, 0:256], scalar1=0.044715, scalar2=1.0,
                                                          op0=ALU.mult, op1=ALU.add),
                 reads=[x2a], writes=[x2a])
            P.op("pool", lambda e, x2=x2, xg=xg: e.tensor_tensor(out=x2[:, 0:256], in0=x2[:, 0:256], in1=xg[:, 0:256], op=ALU.mult),
                 reads=[x2a, xga], writes=[x2a])
            P.op("act", lambda e, x2=x2: e.activation(out=x2[:, 0:256], in_=x2[:, 0:256], func=AF.Tanh, scale=0.7978845608028654),
                 reads=[x2a], writes=[x2a])
            P.op("dve", lambda e, x2=x2, xg=xg: e.scalar_tensor_tensor(out=x2[:, 0:256], in0=x2[:, 0:256], scalar=1.0,
                                                                       in1=xg[:, 0:256], op0=ALU.add, op1=ALU.mult),
                 reads=[x2a, xga], writes=[x2a])
            P.op("dve", lambda e, x2=x2, hh=hh, c=c: e.scalar_tensor_tensor(out=YAv[:, c, :], in0=x2[:, 0:256], scalar=0.5,
                                                                            in1=hh[:, 0:256], op0=ALU.mult, op1=ALU.mult),
                 reads=[x2a, hha], writes=scr_at(YAO + c * 256, 256))
        Ys = [lambda k: YAv[:, k, :], lambda k: YB[:, k, t0:t0 + TT], lambda k: YC[:, k, t0:t0 + TT]]
        Yat = [scr_at(YAO, 2048), ya(YBA, range(KC), t0, t0 + TT), ya(YCA, range(KC), t0, t0 + TT)]
        for c in range(KC):
            sg = load_w(l, f"gate{c}")
            g3 = WS[sg][:, 0:3072].rearrange("p (k n) -> p k n", n=384)
            sbr = load_w(l, f"br{c}")
            b3 = WS[sbr][:, 0:3072].rearrange("p (k n) -> p k n", n=384)
            acc = None
            for n in range(3):
                hgn = next_hb()
                P.op("pe", mm_group(hgn, lambda k, g3=g3, n=n: g3[:, k, 128 * n:128 * n + 128], XHt, KC),
                     reads=WSA[sg] + xall, writes=[PSA[hgn]])
                hbn = next_hb()
                P.op("pe", mm_group(hbn, lambda k, b3=b3, n=n: b3[:, k, 128 * n:128 * n + 128], Ys[n], KC),
                     reads=WSA[sbr] + Yat[n], writes=[PSA[hbn]])
                sgm, sga = tmp()
                P.op("act", lambda e, sgm=sgm, hgn=hgn: e.activation(out=sgm[:, 0:256], in_=PSH(hgn), func=AF.Sigmoid),
                     reads=[PSA[hgn]], writes=[sga])
                P.op("dve", lambda e, sgm=sgm, hbn=hbn: e.tensor_tensor(out=sgm[:, 0:256], in0=sgm[:, 0:256], in1=PSH(hbn),
                                                                        op=ALU.mult),
                     reads=[sga, PSA[hbn]], writes=[sga])
                if n == 0:
                    acc, acca = sgm, sga
                elif n == 1:
                    P.op("pool", lambda e, acc=acc, sgm=sgm: e.tensor_tensor(out=acc[:, 0:256], in0=acc[:, 0:256],
                                                                             in1=sgm[:, 0:256], op=ALU.add),
                         reads=[acca, sga], writes=[acca])
                else:
                    P.op("pool", lambda e, acc=acc, sgm=sgm, c=c: e.tensor_tensor(out=MGv[:, c, :], in0=acc[:, 0:256],
                                                                                  in1=sgm[:, 0:256], op=ALU.add),
                         reads=[acca, sga], writes=scr_at(MGO + c * 256, 256))
        if KDUMP and s == 0 and l == 0:
            P.dma("sp", lambda e: e.dma_start(out=dbg_ya[:, :, t0:t0 + TT], in_=YAv), "dbg%d" % next(_dbgc), reads=scr_at(YAO, 2048), writes=[])
            P.dma("sp", lambda e: e.dma_start(out=dbg_mg[:, :, t0:t0 + TT], in_=MGv), "dbg%d" % next(_dbgc), reads=scr_at(MGO, 2048), writes=[])
        mga = scr_at(MGO, 2048)
        for o in range(2):
            so = load_w(l, f"out{o}")
            w4 = WS[so][:, 0:4096].rearrange("p (k n) -> p k n", n=512)
            for cc in range(4):
                c = 4 * o + cc
                hz = next_hb()
                P.op("pe", mm_group(hz, lambda k, w4=w4, cc=cc: w4[:, k, 128 * cc:128 * cc + 128], lambda k: MGv[:, k, :], KC),
                     reads=WSA[so] + mga, writes=[PSA[hz]])
                residual_and_stats(i, c, hz, Z1O)
        layer_norm_tile(l, i, Z1O, 64, 72, False, s)
        for jp in range(11):
            sf = load_w(l, f"ffi{jp}")
            w4 = WS[sf][:, 0:4096].rearrange("p (k n) -> p k n", n=512)
            for jj in range(2):
                j = 2 * jp + jj
                h1 = next_hb()
                P.op("pe", mm_group(h1, lambda k, w4=w4, jj=jj: w4[:, k, 256 * jj:256 * jj + 128], XHt, KC),
                     reads=WSA[sf] + xall, writes=[PSA[h1]])
                h3 = next_hb()
                P.op("pe", mm_group(h3, lambda k, w4=w4, jj=jj: w4[:, k, 256 * jj + 128:256 * jj + 256], XHt, KC),
                     reads=WSA[sf] + xall, writes=[PSA[h3]])
                sl_, sla = tmp()
                P.op("act", lambda e, sl_=sl_, h1=h1: e.activation(out=sl_[:, 0:256], in_=PSH(h1), func=AF.Silu),
                     reads=[PSA[h1]], writes=[sla])
                P.op("dve", lambda e, sl_=sl_, h3=h3, j=j: e.tensor_tensor(out=HHv[:, j, :], in0=sl_[:, 0:256], in1=PSH(h3),
                                                                          op=ALU.mult),
                     reads=[sla, PSA[h3]], writes=scr_at(HHO + j * 256, 256))
        hha_all = scr_at(HHO, 5632)
        for c in range(KC):
            so = load_w(l, f"ffo{c}")
            w22 = WS[so][:, 0:2816].rearrange("p (j n) -> p j n", n=128)
            hz = next_hb()
            P.op("pe", mm_group(hz, lambda k, w22=w22: w22[:, k, :], lambda k: HHv[:, k, :], NJ),
                 reads=WSA[so] + hha_all, writes=[PSA[hz]])
            residual_and_stats(i, c, hz, Z2O)
        layer_norm_tile(l, i, Z2O, 80, 88, is_last_out, s)

    import os
    PH = os.environ.get("KDEBUG", "xksdc")
    NSQ = int(os.environ.get("KNSEQ", NSEQ))
    for s in range(NSQ):
        if "x" in PH:
            load_x(s)
        for l in range(n_layers):
            if "k" in PH:
                layer_consts(l)
            if "s" in PH:
                for kv in range(int(os.environ.get("KNSWA", 4))):
                    phase_a_swa(l, kv)
            if "d" in PH:
                for g in range(int(os.environ.get("KNDIL", 8))):
                    phase_a_dil(l, g)
            if KDUMP and s == 0 and l == 0:
                P.dma("sp", lambda e: e.dma_start(out=dbg_yb[:, :, :], in_=YB[:, :, :]), "dbg",
                      reads=ya(YBA, range(KC), 0, T), writes=[])
                P.dma("sp", lambda e: e.dma_start(out=dbg_yc[:, :, :], in_=YC[:, :, :]), "dbg",
                      reads=ya(YCA, range(KC), 0, T), writes=[])
                P.dma("sp", lambda e: e.dma_start(out=dbg_xh[:, :, :], in_=XH[:, :, :]), "dbg",
                      reads=xa(range(KC), 0, T), writes=[])
            if "c" in PH:
                for i in range(int(os.environ.get("KNTL", NTL))):
                    phase_c_tile(l, i, last_flags[l], s)
                if KDUMP and s == 0 and l == 0:
                    P.dma("sp", lambda e: e.dma_start(out=dbg_x1[:, :, :], in_=XH[:, :, :]), "dbg%d" % next(_dbgc),
                          reads=xa(range(KC), 0, T), writes=[])

    total_const = P.dma_count.get("const", 0)
    for e_ in P.ENGS:
        for op in P.ops[e_]:
            if op.isdma and op.semkey == "const":
                op.semval = total_const
    for e_ in P.ENGS:
        cnt = 0
        for op in P.ops[e_]:
            if op.sig and not op.isdma:
                cnt += 1
                op.sigval = cnt
    sem_names = ["e_" + e_ for e_ in P.ENGS] + ["d_" + k for k in sorted(P.dma_count)]
    sems = {n: es.enter_context(nc.semaphore(n)) for n in sem_names}
    handles = {"pe": nc.tensor, "act": nc.scalar, "dve": nc.vector, "pool": nc.gpsimd, "sp": nc.sync}

    def emit(eng, e):
        waited = {}
        for op in P.ops[eng]:
            need = {}
            for d in op.deps:
                if d.isdma:
                    key, val = "d_" + d.semkey, d.semval
                else:
                    key, val = "e_" + d.eng, d.sigval
                if need.get(key, 0) < val:
                    need[key] = val
            for key, val in need.items():
                if waited.get(key, 0) < val:
                    e.wait_ge(sems[key], val)
                    waited[key] = val
            inst = op.fn(e)
            if op.isdma:
                inst.then_inc(sems["d_" + op.semkey], 16)
            elif op.sig:
                inst.then_inc(sems["e_" + eng], 1)
        if eng == "sp":
            for k_, v_ in sorted(P.dma_count.items()):
                e.wait_ge(sems["d_" + k_], v_)

    block = es.enter_context(nc.Block())

    @block.tensor
    def _(e):
        emit("pe", e)

    @block.scalar
    def _(e):
        emit("act", e)

    @block.vector
    def _(e):
        emit("dve", e)

    @block.gpsimd
    def _(e):
        emit("pool", e)

    @block.sync
    def _(e):
        emit("sp", e)

    es.close()
    stats = {k: len(v) for k, v in P.ops.items()}
    return nc, stats


_CACHE = {}


def _get_prog(n_layers, last_flags):
    key = (n_layers, tuple(last_flags))
    if key not in _CACHE:
        _CACHE[key] = build_program(n_layers, list(last_flags))
    return _CACHE[key][0]


def kernel(x, w_in, conv_w, conv_b, w_rg, b_rg, w_ig, b_ig, lru_lambda, sinks, w_branch, w_out,
           ln1_g, ln1_b, w_ffn_in, w_ffn_out, ln2_g, ln2_b):
    f = lambda a: np.asarray(a, dtype=np.float32)
    x = f(x); w_in = f(w_in); conv_w = f(conv_w); conv_b = f(conv_b); w_rg = f(w_rg); b_rg = f(b_rg)
    w_ig = f(w_ig); b_ig = f(b_ig); lru_lambda = f(lru_lambda); sinks = f(sinks); w_branch = f(w_branch)
    w_out = f(w_out); ln1_g = f(ln1_g); ln1_b = f(ln1_b); w_ffn_in = f(w_ffn_in); w_ffn_out = f(w_ffn_out)
    ln2_g = f(ln2_g); ln2_b = f(ln2_b)
    maskd, masks = _build_masks()
    blobs = [_build_blob(l, w_in, w_rg, w_ig, w_branch, w_out, w_ffn_in, w_ffn_out) for l in range(DEPTH)]
    pcols = [_build_pcol(l, conv_w, conv_b, b_rg, b_ig, lru_lambda, sinks, ln1_g, ln1_b, ln2_g, ln2_b)
             for l in range(DEPTH)]
    xT = np.ascontiguousarray(x.transpose(0, 2, 1))
    cur = [np.ascontiguousarray(xT[NSEQ * c:NSEQ * (c + 1)]) for c in range(NCORES)]
    if FUSED:
        nc = _get_prog(DEPTH, [False] * (DEPTH - 1) + [True])
        blob = np.ascontiguousarray(np.stack(blobs, 0))
        pc = np.ascontiguousarray(np.concatenate(pcols, axis=1))
        in_maps = [{"xT": cur[c], "blob": blob, "pcol": pc, "maskd": maskd, "masks": masks} for c in range(NCORES)]
        res = run_bass_kernel_spmd(nc, in_maps, core_ids=list(range(NCORES)))
        cur = [np.asarray(res.results[c]["oT"]) for c in range(NCORES)]
    else:
        nc = _get_prog(1, [True])
        for l in range(DEPTH):
            blob = np.ascontiguousarray(blobs[l][None])
            in_maps = [{"xT": cur[c], "blob": blob, "pcol": pcols[l], "maskd": maskd, "masks": masks}
                       for c in range(NCORES)]
            res = run_bass_kernel_spmd(nc, in_maps, core_ids=list(range(NCORES)))
            cur = [np.asarray(res.results[c]["oT"]) for c in range(NCORES)]
    outT = np.concatenate(cur, axis=0)
    return np.ascontiguousarray(outT.transpose(0, 2, 1)).astype(np.float32)
```

```python
import numpy as np
import concourse.bass as bass
import concourse.mybir as mybir
from concourse.bass_utils import run_bass_kernel_spmd

F32 = mybir.dt.float32
BF16 = mybir.dt.bfloat16
AF = mybir.ActivationFunctionType
ALU = mybir.AluOpType

D = 1024
T = 2048
DEPTH = 4
KC = 8
TT = 256
NTL = T // TT
FFH = 2816
NJ = FFH // 128
NSEQ = 2
NCORES = 8
ALPHA = float((2.0 * DEPTH) ** 0.25)
LN_EPS = 1e-5
NS = 3
NTMP = 12
TMPW = 264
NPC = 120

FUSED = True


def _chunked(wsub):
    n = wsub.shape[1]
    return np.ascontiguousarray(wsub.reshape(KC, 128, n).transpose(1, 0, 2).reshape(128, KC * n))


def _tile_table():
    names = []
    for kv in range(4):
        names.append((f"swa{kv}", 8 * 448))
    for g in range(8):
        names.append((f"dil{g}", 8 * 384))
    names.append(("bd", 2048))
    for cp in range(4):
        names.append((f"lru{cp}", 4096))
    for c in range(8):
        names.append((f"gate{c}", 3072))
        names.append((f"br{c}", 3072))
    for o in range(2):
        names.append((f"out{o}", 4096))
    for jp in range(11):
        names.append((f"ffi{jp}", 4096))
    for c in range(8):
        names.append((f"ffo{c}", 2816))
    table = {}
    off = 0
    for n, s in names:
        nb = (s + 511) // 512
        table[n] = (off, s, nb)
        off += nb
    return table, off


TILES, BLOB_LEN = _tile_table()


def _build_blob(l, w_in, w_rg, w_ig, w_branch, w_out, w_ffn_in, w_ffn_out):
    blob = np.zeros((BLOB_LEN, 128, 512), np.float32)
    wi = w_in[l]

    def put(name, arr):
        off, size, nb = TILES[name]
        assert arr.shape == (128, size), (name, arr.shape, size)
        for b in range(nb):
            w = min(512, size - 512 * b)
            blob[off + b, :, 0:w] = arr[:, 512 * b:512 * b + w]

    for kv in range(4):
        cols = np.concatenate([
            np.arange(2048 + 256 * kv, 2048 + 256 * kv + 256),
            np.arange(3072 + 64 * kv, 3072 + 64 * kv + 64),
            np.arange(3072 + 64 * kv, 3072 + 64 * kv + 64),
            np.arange(3328 + 64 * kv, 3328 + 64 * kv + 64)])
        put(f"swa{kv}", _chunked(wi[:, cols]))
    for g in range(8):
        cols = np.concatenate([np.arange(3584 + 128 * g, 3584 + 128 * g + 128),
                               np.arange(4608 + 128 * g, 4608 + 128 * g + 128),
                               np.arange(5632 + 128 * g, 5632 + 128 * g + 128)])
        put(f"dil{g}", _chunked(wi[:, cols]))
    bd = np.zeros((128, 2, 8, 128), np.float32)
    for gi, wg in enumerate((w_rg[l], w_ig[l])):
        for c in range(8):
            for b in range(2):
                bd[64 * b:64 * b + 64, gi, c, 64 * b:64 * b + 64] = wg[2 * c + b]
    put("bd", bd.reshape(128, 2048))
    for cp in range(4):
        cols = []
        for c in (2 * cp, 2 * cp + 1):
            cols.append(np.arange(128 * c, 128 * c + 128))
            cols.append(np.arange(1024 + 128 * c, 1024 + 128 * c + 128))
        put(f"lru{cp}", _chunked(wi[:, np.concatenate(cols)]))
    for c in range(8):
        cols = np.concatenate([np.arange(6656 + 1024 * n + 128 * c, 6656 + 1024 * n + 128 * c + 128) for n in range(3)])
        put(f"gate{c}", _chunked(wi[:, cols]))
        br = np.concatenate([w_branch[l, n][:, 128 * c:128 * c + 128] for n in range(3)], axis=1)
        put(f"br{c}", _chunked(br))
    for o in range(2):
        put(f"out{o}", _chunked(w_out[l][:, 512 * o:512 * o + 512]))
    wf = w_ffn_in[l]
    for jp in range(11):
        cols = []
        for j in (2 * jp, 2 * jp + 1):
            cols.append(np.arange(128 * j, 128 * j + 128))
            cols.append(np.arange(FFH + 128 * j, FFH + 128 * j + 128))
        put(f"ffi{jp}", _chunked(wf[:, np.concatenate(cols)]))
    wo = w_ffn_out[l]
    for c in range(8):
        sub = wo[:, 128 * c:128 * c + 128].reshape(NJ, 128, 128).transpose(1, 0, 2).reshape(128, NJ * 128)
        put(f"ffo{c}", np.ascontiguousarray(sub))
    return blob


def _build_pcol(l, conv_w, conv_b, b_rg, b_ig, lru_lambda, sinks, ln1_g, ln1_b, ln2_g, ln2_b):
    pc = np.zeros((128, NPC), np.float32)

    def col(v):
        return v.reshape(8, 128).T

    for j in range(4):
        pc[:, 8 * j:8 * j + 8] = col(conv_w[l, j])
    pc[:, 32:40] = col(conv_b[l])
    pc[:, 40:48] = col(b_rg[l])
    pc[:, 48:56] = col(b_ig[l])
    pc[:, 56:64] = col(lru_lambda[l])
    pc[:, 64:72] = col(ln1_g[l])
    pc[:, 72:80] = col(ln1_b[l])
    pc[:, 80:88] = col(ln2_g[l])
    pc[:, 88:96] = col(ln2_b[l])
    pc[:, 104:120] = np.broadcast_to(sinks[l][None, :], (128, 16))
    return pc


def _build_masks():
    ki = np.arange(128)[:, None]
    j = np.arange(19 * 128)[None, :]
    dist = 128 * (j // 128 - 3) + (j % 128) - ki
    m = ((dist >= 0) & (dist <= 128)).astype(np.float32)
    m += ((dist >= 0) & (dist % 4 == 0) & (dist <= 512)).astype(np.float32)
    m += ((dist >= 0) & (dist % 16 == 0) & (dist <= 2048)).astype(np.float32)
    j2 = np.arange(8 * 128)[None, :]
    dist2 = 128 * (j2 // 128 - 3) + (j2 % 128) - ki
    m2 = ((dist2 >= 0) & (dist2 <= 128)).astype(np.float32)
    mp = np.zeros((128, 2560), np.float32)
    mp[:, :19 * 128] = m
    mb = np.ascontiguousarray(mp.reshape(128, 5, 512).transpose(1, 0, 2))
    m2b = np.ascontiguousarray(m2.reshape(128, 2, 512).transpose(1, 0, 2))
    return mb, m2b


class Atom:
    __slots__ = ("w", "r")

    def __init__(self):
        self.w = None
        self.r = []


class Op:
    __slots__ = ("eng", "fn", "deps", "sig", "sigval", "isdma", "semkey", "semval")


class Prog:
    ENGS = ("pe", "act", "dve", "pool", "sp")

    def __init__(self):
        self.ops = {e: [] for e in self.ENGS}
        self.dma_count = {}

    def _add(self, eng, fn, reads, writes, isdma=False, semkey=None):
        op = Op()
        op.eng = eng
        op.fn = fn
        op.sig = False
        op.sigval = 0
        op.isdma = isdma
        op.semkey = semkey
        op.semval = 0
        deps = {}
        for a in reads:
            w = a.w
            if w is None:
                continue
            if w.isdma or isdma or w.eng != eng or eng != "pe":
                deps[id(w)] = w
        for a in writes:
            w = a.w
            if w is not None and (w.isdma or isdma or w.eng != eng):
                deps[id(w)] = w
            for r in a.r:
                if r.isdma or isdma or r.eng != eng:
                    deps[id(r)] = r
        op.deps = list(deps.values())
        for d in op.deps:
            if not d.isdma:
                d.sig = True
        for a in reads:
            a.r.append(op)
        for a in writes:
            a.w = op
            a.r = []
        if isdma:
            n = self.dma_count.get(semkey, 0) + 16
            self.dma_count[semkey] = n
            op.semval = n
        self.ops[eng].append(op)
        return op

    def op(self, eng, fn, reads=(), writes=()):
        return self._add(eng, fn, reads, writes)

    def dma(self, queue, fn, semkey, reads=(), writes=()):
        return self._add(queue, fn, reads, writes, isdma=True, semkey=semkey)


def _flat(*items):
    out = []
    for it in items:
        if isinstance(it, Atom):
            out.append(it)
        else:
            out.extend(_flat(*it))
    return out


def build_program(n_layers, last_flags):
    nc = bass.Bass("TRN2", target_bir_lowering=False)
    P = Prog()

    xT = nc.dram_tensor("xT", [NSEQ, D, T], F32, kind="ExternalInput").ap()
    blob = nc.dram_tensor("blob", [n_layers, BLOB_LEN, 128, 512], F32, kind="ExternalInput").ap()
    pcol = nc.dram_tensor("pcol", [128, n_layers * NPC], F32, kind="ExternalInput").ap()
    maskd_d = nc.dram_tensor("maskd", [5, 128, 512], F32, kind="ExternalInput").ap()
    masks_d = nc.dram_tensor("masks", [2, 128, 512], F32, kind="ExternalInput").ap()
    oT = nc.dram_tensor("oT", [NSEQ, D, T], F32, kind="ExternalOutput").ap()
    import os as _os2
    KDUMP = _os2.environ.get("KDUMP", "") == "1"
    import itertools as _it
    _dbgc = _it.count()
    if KDUMP:
        dbg_yb = nc.dram_tensor("dbg_yb", [128, KC, T], BF16, kind="ExternalOutput").ap()
        dbg_yc = nc.dram_tensor("dbg_yc", [128, KC, T], BF16, kind="ExternalOutput").ap()
        dbg_ya = nc.dram_tensor("dbg_ya", [128, KC, T], BF16, kind="ExternalOutput").ap()
        dbg_mg = nc.dram_tensor("dbg_mg", [128, KC, T], BF16, kind="ExternalOutput").ap()
        dbg_xh = nc.dram_tensor("dbg_xh", [128, KC, T], BF16, kind="ExternalOutput").ap()
        dbg_x1 = nc.dram_tensor("dbg_x1", [128, KC, T], BF16, kind="ExternalOutput").ap()

    from contextlib import ExitStack
    es = ExitStack()

    def sb(name, shape, dt):
        return es.enter_context(nc.sbuf_tensor(name, shape, dt))

    XH = sb("XH", [128, KC, T], BF16)
    XL = sb("XL", [128, KC, T], BF16)
    YB = sb("YB", [128, KC, T], BF16)
    YC = sb("YC", [128, KC, T], BF16)
    WS = [sb(f"WS{i}", [128, 4096], BF16) for i in range(NS)]
    MASKD = sb("MASKD", [128, 2560], BF16)
    MASKS = sb("MASKS", [128, 8 * 128], BF16)
    BD = sb("BD", [128, 2048], BF16)
    SCR = sb("SCR", [128, 10240], BF16)
    TMP = sb("TMP", [128, NTMP * TMPW], F32)
    TB = sb("TB", [128, 4 * 256], BF16)
    HALF = sb("HALF", [128, 256], F32)
    NHALF = sb("NHALF", [128, 256], F32)
    PCOL = sb("PCOL", [128, n_layers * NPC], F32)
    DER = sb("DER", [128, 64], F32)
    ONES32 = sb("ONES32", [128, 128], F32)
    CARRY = sb("CARRY", [128, 8, 3], F32)
    HC = sb("HC", [128, 8], F32)
    PS = [es.enter_context(nc.psum_tensor(f"PS{i}", [128, 512], F32)) for i in range(8)]


    def scr_bf(off_el, n_el):
        return SCR[:, off_el:off_el + n_el]

    scr_atoms = [Atom() for _ in range(20)]

    def scr_at(off_el, n_el):
        return scr_atoms[off_el // 512:(off_el + n_el + 511) // 512]

    XA = [[Atom() for _ in range(NTL)] for _ in range(KC)]
    YBA = [[Atom() for _ in range(NTL)] for _ in range(KC)]
    YCA = [[Atom() for _ in range(NTL)] for _ in range(KC)]
    WSB = [[Atom() for _ in range(8)] for _ in range(NS)]
    WSA = WSB
    TMPA = [Atom() for _ in range(NTMP)]
    TBA = [Atom() for _ in range(4)]
    PSA = [Atom() for _ in range(8)]
    CONSTA = Atom()
    MASKA = [Atom() for _ in range(5)]
    MASKA2 = [Atom() for _ in range(2)]
    BDB = [Atom() for _ in range(4)]
    BDA = BDB
    DERA = Atom()
    CARRYA = [Atom() for _ in range(8)]
    HCA = [Atom() for _ in range(8)]

    def xa(cs, t0, t1):
        return [XA[c][i] for c in cs for i in range(t0 // TT, (t1 + TT - 1) // TT)]

    def ya(A, cs, t0, t1):
        return [A[c][i] for c in cs for i in range(t0 // TT, (t1 + TT - 1) // TT)]

    def bank_atoms(b):
        return [PSA[b]]

    def PSH(hb):
        return PS[hb][:, 0:256]

    st = {"w": 0, "tmp": 0, "tb": 0, "hb": 0}

    def tmp():
        i = st["tmp"] % (NTMP - 2)
        st["tmp"] += 1
        return TMP[:, i * TMPW:(i + 1) * TMPW], TMPA[i]

    def tmp_fixed(i):
        return TMP[:, i * TMPW:(i + 1) * TMPW], TMPA[i]

    def tmpb():
        i = st["tb"] % 4
        st["tb"] += 1
        return TB[:, i * 256:(i + 1) * 256], TBA[i]

    def next_hb():
        i = st["hb"] % 6
        st["hb"] += 1
        return i

    def atmp(a):
        return TMP[:, 2 * a * TMPW:2 * a * TMPW + 512], [TMPA[2 * a], TMPA[2 * a + 1]]

    def pcv(l, k):
        return PCOL[:, l * NPC + k:l * NPC + k + 1]

    def cast_dma(dst2d, src_blk, nblk, key, atoms):
        grp = []
        for b in range(nblk):
            d_ = dst2d[:, 512 * b:512 * b + 512]
            s_ = src_blk(b)
            grp.append(P.dma("pool", lambda e, d_=d_, s_=s_: e.dma_start(out=d_, in_=s_), key, reads=[],
                             writes=([atoms[b % len(atoms)]] if atoms else [])))
        tot = P.dma_count[key]
        for o_ in grp:
            o_.semval = tot

    def load_w(l, name):
        s = st["w"] % NS
        st["w"] += 1
        off, size, nb = TILES[name]
        cast_dma(WS[s], lambda b: blob[l, off + b, :, :], nb, f"ws{s}", WSB[s])
        return s

    import os as _os
    SK = _os.environ.get("KSKIP", "")
    if "m" not in SK:
      cast_dma(MASKD, lambda b: maskd_d[b, :, :], 5, "const", MASKA)
    if "n" not in SK:
      cast_dma(MASKS, lambda b: masks_d[b, :, :], 2, "const", MASKA2)
    if "p" not in SK:
      P.dma("sp", lambda e: e.dma_start(out=PCOL[:], in_=pcol[:, :]), "consth", writes=[CONSTA])
    if "z" not in SK:
      P.op("dve", lambda e: e.memset(ONES32[:], 1.0), writes=[CONSTA])
      P.op("dve", lambda e: e.memset(HALF[:], 0.5), writes=[CONSTA])
      P.op("dve", lambda e: e.memset(NHALF[:], -0.5), writes=[CONSTA])

    xs_ctr = [0]

    def load_x(s):
        for c in range(KC):
            for q in range(4):
                a = xs_ctr[0] % 2
                xs_ctr[0] += 1
                stg, stga = atmp(a)
                src = xT[s, c * 128:(c + 1) * 128, q * 512:(q + 1) * 512]
                P.dma("sp", lambda e, stg=stg, src=src: e.dma_start(out=stg, in_=src), f"xs{a}",
                      reads=[], writes=stga)
                hi = XH[:, c, q * 512:(q + 1) * 512]
                lo = XL[:, c, q * 512:(q + 1) * 512]
                xat = xa([c], q * 512, (q + 1) * 512)
                P.op("act", lambda e, hi=hi, stg=stg: e.activation(out=hi, in_=stg, func=AF.Copy),
                     reads=stga, writes=xat)
                P.op("dve", lambda e, lo=lo, hi=hi, stg=stg: e.tensor_tensor(out=lo, in0=stg, in1=hi, op=ALU.subtract),
                     reads=stga + xat, writes=xat)

    def layer_consts(l):
        lam = PCOL[:, l * NPC + 56:l * NPC + 64]
        P.op("act", lambda e: e.activation(out=DER[:, 0:8], in_=lam, func=AF.Exp, scale=-1.0),
             reads=[CONSTA], writes=[DERA])
        P.op("act", lambda e: e.activation(out=DER[:, 8:16], in_=DER[:, 0:8], func=AF.Ln, bias=1.0),
             reads=[DERA], writes=[DERA])
        P.op("act", lambda e: e.activation(out=DER[:, 48:64], in_=PCOL[:, l * NPC + 104:l * NPC + 120], func=AF.Exp),
             reads=[CONSTA], writes=[DERA])
        P.op("dve", lambda e: e.tensor_scalar(out=DER[:, 16:24], in0=DER[:, 8:16], scalar1=-8.0, scalar2=None, op0=ALU.mult),
             reads=[DERA], writes=[DERA])
        P.op("dve", lambda e: e.tensor_scalar(out=DER[:, 24:32], in0=DER[:, 8:16], scalar1=-4.0, scalar2=None, op0=ALU.mult),
             reads=[DERA], writes=[DERA])
        P.op("dve", lambda e: e.tensor_scalar(out=DER[:, 32:48], in0=PCOL[:, l * NPC + 40:l * NPC + 56], scalar1=0.5,
                                              scalar2=None, op0=ALU.mult),
             reads=[DERA, CONSTA], writes=[DERA])
        off, size, nb = TILES["bd"]
        cast_dma(BD, lambda b: blob[l, off + b, :, :], nb, "bd", BDB)

    SBANKS = [0, 1, 2]
    OBANKS = [3, 4]
    PBANKS = [5, 6, 7]
    pa = {"s": 0, "o": 0, "p": 0, "pt": 0, "at": 0}

    def attention_units(heads, mask, mask_w, kb_lo_fn, ptile_off):
        units = []
        for h in heads:
            for Qi in range(4):
                lo = kb_lo_fn(Qi)
                hi = 4 * Qi + 3
                for kb in range(lo, hi + 1):
                    units.append((h, Qi, kb, kb == lo, kb == hi))
        n = len(units)
        LA = 2
        info = [None] * n
        for i in range(n + LA):
            if i < n:
                h, Qi, kb, first, last = units[i]
                sbk = SBANKS[pa["s"] % 3]
                pa["s"] += 1
                pt = pa["pt"] % 4
                pa["pt"] += 1
                Pt = scr_bf(ptile_off + pt * 512, 512)
                Pa = scr_at(ptile_off + pt * 512, 512)
                kT = h["kT"](kb)
                qT = h["qT"](Qi)
                P.op("pe", lambda e, sbk=sbk, kT=kT, qT=qT: e.matmul(PS[sbk][:, :], lhsT=kT, rhs=qT, start=True, stop=True),
                     reads=h["k_atoms"](kb) + h["q_atoms"](Qi), writes=bank_atoms(sbk))
                P.op("act", lambda e, sbk=sbk, Pt=Pt: e.activation(out=Pt, in_=PS[sbk][:, :], func=AF.Exp),
                     reads=bank_atoms(sbk), writes=Pa)
                j0 = (4 * Qi - kb + 3) * 128
                assert 0 <= j0 and j0 + 512 <= mask_w
                mk = mask[:, j0:j0 + 512]
                meng = "dve" if (i % 3) != 2 else "pool"
                P.op(meng, lambda e, Pt=Pt, mk=mk: e.tensor_tensor(out=Pt, in0=Pt, in1=mk, op=ALU.mult),
                     reads=Pa + MASKA + MASKA2, writes=Pa)
                info[i] = (Pt, Pa)
            k = i - LA
            if k >= 0:
                h, Qi, kb, first, last = units[k]
                Pt, Pa = info[k]
                if first:
                    pa["o"] += 1
                ob = OBANKS[pa["o"] % 2]
                v1 = h["v1"](kb)
                P.op("pe", lambda e, ob=ob, v1=v1, Pt=Pt, first=first, last=last:
                     e.matmul(PS[ob][:, :], lhsT=v1, rhs=Pt, start=first, stop=last),
                     reads=Pa + h["v_atoms"](kb), writes=bank_atoms(ob))
                if last:
                    a = pa["at"] % 2
                    pa["at"] += 1
                    ta, taa = atmp(a)
                    sink = h["sink"]
                    if sink is not None:
                        P.op("act", lambda e, ob=ob, ta=ta, sink=sink: e.activation(
                            out=ta[64:128, :], in_=PS[ob][64:128, :], func=AF.Ln, bias=sink),
                            reads=bank_atoms(ob) + [DERA], writes=taa)
                    else:
                        P.op("act", lambda e, ob=ob, ta=ta: e.activation(
                            out=ta[64:128, :], in_=PS[ob][64:128, :], func=AF.Ln),
                            reads=bank_atoms(ob), writes=taa)
                    P.op("act", lambda e, ta=ta: e.activation(out=ta[64:128, :], in_=ta[64:128, :], func=AF.Exp, scale=-1.0),
                         reads=taa, writes=taa)
                    yo = h["out"](Qi)
                    P.op("dve", lambda e, yo=yo, ob=ob, ta=ta: e.tensor_tensor(
                        out=yo, in0=PS[ob][0:64, :], in1=ta[64:128, :], op=ALU.mult),
                        reads=bank_atoms(ob) + taa, writes=h["out_atoms"](Qi))

    def proj_fm(s_slot, wv, ncols0, dst_fn, dst_atoms_fn, scale, eng):
        for Qi in range(4):
            bk = PBANKS[pa["p"] % 3]
            pa["p"] += 1

            def mm(e, bk=bk, Qi=Qi):
                r = None
                for kc in range(KC):
                    r = e.matmul(PS[bk][:, :], lhsT=wv[:, kc, ncols0:ncols0 + 128], rhs=XH[:, kc, Qi * 512:(Qi + 1) * 512],
                                 start=(kc == 0), stop=(kc == KC - 1))
                return r
            P.op("pe", mm, reads=WSA[s_slot] + xa(range(KC), Qi * 512, (Qi + 1) * 512), writes=bank_atoms(bk))
            dst = dst_fn(Qi)
            if eng == "act":
                P.op("act", lambda e, dst=dst, bk=bk: e.activation(out=dst, in_=PS[bk][:, :], func=AF.Copy, scale=scale),
                     reads=bank_atoms(bk), writes=dst_atoms_fn(Qi))
            else:
                P.op("dve", lambda e, dst=dst, bk=bk: e.tensor_copy(out=dst, in_=PS[bk][:, :]),
                     reads=bank_atoms(bk), writes=dst_atoms_fn(Qi))

    def phase_a_dil(l, g):
        s = load_w(l, f"dil{g}")
        wv = WS[s][:, 0:3072].rearrange("p (k n) -> p k n", n=384)
        QO, KO, VO, PO = 0, 2048, 4096, 8192
        proj_fm(s, wv, 0, lambda Qi: scr_bf(QO + Qi * 512, 512), lambda Qi: scr_at(QO + Qi * 512, 512), 0.125, "act")
        proj_fm(s, wv, 128, lambda Qi: scr_bf(KO + Qi * 512, 512), lambda Qi: scr_at(KO + Qi * 512, 512), 1.0, "dve")
        V1 = scr_bf(VO, 4096).rearrange("p (t h d) -> p t h d", h=2, d=128)
        V1a = scr_at(VO, 4096)
        P.op("dve", lambda e: e.memset(V1[:, :, :, 64:128], 1.0), writes=V1a)
        for tg in range(4):
            bk = PBANKS[pa["p"] % 3]
            pa["p"] += 1

            def mm(e, bk=bk, tg=tg):
                r = None
                for tb in range(4 * tg, 4 * tg + 4):
                    for kc in range(KC):
                        r = e.matmul(PS[bk][:, (tb % 4) * 128:(tb % 4) * 128 + 128], lhsT=XH[:, kc, tb * 128:(tb + 1) * 128],
                                     rhs=wv[:, kc, 256:384], start=(kc == 0), stop=(kc == KC - 1))
                return r
            P.op("pe", mm, reads=WSA[s] + xa(range(KC), tg * 512, (tg + 1) * 512), writes=bank_atoms(bk))
            dst = V1[:, 4 * tg:4 * tg + 4, :, 0:64]
            src = PS[bk][:, :].rearrange("p (t h d) -> p t h d", h=2, d=64)
            P.op("act", lambda e, dst=dst, src=src: e.activation(out=dst, in_=src, func=AF.Copy),
                 reads=bank_atoms(bk), writes=scr_at(VO + tg * 1024, 1024))
        heads = []
        for j in range(2):
            pb = 64 * j
            heads.append(dict(
                qT=lambda Qi, pb=pb: SCR[pb:pb + 64, QO + Qi * 512:QO + (Qi + 1) * 512],
                q_atoms=lambda Qi: scr_at(QO + Qi * 512, 512),
                kT=lambda kb, pb=pb: SCR[pb:pb + 64, KO + kb * 128:KO + (kb + 1) * 128],
                k_atoms=lambda kb: scr_at(KO + kb * 128, 128),
                v1=lambda kb, j=j: V1[:, kb, j, :],
                v_atoms=lambda kb: scr_at(VO + kb * 256, 256),
                sink=None,
                out=lambda Qi, pb=pb: YC[pb:pb + 64, g, Qi * 512:(Qi + 1) * 512],
                out_atoms=lambda Qi: ya(YCA, [g], Qi * 512, (Qi + 1) * 512)))
        attention_units(heads, MASKD, 19 * 128, lambda Qi: 0, PO)

    def phase_a_swa(l, kv):
        s = load_w(l, f"swa{kv}")
        wv = WS[s][:, 0:3584].rearrange("p (k n) -> p k n", n=448)
        QO, KO, VO, PO = 0, 4096, 6144, 8192
        for ch in range(2):
            proj_fm(s, wv, 128 * ch, lambda Qi, ch=ch: scr_bf(QO + ch * 2048 + Qi * 512, 512),
                    lambda Qi, ch=ch: scr_at(QO + ch * 2048 + Qi * 512, 512), 0.125, "act")
        proj_fm(s, wv, 256, lambda Qi: scr_bf(KO + Qi * 512, 512), lambda Qi: scr_at(KO + Qi * 512, 512), 1.0, "dve")
        V1 = scr_bf(VO, 2048).rearrange("p (t d) -> p t d", d=128)
        V1a = scr_at(VO, 2048)
        P.op("dve", lambda e: e.memset(V1[:, :, 64:128], 1.0), writes=V1a)
        for tg in range(2):
            bk = PBANKS[pa["p"] % 3]
            pa["p"] += 1

            def mm(e, bk=bk, tg=tg):
                r = None
                for tb in range(8 * tg, 8 * tg + 8):
                    for kc in range(KC):
                        r = e.matmul(PS[bk][:, (tb % 8) * 64:(tb % 8) * 64 + 64], lhsT=XH[:, kc, tb * 128:(tb + 1) * 128],
                                     rhs=wv[:, kc, 384:448], start=(kc == 0), stop=(kc == KC - 1))
                return r
            P.op("pe", mm, reads=WSA[s] + xa(range(KC), tg * 1024, (tg + 1) * 1024), writes=bank_atoms(bk))
            dst = V1[:, 8 * tg:8 * tg + 8, 0:64]
            src = PS[bk][:, :].rearrange("p (t d) -> p t d", d=64)
            P.op("act", lambda e, dst=dst, src=src: e.activation(out=dst, in_=src, func=AF.Copy),
                 reads=bank_atoms(bk), writes=scr_at(VO + tg * 1024, 1024))
        heads = []
        for j in range(4):
            pb = 64 * (j % 2)
            ch = j // 2
            hidx = 4 * kv + j
            cidx = 2 * kv + ch
            heads.append(dict(
                qT=lambda Qi, pb=pb, ch=ch: SCR[pb:pb + 64, QO + ch * 2048 + Qi * 512:QO + ch * 2048 + (Qi + 1) * 512],
                q_atoms=lambda Qi, ch=ch: scr_at(QO + ch * 2048 + Qi * 512, 512),
                kT=lambda kb, pb=pb: SCR[pb:pb + 64, KO + kb * 128:KO + (kb + 1) * 128],
                k_atoms=lambda kb: scr_at(KO + kb * 128, 128),
                v1=lambda kb: V1[:, kb, :],
                v_atoms=lambda kb: scr_at(VO + kb * 128, 128),
                sink=DER[64:128, 48 + hidx:48 + hidx + 1],
                out=lambda Qi, pb=pb, cidx=cidx: YB[pb:pb + 64, cidx, Qi * 512:(Qi + 1) * 512],
                out_atoms=lambda Qi, cidx=cidx: ya(YBA, [cidx], Qi * 512, (Qi + 1) * 512)))
        attention_units(heads, MASKS, 8 * 128, lambda Qi: max(0, 4 * Qi - 1), PO)

    HHO, YAO, MGO, OSO = 0, 5632, 7680, 9728
    Z1O, Z2O = 0, 5632

    def scr_f32(off_el, ncol):
        return SCR[:, off_el:off_el + 2 * ncol].bitcast(F32)

    SBANK1, SBANK2 = 6, 7

    def mm_group(hb, lhs_fn, rhs_fn, nk):
        def mm(e):
            r = None
            for k in range(nk):
                r = e.matmul(PSH(hb), lhsT=lhs_fn(k), rhs=rhs_fn(k), start=(k == 0), stop=(k == nk - 1))
            return r
        return mm

    def layer_norm_tile(l, i, zoff, gk, bk_, is_last_out, s):
        t0 = i * TT
        za = scr_at(zoff, 4096)
        mean, meana = tmp_fixed(NTMP - 2)
        P.op("act", lambda e: e.activation(out=mean[:, 0:256], in_=PS[SBANK1][:, 0:256], func=AF.Copy, scale=1.0 / D),
             reads=bank_atoms(SBANK1), writes=[meana])
        msq, msqa = tmp()
        P.op("dve", lambda e: e.scalar_tensor_tensor(out=msq[:, 0:256], in0=mean[:, 0:256], scalar=-1.0, in1=mean[:, 0:256],
                                                     op0=ALU.mult, op1=ALU.mult),
             reads=[meana], writes=[msqa])
        var, vara = tmp()
        P.op("dve", lambda e: e.scalar_tensor_tensor(out=var[:, 0:256], in0=PS[SBANK2][:, 0:256], scalar=1.0 / D,
                                                     in1=msq[:, 0:256], op0=ALU.mult, op1=ALU.add),
             reads=bank_atoms(SBANK2) + [msqa], writes=[vara])
        P.op("pool", lambda e: e.tensor_scalar(out=var[:, 0:256], in0=var[:, 0:256], scalar1=LN_EPS, scalar2=None, op0=ALU.add),
             reads=[vara], writes=[vara])
        rstd, rstda = tmp_fixed(NTMP - 1)
        P.op("pool", lambda e: e.tensor_tensor(out=rstd[:, 0:256], in0=var[:, 0:256], in1=NHALF[:, :], op=ALU.pow),
             reads=[vara, CONSTA], writes=[rstda])
        for c in range(KC):
            z = scr_f32(zoff + c * 512, 256)
            zc = scr_at(zoff + c * 512, 512)
            u, ua = tmp()
            P.op("dve", lambda e, u=u, z=z: e.tensor_tensor(out=u[:, 0:256], in0=z, in1=mean[:, 0:256], op=ALU.subtract),
                 reads=zc + [meana], writes=[ua])
            P.op("dve", lambda e, u=u: e.tensor_tensor(out=u[:, 0:256], in0=u[:, 0:256], in1=rstd[:, 0:256], op=ALU.mult),
                 reads=[ua, rstda], writes=[ua])
            gcol = pcv(l, gk + c)
            bcol = pcv(l, bk_ + c)
            if is_last_out:
                v = scr_f32(OSO, 256)
                va = scr_at(OSO, 512)
            else:
                v, va1 = tmp()
                v = v[:, 0:256]
                va = [va1]
            P.op("act", lambda e, v=v, u=u, gcol=gcol, bcol=bcol: e.activation(out=v, in_=u[:, 0:256], func=AF.Identity,
                                                                              scale=gcol, bias=bcol),
                 reads=[ua, CONSTA], writes=va)
            if is_last_out:
                dst = oT[s, c * 128:(c + 1) * 128, t0:t0 + TT]
                P.dma("sp", lambda e, v=v, dst=dst: e.dma_start(out=dst, in_=v), "os0", reads=va, writes=[])
            else:
                hi = XH[:, c, t0:t0 + TT]
                lo = XL[:, c, t0:t0 + TT]
                xat = xa([c], t0, t0 + TT)
                P.op("act", lambda e, hi=hi, v=v: e.activation(out=hi, in_=v, func=AF.Copy), reads=va, writes=xat)
                P.op("pool", lambda e, lo=lo, hi=hi, v=v: e.tensor_tensor(out=lo, in0=v, in1=hi, op=ALU.subtract),
                     reads=va + xat, writes=xat)

    def residual_and_stats(i, c, hz, zoff):
        t0 = i * TT
        xf, xfa = tmp()
        xat = xa([c], t0, t0 + TT)
        P.op("pool", lambda e, xf=xf: e.tensor_tensor(out=xf[:, 0:256], in0=XH[:, c, t0:t0 + TT], in1=XL[:, c, t0:t0 + TT],
                                                      op=ALU.add),
             reads=xat, writes=[xfa])
        z = scr_f32(zoff + c * 512, 256)
        zc = scr_at(zoff + c * 512, 512)
        P.op("dve", lambda e, xf=xf, z=z: e.scalar_tensor_tensor(out=z, in0=xf[:, 0:256], scalar=ALPHA, in1=PSH(hz),
                                                                 op0=ALU.mult, op1=ALU.add),
             reads=[xfa, PSA[hz]], writes=zc)
        sq, sqa = tmp()
        P.op("act", lambda e, sq=sq, z=z: e.activation(out=sq[:, 0:256], in_=z, func=AF.Square), reads=zc, writes=[sqa])
        P.op("pe", lambda e, z=z: e.matmul(PS[SBANK1][:, 0:256], lhsT=ONES32[:, :], rhs=z, start=(c == 0), stop=(c == KC - 1)),
             reads=zc + [CONSTA], writes=bank_atoms(SBANK1))
        P.op("pe", lambda e, sq=sq: e.matmul(PS[SBANK2][:, 0:256], lhsT=ONES32[:, :], rhs=sq[:, 0:256], start=(c == 0),
                                             stop=(c == KC - 1)),
             reads=[sqa, CONSTA], writes=bank_atoms(SBANK2))

    def phase_c_tile(l, i, is_last_out, s):
        t0 = i * TT
        XHt = lambda kc: XH[:, kc, t0:t0 + TT]
        xall = xa(range(KC), t0, t0 + TT)
        YAv = scr_bf(YAO, 2048).rearrange("p (c t) -> p c t", t=256)
        MGv = scr_bf(MGO, 2048).rearrange("p (c t) -> p c t", t=256)
        HHv = scr_bf(HHO, 5632).rearrange("p (j t) -> p j t", t=256)
        BDv = BD[:, :].rearrange("p (g c n) -> p g c n", g=2, c=8)
        for c in range(KC):
            if c % 2 == 0:
                sl = load_w(l, f"lru{c // 2}")
                w4 = WS[sl][:, 0:4096].rearrange("p (k n) -> p k n", n=512)
            base = (c % 2) * 256
            hx = next_hb()
            P.op("pe", mm_group(hx, lambda k, w4=w4, base=base: w4[:, k, base:base + 128], XHt, KC),
                 reads=WSA[sl] + xall, writes=[PSA[hx]])
            hg = next_hb()
            P.op("pe", mm_group(hg, lambda k, w4=w4, base=base: w4[:, k, base + 128:base + 256], XHt, KC),
                 reads=WSA[sl] + xall, writes=[PSA[hg]])
            lx, lxa = tmp()
            if i == 0:
                P.op("dve", lambda e, lx=lx: e.memset(lx[:, 0:3], 0.0), writes=[lxa])
            else:
                P.op("act", lambda e, lx=lx, c=c: e.activation(out=lx[:, 0:3], in_=CARRY[:, c, :], func=AF.Copy),
                     reads=[CARRYA[c]], writes=[lxa])
            P.op("act", lambda e, lx=lx, hx=hx: e.activation(out=lx[:, 3:259], in_=PSH(hx), func=AF.Copy),
                 reads=[PSA[hx]], writes=[lxa])
            P.op("act", lambda e, lx=lx, c=c: e.activation(out=CARRY[:, c, :], in_=lx[:, 256:259], func=AF.Copy),
                 reads=[lxa], writes=[CARRYA[c]])
            xc, xca = tmp()
            P.op("dve", lambda e, xc=xc, lx=lx, c=c: e.tensor_scalar(out=xc[:, 0:256], in0=lx[:, 3:259], scalar1=pcv(l, 24 + c),
                                                                     scalar2=pcv(l, 32 + c), op0=ALU.mult, op1=ALU.add),
                 reads=[lxa, CONSTA], writes=[xca])
            for jj in (2, 1, 0):
                P.op("dve", lambda e, xc=xc, lx=lx, c=c, jj=jj: e.scalar_tensor_tensor(
                    out=xc[:, 0:256], in0=lx[:, jj:jj + 256], scalar=pcv(l, 8 * jj + c), in1=xc[:, 0:256],
                    op0=ALU.mult, op1=ALU.add),
                    reads=[lxa, xca, CONSTA], writes=[xca])
            xcb, xcba = tmpb()
            P.op("act", lambda e, xcb=xcb, xc=xc: e.activation(out=xcb, in_=xc[:, 0:256], func=AF.Copy),
                 reads=[xca], writes=[xcba])
            hr = next_hb()
            P.op("pe", lambda e, hr=hr, c=c, xcb=xcb: e.matmul(PSH(hr), lhsT=BDv[:, 0, c, :], rhs=xcb, start=True, stop=True),
                 reads=BDA + [xcba], writes=[PSA[hr]])
            hi_ = next_hb()
            P.op("pe", lambda e, hi_=hi_, c=c, xcb=xcb: e.matmul(PSH(hi_), lhsT=BDv[:, 1, c, :], rhs=xcb, start=True, stop=True),
                 reads=BDA + [xcba], writes=[PSA[hi_]])
            tr, tra = tmp()
            P.op("act", lambda e, tr=tr, hr=hr, c=c: e.activation(out=tr[:, 0:256], in_=PSH(hr), func=AF.Tanh, scale=0.5,
                                                                 bias=DER[:, 32 + c:33 + c]),
                 reads=[PSA[hr], DERA], writes=[tra])
            ti, tia = tmp()
            P.op("act", lambda e, ti=ti, hi_=hi_, c=c: e.activation(out=ti[:, 0:256], in_=PSH(hi_), func=AF.Tanh, scale=0.5,
                                                                   bias=DER[:, 40 + c:41 + c]),
                 reads=[PSA[hi_], DERA], writes=[tia])
            av, ava = tmp()
            P.op("act", lambda e, av=av, tr=tr, c=c: e.activation(out=av[:, 0:256], in_=tr[:, 0:256], func=AF.Exp,
                                                                 scale=DER[:, 24 + c:25 + c], bias=DER[:, 24 + c:25 + c]),
                 reads=[tra, DERA], writes=[ava])
            a2, a2a = tmp()
            P.op("act", lambda e, a2=a2, tr=tr, c=c: e.activation(out=a2[:, 0:256], in_=tr[:, 0:256], func=AF.Exp,
                                                                 scale=DER[:, 16 + c:17 + c], bias=DER[:, 16 + c:17 + c]),
                 reads=[tra, DERA], writes=[a2a])
            P.op("pool", lambda e, a2=a2: e.tensor_scalar(out=a2[:, 0:256], in0=a2[:, 0:256], scalar1=-1.0, scalar2=1.0,
                                                          op0=ALU.mult, op1=ALU.add),
                 reads=[a2a], writes=[a2a])
            P.op("pool", lambda e, a2=a2: e.tensor_tensor(out=a2[:, 0:256], in0=a2[:, 0:256], in1=HALF[:, :], op=ALU.pow),
                 reads=[a2a, CONSTA], writes=[a2a])
            P.op("dve", lambda e, ti=ti, xc=xc: e.scalar_tensor_tensor(out=ti[:, 0:256], in0=ti[:, 0:256], scalar=1.0,
                                                                       in1=xc[:, 0:256], op0=ALU.add, op1=ALU.mult),
                 reads=[tia, xca], writes=[tia])
            P.op("dve", lambda e, ti=ti, a2=a2: e.scalar_tensor_tensor(out=ti[:, 0:256], in0=ti[:, 0:256], scalar=0.5,
                                                                       in1=a2[:, 0:256], op0=ALU.mult, op1=ALU.mult),
                 reads=[tia, a2a], writes=[tia])
            hh, hha = tmp()
            if i == 0:
                P.op("dve", lambda e, hh=hh, av=av, ti=ti: e.tensor_tensor_scan(
                    out=hh[:, 0:256], data0=av[:, 0:256], data1=ti[:, 0:256], initial=0.0, op0=ALU.mult, op1=ALU.add),
                    reads=[ava, tia], writes=[hha])
            else:
                P.op("dve", lambda e, hh=hh, av=av, ti=ti, c=c: e.tensor_tensor_scan(
                    out=hh[:, 0:256], data0=av[:, 0:256], data1=ti[:, 0:256], initial=HC[:, c:c + 1], op0=ALU.mult,
                    op1=ALU.add),
                    reads=[ava, tia, HCA[c]], writes=[hha])
            P.op("act", lambda e, hh=hh, c=c: e.activation(out=HC[:, c:c + 1], in_=hh[:, 255:256], func=AF.Copy),
                 reads=[hha], writes=[HCA[c]])
            xg, xga = tmp()
            P.op("act", lambda e, xg=xg, hg=hg: e.activation(out=xg[:, 0:256], in_=PSH(hg), func=AF.Copy),
                 reads=[PSA[hg]], writes=[xga])
            x2, x2a = tmp()
            P.op("pool", lambda e, x2=x2, xg=xg: e.tensor_tensor(out=x2[:, 0:256], in0=xg[:, 0:256], in1=xg[:, 0:256], op=ALU.mult),
                 reads=[xga], writes=[x2a])
            P.op("pool", lambda e, x2=x2: e.tensor_scalar(out=x2[:, 0:256], in0=x2[:, 0:256], scalar1=0.044715, scalar2=1.0,
                                                          op0=ALU.mult, op1=ALU.add),
                 reads=[x2a], writes=[x2a])
            P.op("pool", lambda e, x2=x2, xg=xg: e.tensor_tensor(out=x2[:, 0:256], in0=x2[:, 0:256], in1=xg[:, 0:256], op=ALU.mult),
                 reads=[x2a, xga], writes=[x2a])
            P.op("act", lambda e, x2=x2: e.activation(out=x2[:, 0:256], in_=x2[:, 0:256], func=AF.Tanh, scale=0.7978845608028654),
                 reads=[x2a], writes=[x2a])
            P.op("dve", lambda e, x2=x2, xg=xg: e.scalar_tensor_tensor(out=x2[:, 0:256], in0=x2[:, 0:256], scalar=1.0,
                                                                       in1=xg[:, 0:256], op0=ALU.add, op1=ALU.mult),
                 reads=[x2a, xga], writes=[x2a])
            P.op("dve", lambda e, x2=x2, hh=hh, c=c: e.scalar_tensor_tensor(out=YAv[:, c, :], in0=x2[:, 0:256], scalar=0.5,
                                                                            in1=hh[:, 0:256], op0=ALU.mult, op1=ALU.mult),
                 reads=[x2a, hha], writes=scr_at(YAO + c * 256, 256))
        Ys = [lambda k: YAv[:, k, :], lambda k: YB[:, k, t0:t0 + TT], lambda k: YC[:, k, t0:t0 + TT]]
        Yat = [scr_at(YAO, 2048), ya(YBA, range(KC), t0, t0 + TT), ya(YCA, range(KC), t0, t0 + TT)]
        for c in range(KC):
            sg = load_w(l, f"gate{c}")
            g3 = WS[sg][:, 0:3072].rearrange("p (k n) -> p k n", n=384)
            sbr = load_w(l, f"br{c}")
            b3 = WS[sbr][:, 0:3072].rearrange("p (k n) -> p k n", n=384)
            acc = None
            for n in range(3):
                hgn = next_hb()
                P.op("pe", mm_group(hgn, lambda k, g3=g3, n=n: g3[:, k, 128 * n:128 * n + 128], XHt, KC),
                     reads=WSA[sg] + xall, writes=[PSA[hgn]])
                hbn = next_hb()
                P.op("pe", mm_group(hbn, lambda k, b3=b3, n=n: b3[:, k, 128 * n:128 * n + 128], Ys[n], KC),
                     reads=WSA[sbr] + Yat[n], writes=[PSA[hbn]])
                sgm, sga = tmp()
                P.op("act", lambda e, sgm=sgm, hgn=hgn: e.activation(out=sgm[:, 0:256], in_=PSH(hgn), func=AF.Sigmoid),
                     reads=[PSA[hgn]], writes=[sga])
                P.op("dve", lambda e, sgm=sgm, hbn=hbn: e.tensor_tensor(out=sgm[:, 0:256], in0=sgm[:, 0:256], in1=PSH(hbn),
                                                                        op=ALU.mult),
                     reads=[sga, PSA[hbn]], writes=[sga])
                if n == 0:
                    acc, acca = sgm, sga
                elif n == 1:
                    P.op("pool", lambda e, acc=acc, sgm=sgm: e.tensor_tensor(out=acc[:, 0:256], in0=acc[:, 0:256],
                                                                             in1=sgm[:, 0:256], op=ALU.add),
                         reads=[acca, sga], writes=[acca])
                else:
                    P.op("pool", lambda e, acc=acc, sgm=sgm, c=c: e.tensor_tensor(out=MGv[:, c, :], in0=acc[:, 0:256],
                                                                                  in1=sgm[:, 0:256], op=ALU.add),
                         reads=[acca, sga], writes=scr_at(MGO + c * 256, 256))
        if KDUMP and s == 0 and l == 0:
            P.dma("sp", lambda e: e.dma_start(out=dbg_ya[:, :, t0:t0 + TT], in_=YAv), "dbg%d" % next(_dbgc), reads=scr_at(YAO, 2048), writes=[])
            P.dma("sp", lambda e: e.dma_start(out=dbg_mg[:, :, t0:t0 + TT], in_=MGv), "dbg%d" % next(_dbgc), reads=scr_at(MGO, 2048), writes=[])
        mga = scr_at(MGO, 2048)
        for o in range(2):
            so = load_w(l, f"out{o}")
            w4 = WS[so][:, 0:4096].rearrange("p (k n) -> p k n", n=512)
            for cc in range(4):
                c = 4 * o + cc
                hz = next_hb()
                P.op("pe", mm_group(hz, lambda k, w4=w4, cc=cc: w4[:, k, 128 * cc:128 * cc + 128], lambda k: MGv[:, k, :], KC),
                     reads=WSA[so] + mga, writes=[PSA[hz]])
                residual_and_stats(i, c, hz, Z1O)
        layer_norm_tile(l, i, Z1O, 64, 72, False, s)
        for jp in range(11):
            sf = load_w(l, f"ffi{jp}")
            w4 = WS[sf][:, 0:4096].rearrange("p (k n) -> p k n", n=512)
            for jj in range(2):
                j = 2 * jp + jj
                h1 = next_hb()
                P.op("pe", mm_group(h1, lambda k, w4=w4, jj=jj: w4[:, k, 256 * jj:256 * jj + 128], XHt, KC),
                     reads=WSA[sf] + xall, writes=[PSA[h1]])
                h3 = next_hb()
                P.op("pe", mm_group(h3, lambda k, w4=w4, jj=jj: w4[:, k, 256 * jj + 128:256 * jj + 256], XHt, KC),
                     reads=WSA[sf] + xall, writes=[PSA[h3]])
                sl_, sla = tmp()
                P.op("act", lambda e, sl_=sl_, h1=h1: e.activation(out=sl_[:, 0:256], in_=PSH(h1), func=AF.Silu),
                     reads=[PSA[h1]], writes=[sla])
                P.op("dve", lambda e, sl_=sl_, h3=h3, j=j: e.tensor_tensor(out=HHv[:, j, :], in0=sl_[:, 0:256], in1=PSH(h3),
                                                                          op=ALU.mult),
                     reads=[sla, PSA[h3]], writes=scr_at(HHO + j * 256, 256))
        hha_all = scr_at(HHO, 5632)
        for c in range(KC):
            so = load_w(l, f"ffo{c}")
            w22 = WS[so][:, 0:2816].rearrange("p (j n) -> p j n", n=128)
            hz = next_hb()
            P.op("pe", mm_group(hz, lambda k, w22=w22: w22[:, k, :], lambda k: HHv[:, k, :], NJ),
                 reads=WSA[so] + hha_all, writes=[PSA[hz]])
            residual_and_stats(i, c, hz, Z2O)
        layer_norm_tile(l, i, Z2O, 80, 88, is_last_out, s)

    import os
    PH = os.environ.get("KDEBUG", "xksdc")
    NSQ = int(os.environ.get("KNSEQ", NSEQ))
    for s in range(NSQ):
        if "x" in PH:
            load_x(s)
        for l in range(n_layers):
            if "k" in PH:
                layer_consts(l)
            if "s" in PH:
                for kv in range(int(os.environ.get("KNSWA", 4))):
                    phase_a_swa(l, kv)
            if "d" in PH:
                for g in range(int(os.environ.get("KNDIL", 8))):
                    phase_a_dil(l, g)
            if KDUMP and s == 0 and l == 0:
                P.dma("sp", lambda e: e.dma_start(out=dbg_yb[:, :, :], in_=YB[:, :, :]), "dbg",
                      reads=ya(YBA, range(KC), 0, T), writes=[])
                P.dma("sp", lambda e: e.dma_start(out=dbg_yc[:, :, :], in_=YC[:, :, :]), "dbg",
                      reads=ya(YCA, range(KC), 0, T), writes=[])
                P.dma("sp", lambda e: e.dma_start(out=dbg_xh[:, :, :], in_=XH[:, :, :]), "dbg",
                      reads=xa(range(KC), 0, T), writes=[])
            if "c" in PH:
                for i in range(int(os.environ.get("KNTL", NTL))):
                    phase_c_tile(l, i, last_flags[l], s)
                if KDUMP and s == 0 and l == 0:
                    P.dma("sp", lambda e: e.dma_start(out=dbg_x1[:, :, :], in_=XH[:, :, :]), "dbg%d" % next(_dbgc),
                          reads=xa(range(KC), 0, T), writes=[])

    total_const = P.dma_count.get("const", 0)
    for e_ in P.ENGS:
        for op in P.ops[e_]:
            if op.isdma and op.semkey == "const":
                op.semval = total_const
    for e_ in P.ENGS:
        cnt = 0
        for op in P.ops[e_]:
            if op.sig and not op.isdma:
                cnt += 1
                op.sigval = cnt
    sem_names = ["e_" + e_ for e_ in P.ENGS] + ["d_" + k for k in sorted(P.dma_count)]
    sems = {n: es.enter_context(nc.semaphore(n)) for n in sem_names}
    handles = {"pe": nc.tensor, "act": nc.scalar, "dve": nc.vector, "pool": nc.gpsimd, "sp": nc.sync}

    def emit(eng, e):
        waited = {}
        for op in P.ops[eng]:
            need = {}
            for d in op.deps:
                if d.isdma:
                    key, val = "d_" + d.semkey, d.semval
                else:
                    key, val = "e_" + d.eng, d.sigval
                if need.get(key, 0) < val:
                    need[key] = val
            for key, val in need.items():
                if waited.get(key, 0) < val:
                    e.wait_ge(sems[key], val)
                    waited[key] = val
            inst = op.fn(e)
            if op.isdma:
                inst.then_inc(sems["d_" + op.semkey], 16)
            elif op.sig:
                inst.then_inc(sems["e_" + eng], 1)
        if eng == "sp":
            for k_, v_ in sorted(P.dma_count.items()):
                e.wait_ge(sems["d_" + k_], v_)

    block = es.enter_context(nc.Block())

    @block.tensor
    def _(e):
        emit("pe", e)

    @block.scalar
    def _(e):
        emit("act", e)

    @block.vector
    def _(e):
        emit("dve", e)

    @block.gpsimd
    def _(e):
        emit("pool", e)

    @block.sync
    def _(e):
        emit("sp", e)

    es.close()
    stats = {k: len(v) for k, v in P.ops.items()}
    return nc, stats


_CACHE = {}


def _get_prog(n_layers, last_flags):
    key = (n_layers, tuple(last_flags))
    if key not in _CACHE:
        _CACHE[key] = build_program(n_layers, list(last_flags))
    return _CACHE[key][0]


def kernel(x, w_in, conv_w, conv_b, w_rg, b_rg, w_ig, b_ig, lru_lambda, sinks, w_branch, w_out,
           ln1_g, ln1_b, w_ffn_in, w_ffn_out, ln2_g, ln2_b):
    f = lambda a: np.asarray(a, dtype=np.float32)
    x = f(x); w_in = f(w_in); conv_w = f(conv_w); conv_b = f(conv_b); w_rg = f(w_rg); b_rg = f(b_rg)
    w_ig = f(w_ig); b_ig = f(b_ig); lru_lambda = f(lru_lambda); sinks = f(sinks); w_branch = f(w_branch)
    w_out = f(w_out); ln1_g = f(ln1_g); ln1_b = f(ln1_b); w_ffn_in = f(w_ffn_in); w_ffn_out = f(w_ffn_out)
    ln2_g = f(ln2_g); ln2_b = f(ln2_b)
    maskd, masks = _build_masks()
    blobs = [_build_blob(l, w_in, w_rg, w_ig, w_branch, w_out, w_ffn_in, w_ffn_out) for l in range(DEPTH)]
    pcols = [_build_pcol(l, conv_w, conv_b, b_rg, b_ig, lru_lambda, sinks, ln1_g, ln1_b, ln2_g, ln2_b)
             for l in range(DEPTH)]
    xT = np.ascontiguousarray(x.transpose(0, 2, 1))
    cur = [np.ascontiguousarray(xT[NSEQ * c:NSEQ * (c + 1)]) for c in range(NCORES)]
    if FUSED:
        nc = _get_prog(DEPTH, [False] * (DEPTH - 1) + [True])
        blob = np.ascontiguousarray(np.stack(blobs, 0))
        pc = np.ascontiguousarray(np.concatenate(pcols, axis=1))
        in_maps = [{"xT": cur[c], "blob": blob, "pcol": pc, "maskd": maskd, "masks": masks} for c in range(NCORES)]
        res = run_bass_kernel_spmd(nc, in_maps, core_ids=list(range(NCORES)))
        cur = [np.asarray(res.results[c]["oT"]) for c in range(NCORES)]
    else:
        nc = _get_prog(1, [True])
        for l in range(DEPTH):
            blob = np.ascontiguousarray(blobs[l][None])
            in_maps = [{"xT": cur[c], "blob": blob, "pcol": pcols[l], "maskd": maskd, "masks": masks}
                       for c in range(NCORES)]
            res = run_bass_kernel_spmd(nc, in_maps, core_ids=list(range(NCORES)))
            cur = [np.asarray(res.results[c]["oT"]) for c in range(NCORES)]
    outT = np.concatenate(cur, axis=0)
    return np.ascontiguousarray(outT.transpose(0, 2, 1)).astype(np.float32)
```

```python
import numpy as np
import concourse.bass as bass
import concourse.mybir as mybir
from concourse.bass_utils import run_bass_kernel_spmd

F32 = mybir.dt.float32
BF16 = mybir.dt.bfloat16
AF = mybir.ActivationFunctionType
ALU = mybir.AluOpType

D = 1024
T = 2048
DEPTH = 4
KC = 8
TT = 256
NTL = T // TT
FFH = 2816
NJ = FFH // 128
NSEQ = 2
NCORES = 8
ALPHA = float((2.0 * DEPTH) ** 0.25)
LN_EPS = 1e-5
NS = 3
NTMP = 12
TMPW = 264
NPC = 120

FUSED = True


def _chunked(wsub):
    n = wsub.shape[1]
    return np.ascontiguousarray(wsub.reshape(KC, 128, n).transpose(1, 0, 2).reshape(128, KC * n))


def _tile_table():
    names = []
    for kv in range(4):
        names.append((f"swa{kv}", 8 * 448))
    for g in range(8):
        names.append((f"dil{g}", 8 * 384))
    names.append(("bd", 2048))
    for cp in range(4):
        names.append((f"lru{cp}", 4096))
    for c in range(8):
        names.append((f"gate{c}", 3072))
        names.append((f"br{c}", 3072))
    for o in range(2):
        names.append((f"out{o}", 4096))
    for jp in range(11):
        names.append((f"ffi{jp}", 4096))
    for c in range(8):
        names.append((f"ffo{c}", 2816))
    table = {}
    off = 0
    for n, s in names:
        nb = (s + 511) // 512
        table[n] = (off, s, nb)
        off += nb
    return table, off


TILES, BLOB_LEN = _tile_table()


def _build_blob(l, w_in, w_rg, w_ig, w_branch, w_out, w_ffn_in, w_ffn_out):
    blob = np.zeros((BLOB_LEN, 128, 512), np.float32)
    wi = w_in[l]

    def put(name, arr):
        off, size, nb = TILES[name]
        assert arr.shape == (128, size), (name, arr.shape, size)
        for b in range(nb):
            w = min(512, size - 512 * b)
            blob[off + b, :, 0:w] = arr[:, 512 * b:512 * b + w]

    for kv in range(4):
        cols = np.concatenate([
            np.arange(2048 + 256 * kv, 2048 + 256 * kv + 256),
            np.arange(3072 + 64 * kv, 3072 + 64 * kv + 64),
            np.arange(3072 + 64 * kv, 3072 + 64 * kv + 64),
            np.arange(3328 + 64 * kv, 3328 + 64 * kv + 64)])
        put(f"swa{kv}", _chunked(wi[:, cols]))
    for g in range(8):
        cols = np.concatenate([np.arange(3584 + 128 * g, 3584 + 128 * g + 128),
                               np.arange(4608 + 128 * g, 4608 + 128 * g + 128),
                               np.arange(5632 + 128 * g, 5632 + 128 * g + 128)])
        put(f"dil{g}", _chunked(wi[:, cols]))
    bd = np.zeros((128, 2, 8, 128), np.float32)
    for gi, wg in enumerate((w_rg[l], w_ig[l])):
        for c in range(8):
            for b in range(2):
                bd[64 * b:64 * b + 64, gi, c, 64 * b:64 * b + 64] = wg[2 * c + b]
    put("bd", bd.reshape(128, 2048))
    for cp in range(4):
        cols = []
        for c in (2 * cp, 2 * cp + 1):
            cols.append(np.arange(128 * c, 128 * c + 128))
            cols.append(np.arange(1024 + 128 * c, 1024 + 128 * c + 128))
        put(f"lru{cp}", _chunked(wi[:, np.concatenate(cols)]))
    for c in range(8):
        cols = np.concatenate([np.arange(6656 + 1024 * n + 128 * c, 6656 + 1024 * n + 128 * c + 128) for n in range(3)])
        put(f"gate{c}", _chunked(wi[:, cols]))
        br = np.concatenate([w_branch[l, n][:, 128 * c:128 * c + 128] for n in range(3)], axis=1)
        put(f"br{c}", _chunked(br))
    for o in range(2):
        put(f"out{o}", _chunked(w_out[l][:, 512 * o:512 * o + 512]))
    wf = w_ffn_in[l]
    for jp in range(11):
        cols = []
        for j in (2 * jp, 2 * jp + 1):
            cols.append(np.arange(128 * j, 128 * j + 128))
            cols.append(np.arange(FFH + 128 * j, FFH + 128 * j + 128))
        put(f"ffi{jp}", _chunked(wf[:, np.concatenate(cols)]))
    wo = w_ffn_out[l]
    for c in range(8):
        sub = wo[:, 128 * c:128 * c + 128].reshape(NJ, 128, 128).transpose(1, 0, 2).reshape(128, NJ * 128)
        put(f"ffo{c}", np.ascontiguousarray(sub))
    return blob


def _build_pcol(l, conv_w, conv_b, b_rg, b_ig, lru_lambda, sinks, ln1_g, ln1_b, ln2_g, ln2_b):
    pc = np.zeros((128, NPC), np.float32)

    def col(v):
        return v.reshape(8, 128).T

    for j in range(4):
        pc[:, 8 * j:8 * j + 8] = col(conv_w[l, j])
    pc[:, 32:40] = col(conv_b[l])
    pc[:, 40:48] = col(b_rg[l])
    pc[:, 48:56] = col(b_ig[l])
    pc[:, 56:64] = col(lru_lambda[l])
    pc[:, 64:72] = col(ln1_g[l])
    pc[:, 72:80] = col(ln1_b[l])
    pc[:, 80:88] = col(ln2_g[l])
    pc[:, 88:96] = col(ln2_b[l])
    pc[:, 104:120] = np.broadcast_to(sinks[l][None, :], (128, 16))
    return pc


def _build_masks():
    ki = np.arange(128)[:, None]
    j = np.arange(19 * 128)[None, :]
    dist = 128 * (j // 128 - 3) + (j % 128) - ki
    m = ((dist >= 0) & (dist <= 128)).astype(np.float32)
    m += ((dist >= 0) & (dist % 4 == 0) & (dist <= 512)).astype(np.float32)
    m += ((dist >= 0) & (dist % 16 == 0) & (dist <= 2048)).astype(np.float32)
    j2 = np.arange(8 * 128)[None, :]
    dist2 = 128 * (j2 // 128 - 3) + (j2 % 128) - ki
    m2 = ((dist2 >= 0) & (dist2 <= 128)).astype(np.float32)
    mp = np.zeros((128, 2560), np.float32)
    mp[:, :19 * 128] = m
    mb = np.ascontiguousarray(mp.reshape(128, 5, 512).transpose(1, 0, 2))
    m2b = np.ascontiguousarray(m2.reshape(128, 2, 512).transpose(1, 0, 2))
    return mb, m2b


class Atom:
    __slots__ = ("w", "r")

    def __init__(self):
        self.w = None
        self.r = []


class Op:
    __slots__ = ("eng", "fn", "deps", "sig", "sigval", "isdma", "semkey", "semval")


class Prog:
    ENGS = ("pe", "act", "dve", "pool", "sp")

    def __init__(self):
        self.ops = {e: [] for e in self.ENGS}
        self.dma_count = {}

    def _add(self, eng, fn, reads, writes, isdma=False, semkey=None):
        op = Op()
        op.eng = eng
        op.fn = fn
        op.sig = False
        op.sigval = 0
        op.isdma = isdma
        op.semkey = semkey
        op.semval = 0
        deps = {}
        for a in reads:
            w = a.w
            if w is None:
                continue
            if w.isdma or isdma or w.eng != eng or eng != "pe":
                deps[id(w)] = w
        for a in writes:
            w = a.w
            if w is not None and (w.isdma or isdma or w.eng != eng):
                deps[id(w)] = w
            for r in a.r:
                if r.isdma or isdma or r.eng != eng:
                    deps[id(r)] = r
        op.deps = list(deps.values())
        for d in op.deps:
            if not d.isdma:
                d.sig = True
        for a in reads:
            a.r.append(op)
        for a in writes:
            a.w = op
            a.r = []
        if isdma:
            n = self.dma_count.get(semkey, 0) + 16
            self.dma_count[semkey] = n
            op.semval = n
        self.ops[eng].append(op)
        return op

    def op(self, eng, fn, reads=(), writes=()):
        return self._add(eng, fn, reads, writes)

    def dma(self, queue, fn, semkey, reads=(), writes=()):
        return self._add(queue, fn, reads, writes, isdma=True, semkey=semkey)


def _flat(*items):
    out = []
    for it in items:
        if isinstance(it, Atom):
            out.append(it)
        else:
            out.extend(_flat(*it))
    return out


def build_program(n_layers, last_flags):
    nc = bass.Bass("TRN2", target_bir_lowering=False)
    P = Prog()

    xT = nc.dram_tensor("xT", [NSEQ, D, T], F32, kind="ExternalInput").ap()
    blob = nc.dram_tensor("blob", [n_layers, BLOB_LEN, 128, 512], F32, kind="ExternalInput").ap()
    pcol = nc.dram_tensor("pcol", [128, n_layers * NPC], F32, kind="ExternalInput").ap()
    maskd_d = nc.dram_tensor("maskd", [5, 128, 512], F32, kind="ExternalInput").ap()
    masks_d = nc.dram_tensor("masks", [2, 128, 512], F32, kind="ExternalInput").ap()
    oT = nc.dram_tensor("oT", [NSEQ, D, T], F32, kind="ExternalOutput").ap()
    wbf = nc.dram_tensor("wbf", [n_layers, BLOB_LEN, 128, 512], BF16, kind="Internal").ap()
    import os as _os2
    KDUMP = _os2.environ.get("KDUMP", "") == "1"
    import itertools as _it
    _dbgc = _it.count()
    if KDUMP:
        dbg_yb = nc.dram_tensor("dbg_yb", [128, KC, T], BF16, kind="ExternalOutput").ap()
        dbg_yc = nc.dram_tensor("dbg_yc", [128, KC, T], BF16, kind="ExternalOutput").ap()
        dbg_ya = nc.dram_tensor("dbg_ya", [128, KC, T], BF16, kind="ExternalOutput").ap()
        dbg_mg = nc.dram_tensor("dbg_mg", [128, KC, T], BF16, kind="ExternalOutput").ap()
        dbg_xh = nc.dram_tensor("dbg_xh", [128, KC, T], BF16, kind="ExternalOutput").ap()
        dbg_x1 = nc.dram_tensor("dbg_x1", [128, KC, T], BF16, kind="ExternalOutput").ap()

    from contextlib import ExitStack
    es = ExitStack()

    def sb(name, shape, dt):
        return es.enter_context(nc.sbuf_tensor(name, shape, dt))

    XH = sb("XH", [128, KC, T], BF16)
    XL = sb("XL", [128, KC, T], BF16)
    YB = sb("YB", [128, KC, T], BF16)
    YC = sb("YC", [128, KC, T], BF16)
    WS = [sb(f"WS{i}", [128, 4096], BF16) for i in range(NS)]
    MASKD = sb("MASKD", [128, 2560], BF16)
    MASKS = sb("MASKS", [128, 8 * 128], BF16)
    BD = sb("BD", [128, 2048], BF16)
    SCR = sb("SCR", [128, 10240], BF16)
    TMP = sb("TMP", [128, NTMP * TMPW], F32)
    TB = sb("TB", [128, 4 * 256], BF16)
    HALF = sb("HALF", [128, 256], F32)
    NHALF = sb("NHALF", [128, 256], F32)
    PCOL = sb("PCOL", [128, n_layers * NPC], F32)
    DER = sb("DER", [128, 64], F32)
    ONES32 = sb("ONES32", [128, 128], F32)
    CARRY = sb("CARRY", [128, 8, 3], F32)
    HC = sb("HC", [128, 8], F32)
    PS = [es.enter_context(nc.psum_tensor(f"PS{i}", [128, 512], F32)) for i in range(8)]


    def scr_bf(off_el, n_el):
        return SCR[:, off_el:off_el + n_el]

    scr_atoms = [Atom() for _ in range(20)]

    def scr_at(off_el, n_el):
        return scr_atoms[off_el // 512:(off_el + n_el + 511) // 512]

    XA = [[Atom() for _ in range(NTL)] for _ in range(KC)]
    YBA = [[Atom() for _ in range(NTL)] for _ in range(KC)]
    YCA = [[Atom() for _ in range(NTL)] for _ in range(KC)]
    WSB = [[Atom() for _ in range(8)] for _ in range(NS)]
    WSA = WSB
    TMPA = [Atom() for _ in range(NTMP)]
    TBA = [Atom() for _ in range(4)]
    PSA = [Atom() for _ in range(8)]
    CONSTA = Atom()
    MASKA = [Atom() for _ in range(5)]
    MASKA2 = [Atom() for _ in range(2)]
    BDB = [Atom() for _ in range(4)]
    BDA = BDB
    DERA = Atom()
    CARRYA = [Atom() for _ in range(8)]
    HCA = [Atom() for _ in range(8)]

    def xa(cs, t0, t1):
        return [XA[c][i] for c in cs for i in range(t0 // TT, (t1 + TT - 1) // TT)]

    def ya(A, cs, t0, t1):
        return [A[c][i] for c in cs for i in range(t0 // TT, (t1 + TT - 1) // TT)]

    def bank_atoms(b):
        return [PSA[b]]

    def PSH(hb):
        return PS[hb][:, 0:256]

    st = {"w": 0, "tmp": 0, "tb": 0, "hb": 0}

    def tmp():
        i = st["tmp"] % (NTMP - 2)
        st["tmp"] += 1
        return TMP[:, i * TMPW:(i + 1) * TMPW], TMPA[i]

    def tmp_fixed(i):
        return TMP[:, i * TMPW:(i + 1) * TMPW], TMPA[i]

    def tmpb():
        i = st["tb"] % 4
        st["tb"] += 1
        return TB[:, i * 256:(i + 1) * 256], TBA[i]

    def next_hb():
        i = st["hb"] % 6
        st["hb"] += 1
        return i

    def atmp(a):
        return TMP[:, 2 * a * TMPW:2 * a * TMPW + 512], [TMPA[2 * a], TMPA[2 * a + 1]]

    def pcv(l, k):
        return PCOL[:, l * NPC + k:l * NPC + k + 1]

    def cast_dma(dst2d, src_blk, nblk, key, atoms):
        grp = []
        for b in range(nblk):
            d_ = dst2d[:, 512 * b:512 * b + 512]
            s_ = src_blk(b)
            grp.append(P.dma("pool", lambda e, d_=d_, s_=s_: e.dma_start(out=d_, in_=s_), key, reads=[],
                             writes=([atoms[b % len(atoms)]] if atoms else [])))
        tot = P.dma_count[key]
        for o_ in grp:
            o_.semval = tot

    CVB = 32
    cv = {"n": 0, "batch_last": {}, "key_last": {}, "layer_last": {}}

    def convert_blocks(l, b0, b1):
        for b in range(b0, b1):
            n = cv["n"]
            bi = n // CVB
            key = f"cv{bi % 3}"
            o_ = P.dma("pool", lambda e, b=b: e.dma_start(out=wbf[l, b, :, :], in_=blob[l, b, :, :]), key, reads=[], writes=[])
            if n % CVB == 0 and (bi - 2) in cv["batch_last"]:
                o_.deps.append(cv["batch_last"][bi - 2])
            cv["batch_last"][bi] = o_
            cv["key_last"][key] = o_
            cv["n"] = n + 1
            if b == BLOB_LEN - 1:
                cv["layer_last"][l] = list(cv["key_last"].values())

    def load_w(l, name):
        s = st["w"] % NS
        st["w"] += 1
        off, size, nb = TILES[name]
        src = wbf[l, off:off + nb, :, :].rearrange("b p n -> p b n")
        dst = WS[s][:, 0:nb * 512].rearrange("p (b n) -> p b n", n=512)
        o_ = P.dma("sp", lambda e, src=src, dst=dst: e.dma_start(out=dst, in_=src), f"ws{s}", reads=[], writes=WSB[s])
        o_.deps.extend(cv["layer_last"][l])
        return s

    import os as _os
    SK = _os.environ.get("KSKIP", "")
    if "m" not in SK:
      cast_dma(MASKD, lambda b: maskd_d[b, :, :], 5, "const", MASKA)
    if "n" not in SK:
      cast_dma(MASKS, lambda b: masks_d[b, :, :], 2, "const", MASKA2)
    if "p" not in SK:
      P.dma("sp", lambda e: e.dma_start(out=PCOL[:], in_=pcol[:, :]), "consth", writes=[CONSTA])
    if "z" not in SK:
      P.op("dve", lambda e: e.memset(ONES32[:], 1.0), writes=[CONSTA])
      P.op("dve", lambda e: e.memset(HALF[:], 0.5), writes=[CONSTA])
      P.op("dve", lambda e: e.memset(NHALF[:], -0.5), writes=[CONSTA])

    xs_ctr = [0]

    def load_x(s):
        for c in range(KC):
            for q in range(4):
                a = xs_ctr[0] % 2
                xs_ctr[0] += 1
                stg, stga = atmp(a)
                src = xT[s, c * 128:(c + 1) * 128, q * 512:(q + 1) * 512]
                P.dma("sp", lambda e, stg=stg, src=src: e.dma_start(out=stg, in_=src), f"xs{a}",
                      reads=[], writes=stga)
                hi = XH[:, c, q * 512:(q + 1) * 512]
                lo = XL[:, c, q * 512:(q + 1) * 512]
                xat = xa([c], q * 512, (q + 1) * 512)
                P.op("act", lambda e, hi=hi, stg=stg: e.activation(out=hi, in_=stg, func=AF.Copy),
                     reads=stga, writes=xat)
                P.op("dve", lambda e, lo=lo, hi=hi, stg=stg: e.tensor_tensor(out=lo, in0=stg, in1=hi, op=ALU.subtract),
                     reads=stga + xat, writes=xat)

    def layer_consts(l):
        lam = PCOL[:, l * NPC + 56:l * NPC + 64]
        P.op("act", lambda e: e.activation(out=DER[:, 0:8], in_=lam, func=AF.Exp, scale=-1.0),
             reads=[CONSTA], writes=[DERA])
        P.op("act", lambda e: e.activation(out=DER[:, 8:16], in_=DER[:, 0:8], func=AF.Ln, bias=1.0),
             reads=[DERA], writes=[DERA])
        P.op("act", lambda e: e.activation(out=DER[:, 48:64], in_=PCOL[:, l * NPC + 104:l * NPC + 120], func=AF.Exp),
             reads=[CONSTA], writes=[DERA])
        P.op("dve", lambda e: e.tensor_scalar(out=DER[:, 16:24], in0=DER[:, 8:16], scalar1=-8.0, scalar2=None, op0=ALU.mult),
             reads=[DERA], writes=[DERA])
        P.op("dve", lambda e: e.tensor_scalar(out=DER[:, 24:32], in0=DER[:, 8:16], scalar1=-4.0, scalar2=None, op0=ALU.mult),
             reads=[DERA], writes=[DERA])
        P.op("dve", lambda e: e.tensor_scalar(out=DER[:, 32:48], in0=PCOL[:, l * NPC + 40:l * NPC + 56], scalar1=0.5,
                                              scalar2=None, op0=ALU.mult),
             reads=[DERA, CONSTA], writes=[DERA])
        off, size, nb = TILES["bd"]
        src_bd = wbf[l, off:off + nb, :, :].rearrange("b p n -> p b n")
        dst_bd = BD[:, :].rearrange("p (b n) -> p b n", n=512)
        o_ = P.dma("sp", lambda e: e.dma_start(out=dst_bd, in_=src_bd), "bd", reads=[], writes=BDB)
        o_.deps.extend(cv["layer_last"][l])

    SBANKS = [0, 1, 2, 3]
    OBANKS = [4, 5]
    PBANKS = [6, 7]
    pa = {"s": 0, "o": 0, "p": 0, "pt": 0, "at": 0}

    def attention_units(heads, mask, mask_w, kb_lo_fn, ptile_off):
        units = []
        for h in heads:
            for Qi in range(4):
                lo = kb_lo_fn(Qi)
                hi = 4 * Qi + 3
                for kb in range(lo, hi + 1):
                    units.append((h, Qi, kb, kb == lo, kb == hi))
        n = len(units)
        LA = 3
        info = [None] * n
        for i in range(n + LA):
            if i < n:
                h, Qi, kb, first, last = units[i]
                sbk = SBANKS[pa["s"] % 4]
                pa["s"] += 1
                pt = pa["pt"] % 4
                pa["pt"] += 1
                Pt = scr_bf(ptile_off + pt * 512, 512)
                Pa = scr_at(ptile_off + pt * 512, 512)
                kT = h["kT"](kb)
                qT = h["qT"](Qi)
                P.op("pe", lambda e, sbk=sbk, kT=kT, qT=qT: e.matmul(PS[sbk][:, :], lhsT=kT, rhs=qT, start=True, stop=True),
                     reads=h["k_atoms"](kb) + h["q_atoms"](Qi), writes=bank_atoms(sbk))
                P.op("act", lambda e, sbk=sbk, Pt=Pt: e.activation(out=Pt, in_=PS[sbk][:, :], func=AF.Exp),
                     reads=bank_atoms(sbk), writes=Pa)
                j0 = (4 * Qi - kb + 3) * 128
                assert 0 <= j0 and j0 + 512 <= mask_w
                mk = mask[:, j0:j0 + 512]
                meng = "dve" if (i % 3) != 2 else "pool"
                P.op(meng, lambda e, Pt=Pt, mk=mk: e.tensor_tensor(out=Pt, in0=Pt, in1=mk, op=ALU.mult),
                     reads=Pa + MASKA + MASKA2, writes=Pa)
                info[i] = (Pt, Pa)
            k = i - LA
            if k >= 0:
                h, Qi, kb, first, last = units[k]
                Pt, Pa = info[k]
                if first:
                    pa["o"] += 1
                ob = OBANKS[pa["o"] % 2]
                v1 = h["v1"](kb)
                P.op("pe", lambda e, ob=ob, v1=v1, Pt=Pt, first=first, last=last:
                     e.matmul(PS[ob][:, :], lhsT=v1, rhs=Pt, start=first, stop=last),
                     reads=Pa + h["v_atoms"](kb), writes=bank_atoms(ob))
                if last:
                    a = pa["at"] % 2
                    pa["at"] += 1
                    ta, taa = atmp(a)
                    sink = h["sink"]
                    if sink is not None:
                        P.op("act", lambda e, ob=ob, ta=ta, sink=sink: e.activation(
                            out=ta[64:128, :], in_=PS[ob][64:128, :], func=AF.Ln, bias=sink),
                            reads=bank_atoms(ob) + [DERA], writes=taa)
                    else:
                        P.op("act", lambda e, ob=ob, ta=ta: e.activation(
                            out=ta[64:128, :], in_=PS[ob][64:128, :], func=AF.Ln),
                            reads=bank_atoms(ob), writes=taa)
                    P.op("act", lambda e, ta=ta: e.activation(out=ta[64:128, :], in_=ta[64:128, :], func=AF.Exp, scale=-1.0),
                         reads=taa, writes=taa)
                    yo = h["out"](Qi)
                    P.op("dve", lambda e, yo=yo, ob=ob, ta=ta: e.tensor_tensor(
                        out=yo, in0=PS[ob][0:64, :], in1=ta[64:128, :], op=ALU.mult),
                        reads=bank_atoms(ob) + taa, writes=h["out_atoms"](Qi))

    def proj_fm(s_slot, wv, ncols0, dst_fn, dst_atoms_fn, scale, eng):
        for Qi in range(4):
            bk = PBANKS[pa["p"] % 2]
            pa["p"] += 1

            def mm(e, bk=bk, Qi=Qi):
                r = None
                for kc in range(KC):
                    r = e.matmul(PS[bk][:, :], lhsT=wv[:, kc, ncols0:ncols0 + 128], rhs=XH[:, kc, Qi * 512:(Qi + 1) * 512],
                                 start=(kc == 0), stop=(kc == KC - 1))
                return r
            P.op("pe", mm, reads=WSA[s_slot] + xa(range(KC), Qi * 512, (Qi + 1) * 512), writes=bank_atoms(bk))
            dst = dst_fn(Qi)
            if eng == "act":
                P.op("act", lambda e, dst=dst, bk=bk: e.activation(out=dst, in_=PS[bk][:, :], func=AF.Copy, scale=scale),
                     reads=bank_atoms(bk), writes=dst_atoms_fn(Qi))
            else:
                P.op("dve", lambda e, dst=dst, bk=bk: e.tensor_copy(out=dst, in_=PS[bk][:, :]),
                     reads=bank_atoms(bk), writes=dst_atoms_fn(Qi))

    def phase_a_dil(l, g):
        s = load_w(l, f"dil{g}")
        wv = WS[s][:, 0:3072].rearrange("p (k n) -> p k n", n=384)
        QO, KO, VO, PO = 0, 2048, 4096, 8192
        proj_fm(s, wv, 0, lambda Qi: scr_bf(QO + Qi * 512, 512), lambda Qi: scr_at(QO + Qi * 512, 512), 0.125, "act")
        proj_fm(s, wv, 128, lambda Qi: scr_bf(KO + Qi * 512, 512), lambda Qi: scr_at(KO + Qi * 512, 512), 1.0, "dve")
        V1 = scr_bf(VO, 4096).rearrange("p (t h d) -> p t h d", h=2, d=128)
        V1a = scr_at(VO, 4096)
        P.op("dve", lambda e: e.memset(V1[:, :, :, 64:128], 1.0), writes=V1a)
        for tg in range(4):
            bk = PBANKS[pa["p"] % 2]
            pa["p"] += 1

            def mm(e, bk=bk, tg=tg):
                r = None
                for tb in range(4 * tg, 4 * tg + 4):
                    for kc in range(KC):
                        r = e.matmul(PS[bk][:, (tb % 4) * 128:(tb % 4) * 128 + 128], lhsT=XH[:, kc, tb * 128:(tb + 1) * 128],
                                     rhs=wv[:, kc, 256:384], start=(kc == 0), stop=(kc == KC - 1))
                return r
            P.op("pe", mm, reads=WSA[s] + xa(range(KC), tg * 512, (tg + 1) * 512), writes=bank_atoms(bk))
            dst = V1[:, 4 * tg:4 * tg + 4, :, 0:64]
            src = PS[bk][:, :].rearrange("p (t h d) -> p t h d", h=2, d=64)
            P.op("act", lambda e, dst=dst, src=src: e.activation(out=dst, in_=src, func=AF.Copy),
                 reads=bank_atoms(bk), writes=scr_at(VO + tg * 1024, 1024))
        heads = []
        for j in range(2):
            pb = 64 * j
            heads.append(dict(
                qT=lambda Qi, pb=pb: SCR[pb:pb + 64, QO + Qi * 512:QO + (Qi + 1) * 512],
                q_atoms=lambda Qi: scr_at(QO + Qi * 512, 512),
                kT=lambda kb, pb=pb: SCR[pb:pb + 64, KO + kb * 128:KO + (kb + 1) * 128],
                k_atoms=lambda kb: scr_at(KO + kb * 128, 128),
                v1=lambda kb, j=j: V1[:, kb, j, :],
                v_atoms=lambda kb: scr_at(VO + kb * 256, 256),
                sink=None,
                out=lambda Qi, pb=pb: YC[pb:pb + 64, g, Qi * 512:(Qi + 1) * 512],
                out_atoms=lambda Qi: ya(YCA, [g], Qi * 512, (Qi + 1) * 512)))
        attention_units(heads, MASKD, 19 * 128, lambda Qi: 0, PO)

    def phase_a_swa(l, kv):
        s = load_w(l, f"swa{kv}")
        wv = WS[s][:, 0:3584].rearrange("p (k n) -> p k n", n=448)
        QO, KO, VO, PO = 0, 4096, 6144, 8192
        for ch in range(2):
            proj_fm(s, wv, 128 * ch, lambda Qi, ch=ch: scr_bf(QO + ch * 2048 + Qi * 512, 512),
                    lambda Qi, ch=ch: scr_at(QO + ch * 2048 + Qi * 512, 512), 0.125, "act")
        proj_fm(s, wv, 256, lambda Qi: scr_bf(KO + Qi * 512, 512), lambda Qi: scr_at(KO + Qi * 512, 512), 1.0, "dve")
        V1 = scr_bf(VO, 2048).rearrange("p (t d) -> p t d", d=128)
        V1a = scr_at(VO, 2048)
        P.op("dve", lambda e: e.memset(V1[:, :, 64:128], 1.0), writes=V1a)
        for tg in range(2):
            bk = PBANKS[pa["p"] % 2]
            pa["p"] += 1

            def mm(e, bk=bk, tg=tg):
                r = None
                for tb in range(8 * tg, 8 * tg + 8):
                    for kc in range(KC):
                        r = e.matmul(PS[bk][:, (tb % 8) * 64:(tb % 8) * 64 + 64], lhsT=XH[:, kc, tb * 128:(tb + 1) * 128],
                                     rhs=wv[:, kc, 384:448], start=(kc == 0), stop=(kc == KC - 1))
                return r
            P.op("pe", mm, reads=WSA[s] + xa(range(KC), tg * 1024, (tg + 1) * 1024), writes=bank_atoms(bk))
            dst = V1[:, 8 * tg:8 * tg + 8, 0:64]
            src = PS[bk][:, :].rearrange("p (t d) -> p t d", d=64)
            P.op("act", lambda e, dst=dst, src=src: e.activation(out=dst, in_=src, func=AF.Copy),
                 reads=bank_atoms(bk), writes=scr_at(VO + tg * 1024, 1024))
        heads = []
        for j in range(4):
            pb = 64 * (j % 2)
            ch = j // 2
            hidx = 4 * kv + j
            cidx = 2 * kv + ch
            heads.append(dict(
                qT=lambda Qi, pb=pb, ch=ch: SCR[pb:pb + 64, QO + ch * 2048 + Qi * 512:QO + ch * 2048 + (Qi + 1) * 512],
                q_atoms=lambda Qi, ch=ch: scr_at(QO + ch * 2048 + Qi * 512, 512),
                kT=lambda kb, pb=pb: SCR[pb:pb + 64, KO + kb * 128:KO + (kb + 1) * 128],
                k_atoms=lambda kb: scr_at(KO + kb * 128, 128),
                v1=lambda kb: V1[:, kb, :],
                v_atoms=lambda kb: scr_at(VO + kb * 128, 128),
                sink=DER[64:128, 48 + hidx:48 + hidx + 1],
                out=lambda Qi, pb=pb, cidx=cidx: YB[pb:pb + 64, cidx, Qi * 512:(Qi + 1) * 512],
                out_atoms=lambda Qi, cidx=cidx: ya(YBA, [cidx], Qi * 512, (Qi + 1) * 512)))
        attention_units(heads, MASKS, 8 * 128, lambda Qi: max(0, 4 * Qi - 1), PO)

    HHO, YAO, MGO, OSO = 0, 5632, 7680, 9728
    Z1O, Z2O = 0, 5632

    def scr_f32(off_el, ncol):
        return SCR[:, off_el:off_el + 2 * ncol].bitcast(F32)

    SBANK1, SBANK2 = 6, 7

    def mm_group(hb, lhs_fn, rhs_fn, nk):
        def mm(e):
            r = None
            for k in range(nk):
                r = e.matmul(PSH(hb), lhsT=lhs_fn(k), rhs=rhs_fn(k), start=(k == 0), stop=(k == nk - 1))
            return r
        return mm

    def layer_norm_tile(l, i, zoff, gk, bk_, is_last_out, s):
        t0 = i * TT
        za = scr_at(zoff, 4096)
        mean, meana = tmp_fixed(NTMP - 2)
        P.op("act", lambda e: e.activation(out=mean[:, 0:256], in_=PS[SBANK1][:, 0:256], func=AF.Copy, scale=1.0 / D),
             reads=bank_atoms(SBANK1), writes=[meana])
        msq, msqa = tmp()
        P.op("dve", lambda e: e.scalar_tensor_tensor(out=msq[:, 0:256], in0=mean[:, 0:256], scalar=-1.0, in1=mean[:, 0:256],
                                                     op0=ALU.mult, op1=ALU.mult),
             reads=[meana], writes=[msqa])
        var, vara = tmp()
        P.op("dve", lambda e: e.scalar_tensor_tensor(out=var[:, 0:256], in0=PS[SBANK2][:, 0:256], scalar=1.0 / D,
                                                     in1=msq[:, 0:256], op0=ALU.mult, op1=ALU.add),
             reads=bank_atoms(SBANK2) + [msqa], writes=[vara])
        P.op("pool", lambda e: e.tensor_scalar(out=var[:, 0:256], in0=var[:, 0:256], scalar1=LN_EPS, scalar2=None, op0=ALU.add),
             reads=[vara], writes=[vara])
        rstd, rstda = tmp_fixed(NTMP - 1)
        P.op("pool", lambda e: e.tensor_tensor(out=rstd[:, 0:256], in0=var[:, 0:256], in1=NHALF[:, :], op=ALU.pow),
             reads=[vara, CONSTA], writes=[rstda])
        for c in range(KC):
            z = scr_f32(zoff + c * 512, 256)
            zc = scr_at(zoff + c * 512, 512)
            u, ua = tmp()
            P.op("dve", lambda e, u=u, z=z: e.tensor_tensor(out=u[:, 0:256], in0=z, in1=mean[:, 0:256], op=ALU.subtract),
                 reads=zc + [meana], writes=[ua])
            P.op("dve", lambda e, u=u: e.tensor_tensor(out=u[:, 0:256], in0=u[:, 0:256], in1=rstd[:, 0:256], op=ALU.mult),
                 reads=[ua, rstda], writes=[ua])
            gcol = pcv(l, gk + c)
            bcol = pcv(l, bk_ + c)
            if is_last_out:
                v = scr_f32(OSO, 256)
                va = scr_at(OSO, 512)
            else:
                v, va1 = tmp()
                v = v[:, 0:256]
                va = [va1]
            P.op("act", lambda e, v=v, u=u, gcol=gcol, bcol=bcol: e.activation(out=v, in_=u[:, 0:256], func=AF.Identity,
                                                                              scale=gcol, bias=bcol),
                 reads=[ua, CONSTA], writes=va)
            if is_last_out:
                dst = oT[s, c * 128:(c + 1) * 128, t0:t0 + TT]
                P.dma("sp", lambda e, v=v, dst=dst: e.dma_start(out=dst, in_=v), "os0", reads=va, writes=[])
            else:
                hi = XH[:, c, t0:t0 + TT]
                lo = XL[:, c, t0:t0 + TT]
                xat = xa([c], t0, t0 + TT)
                P.op("act", lambda e, hi=hi, v=v: e.activation(out=hi, in_=v, func=AF.Copy), reads=va, writes=xat)
                P.op("pool", lambda e, lo=lo, hi=hi, v=v: e.tensor_tensor(out=lo, in0=v, in1=hi, op=ALU.subtract),
                     reads=va + xat, writes=xat)

    def residual_and_stats(i, c, hz, zoff):
        t0 = i * TT
        xf, xfa = tmp()
        xat = xa([c], t0, t0 + TT)
        P.op("pool", lambda e, xf=xf: e.tensor_tensor(out=xf[:, 0:256], in0=XH[:, c, t0:t0 + TT], in1=XL[:, c, t0:t0 + TT],
                                                      op=ALU.add),
             reads=xat, writes=[xfa])
        z = scr_f32(zoff + c * 512, 256)
        zc = scr_at(zoff + c * 512, 512)
        P.op("dve", lambda e, xf=xf, z=z: e.scalar_tensor_tensor(out=z, in0=xf[:, 0:256], scalar=ALPHA, in1=PSH(hz),
                                                                 op0=ALU.mult, op1=ALU.add),
             reads=[xfa, PSA[hz]], writes=zc)
        sq, sqa = tmp()
        P.op("act", lambda e, sq=sq, z=z: e.activation(out=sq[:, 0:256], in_=z, func=AF.Square), reads=zc, writes=[sqa])
        P.op("pe", lambda e, z=z: e.matmul(PS[SBANK1][:, 0:256], lhsT=ONES32[:, :], rhs=z, start=(c == 0), stop=(c == KC - 1)),
             reads=zc + [CONSTA], writes=bank_atoms(SBANK1))
        P.op("pe", lambda e, sq=sq: e.matmul(PS[SBANK2][:, 0:256], lhsT=ONES32[:, :], rhs=sq[:, 0:256], start=(c == 0),
                                             stop=(c == KC - 1)),
             reads=[sqa, CONSTA], writes=bank_atoms(SBANK2))

    def phase_c_tile(l, i, is_last_out, s):
        t0 = i * TT
        XHt = lambda kc: XH[:, kc, t0:t0 + TT]
        xall = xa(range(KC), t0, t0 + TT)
        YAv = scr_bf(YAO, 2048).rearrange("p (c t) -> p c t", t=256)
        MGv = scr_bf(MGO, 2048).rearrange("p (c t) -> p c t", t=256)
        HHv = scr_bf(HHO, 5632).rearrange("p (j t) -> p j t", t=256)
        BDv = BD[:, :].rearrange("p (g c n) -> p g c n", g=2, c=8)
        lru_proj = {}

        def issue_lru_proj(c):
            if c % 2 == 0:
                lru_proj["sl"] = load_w(l, f"lru{c // 2}")
            sl = lru_proj["sl"]
            w4 = WS[sl][:, 0:4096].rearrange("p (k n) -> p k n", n=512)
            base = (c % 2) * 256
            hx = next_hb()
            P.op("pe", mm_group(hx, lambda k, w4=w4, base=base: w4[:, k, base:base + 128], XHt, KC),
                 reads=WSA[sl] + xall, writes=[PSA[hx]])
            hg = next_hb()
            P.op("pe", mm_group(hg, lambda k, w4=w4, base=base: w4[:, k, base + 128:base + 256], XHt, KC),
                 reads=WSA[sl] + xall, writes=[PSA[hg]])
            lru_proj[c] = (hx, hg)

        issue_lru_proj(0)
        for c in range(KC):
            hx, hg = lru_proj[c]
            lx, lxa = tmp()
            if i == 0:
                P.op("dve", lambda e, lx=lx: e.memset(lx[:, 0:3], 0.0), writes=[lxa])
            else:
                P.op("act", lambda e, lx=lx, c=c: e.activation(out=lx[:, 0:3], in_=CARRY[:, c, :], func=AF.Copy),
                     reads=[CARRYA[c]], writes=[lxa])
            P.op("act", lambda e, lx=lx, hx=hx: e.activation(out=lx[:, 3:259], in_=PSH(hx), func=AF.Copy),
                 reads=[PSA[hx]], writes=[lxa])
            P.op("act", lambda e, lx=lx, c=c: e.activation(out=CARRY[:, c, :], in_=lx[:, 256:259], func=AF.Copy),
                 reads=[lxa], writes=[CARRYA[c]])
            xg, xga = tmp()
            P.op("act", lambda e, xg=xg, hg=hg: e.activation(out=xg[:, 0:256], in_=PSH(hg), func=AF.Copy),
                 reads=[PSA[hg]], writes=[xga])
            xc, xca = tmp()
            P.op("dve", lambda e, xc=xc, lx=lx, c=c: e.tensor_scalar(out=xc[:, 0:256], in0=lx[:, 3:259], scalar1=pcv(l, 24 + c),
                                                                     scalar2=pcv(l, 32 + c), op0=ALU.mult, op1=ALU.add),
                 reads=[lxa, CONSTA], writes=[xca])
            for jj in (2, 1, 0):
                P.op("dve", lambda e, xc=xc, lx=lx, c=c, jj=jj: e.scalar_tensor_tensor(
                    out=xc[:, 0:256], in0=lx[:, jj:jj + 256], scalar=pcv(l, 8 * jj + c), in1=xc[:, 0:256],
                    op0=ALU.mult, op1=ALU.add),
                    reads=[lxa, xca, CONSTA], writes=[xca])
            xcb, xcba = tmpb()
            P.op("act", lambda e, xcb=xcb, xc=xc: e.activation(out=xcb, in_=xc[:, 0:256], func=AF.Copy),
                 reads=[xca], writes=[xcba])
            if c + 1 < KC:
                issue_lru_proj(c + 1)
            hr = next_hb()
            P.op("pe", lambda e, hr=hr, c=c, xcb=xcb: e.matmul(PSH(hr), lhsT=BDv[:, 0, c, :], rhs=xcb, start=True, stop=True),
                 reads=BDA + [xcba], writes=[PSA[hr]])
            hi_ = next_hb()
            P.op("pe", lambda e, hi_=hi_, c=c, xcb=xcb: e.matmul(PSH(hi_), lhsT=BDv[:, 1, c, :], rhs=xcb, start=True, stop=True),
                 reads=BDA + [xcba], writes=[PSA[hi_]])
            tr, tra = tmp()
            P.op("act", lambda e, tr=tr, hr=hr, c=c: e.activation(out=tr[:, 0:256], in_=PSH(hr), func=AF.Tanh, scale=0.5,
                                                                 bias=DER[:, 32 + c:33 + c]),
                 reads=[PSA[hr], DERA], writes=[tra])
            ti, tia = tmp()
            P.op("act", lambda e, ti=ti, hi_=hi_, c=c: e.activation(out=ti[:, 0:256], in_=PSH(hi_), func=AF.Tanh, scale=0.5,
                                                                   bias=DER[:, 40 + c:41 + c]),
                 reads=[PSA[hi_], DERA], writes=[tia])
            av, ava = tmp()
            P.op("act", lambda e, av=av, tr=tr, c=c: e.activation(out=av[:, 0:256], in_=tr[:, 0:256], func=AF.Exp,
                                                                 scale=DER[:, 24 + c:25 + c], bias=DER[:, 24 + c:25 + c]),
                 reads=[tra, DERA], writes=[ava])
            a2, a2a = tmp()
            P.op("act", lambda e, a2=a2, tr=tr, c=c: e.activation(out=a2[:, 0:256], in_=tr[:, 0:256], func=AF.Exp,
                                                                 scale=DER[:, 16 + c:17 + c], bias=DER[:, 16 + c:17 + c]),
                 reads=[tra, DERA], writes=[a2a])
            P.op("pool", lambda e, a2=a2: e.tensor_scalar(out=a2[:, 0:256], in0=a2[:, 0:256], scalar1=-1.0, scalar2=1.0,
                                                          op0=ALU.mult, op1=ALU.add),
                 reads=[a2a], writes=[a2a])
            P.op("pool", lambda e, a2=a2: e.tensor_tensor(out=a2[:, 0:256], in0=a2[:, 0:256], in1=HALF[:, :], op=ALU.pow),
                 reads=[a2a, CONSTA], writes=[a2a])
            P.op("dve", lambda e, ti=ti, xc=xc: e.scalar_tensor_tensor(out=ti[:, 0:256], in0=ti[:, 0:256], scalar=1.0,
                                                                       in1=xc[:, 0:256], op0=ALU.add, op1=ALU.mult),
                 reads=[tia, xca], writes=[tia])
            P.op("dve", lambda e, ti=ti, a2=a2: e.scalar_tensor_tensor(out=ti[:, 0:256], in0=ti[:, 0:256], scalar=0.5,
                                                                       in1=a2[:, 0:256], op0=ALU.mult, op1=ALU.mult),
                 reads=[tia, a2a], writes=[tia])
            hh, hha = tmp()
            if i == 0:
                P.op("dve", lambda e, hh=hh, av=av, ti=ti: e.tensor_tensor_scan(
                    out=hh[:, 0:256], data0=av[:, 0:256], data1=ti[:, 0:256], initial=0.0, op0=ALU.mult, op1=ALU.add),
                    reads=[ava, tia], writes=[hha])
            else:
                P.op("dve", lambda e, hh=hh, av=av, ti=ti, c=c: e.tensor_tensor_scan(
                    out=hh[:, 0:256], data0=av[:, 0:256], data1=ti[:, 0:256], initial=HC[:, c:c + 1], op0=ALU.mult,
                    op1=ALU.add),
                    reads=[ava, tia, HCA[c]], writes=[hha])
            P.op("act", lambda e, hh=hh, c=c: e.activation(out=HC[:, c:c + 1], in_=hh[:, 255:256], func=AF.Copy),
                 reads=[hha], writes=[HCA[c]])
            x2, x2a = tmp()
            P.op("pool", lambda e, x2=x2, xg=xg: e.tensor_tensor(out=x2[:, 0:256], in0=xg[:, 0:256], in1=xg[:, 0:256], op=ALU.mult),
                 reads=[xga], writes=[x2a])
            P.op("pool", lambda e, x2=x2: e.tensor_scalar(out=x2[:, 0:256], in0=x2[:, 0:256], scalar1=0.044715, scalar2=1.0,
                                                          op0=ALU.mult, op1=ALU.add),
                 reads=[x2a], writes=[x2a])
            P.op("pool", lambda e, x2=x2, xg=xg: e.tensor_tensor(out=x2[:, 0:256], in0=x2[:, 0:256], in1=xg[:, 0:256], op=ALU.mult),
                 reads=[x2a, xga], writes=[x2a])
            P.op("act", lambda e, x2=x2: e.activation(out=x2[:, 0:256], in_=x2[:, 0:256], func=AF.Tanh, scale=0.7978845608028654),
                 reads=[x2a], writes=[x2a])
            P.op("dve", lambda e, x2=x2, xg=xg: e.scalar_tensor_tensor(out=x2[:, 0:256], in0=x2[:, 0:256], scalar=1.0,
                                                                       in1=xg[:, 0:256], op0=ALU.add, op1=ALU.mult),
                 reads=[x2a, xga], writes=[x2a])
            P.op("dve", lambda e, x2=x2, hh=hh, c=c: e.scalar_tensor_tensor(out=YAv[:, c, :], in0=x2[:, 0:256], scalar=0.5,
                                                                            in1=hh[:, 0:256], op0=ALU.mult, op1=ALU.mult),
                 reads=[x2a, hha], writes=scr_at(YAO + c * 256, 256))
        Ys = [lambda k: YAv[:, k, :], lambda k: YB[:, k, t0:t0 + TT], lambda k: YC[:, k, t0:t0 + TT]]
        Yat = [scr_at(YAO, 2048), ya(YBA, range(KC), t0, t0 + TT), ya(YCA, range(KC), t0, t0 + TT)]
        for c in range(KC):
            sg = load_w(l, f"gate{c}")
            g3 = WS[sg][:, 0:3072].rearrange("p (k n) -> p k n", n=384)
            sbr = load_w(l, f"br{c}")
            b3 = WS[sbr][:, 0:3072].rearrange("p (k n) -> p k n", n=384)
            acc = None
            for n in range(3):
                hgn = next_hb()
                P.op("pe", mm_group(hgn, lambda k, g3=g3, n=n: g3[:, k, 128 * n:128 * n + 128], XHt, KC),
                     reads=WSA[sg] + xall, writes=[PSA[hgn]])
                hbn = next_hb()
                P.op("pe", mm_group(hbn, lambda k, b3=b3, n=n: b3[:, k, 128 * n:128 * n + 128], Ys[n], KC),
                     reads=WSA[sbr] + Yat[n], writes=[PSA[hbn]])
                sgm, sga = tmp()
                P.op("act", lambda e, sgm=sgm, hgn=hgn: e.activation(out=sgm[:, 0:256], in_=PSH(hgn), func=AF.Sigmoid),
                     reads=[PSA[hgn]], writes=[sga])
                P.op("dve", lambda e, sgm=sgm, hbn=hbn: e.tensor_tensor(out=sgm[:, 0:256], in0=sgm[:, 0:256], in1=PSH(hbn),
                                                                        op=ALU.mult),
                     reads=[sga, PSA[hbn]], writes=[sga])
                if n == 0:
                    acc, acca = sgm, sga
                elif n == 1:
                    P.op("pool", lambda e, acc=acc, sgm=sgm: e.tensor_tensor(out=acc[:, 0:256], in0=acc[:, 0:256],
                                                                             in1=sgm[:, 0:256], op=ALU.add),
                         reads=[acca, sga], writes=[acca])
                else:
                    P.op("pool", lambda e, acc=acc, sgm=sgm, c=c: e.tensor_tensor(out=MGv[:, c, :], in0=acc[:, 0:256],
                                                                                  in1=sgm[:, 0:256], op=ALU.add),
                         reads=[acca, sga], writes=scr_at(MGO + c * 256, 256))
        if KDUMP and s == 0 and l == 0:
            P.dma("sp", lambda e: e.dma_start(out=dbg_ya[:, :, t0:t0 + TT], in_=YAv), "dbg%d" % next(_dbgc), reads=scr_at(YAO, 2048), writes=[])
            P.dma("sp", lambda e: e.dma_start(out=dbg_mg[:, :, t0:t0 + TT], in_=MGv), "dbg%d" % next(_dbgc), reads=scr_at(MGO, 2048), writes=[])
        mga = scr_at(MGO, 2048)
        for o in range(2):
            so = load_w(l, f"out{o}")
            w4 = WS[so][:, 0:4096].rearrange("p (k n) -> p k n", n=512)
            for cc in range(4):
                c = 4 * o + cc
                hz = next_hb()
                P.op("pe", mm_group(hz, lambda k, w4=w4, cc=cc: w4[:, k, 128 * cc:128 * cc + 128], lambda k: MGv[:, k, :], KC),
                     reads=WSA[so] + mga, writes=[PSA[hz]])
                residual_and_stats(i, c, hz, Z1O)
        layer_norm_tile(l, i, Z1O, 64, 72, False, s)
        for jp in range(11):
            sf = load_w(l, f"ffi{jp}")
            w4 = WS[sf][:, 0:4096].rearrange("p (k n) -> p k n", n=512)
            for jj in range(2):
                j = 2 * jp + jj
                h1 = next_hb()
                P.op("pe", mm_group(h1, lambda k, w4=w4, jj=jj: w4[:, k, 256 * jj:256 * jj + 128], XHt, KC),
                     reads=WSA[sf] + xall, writes=[PSA[h1]])
                h3 = next_hb()
                P.op("pe", mm_group(h3, lambda k, w4=w4, jj=jj: w4[:, k, 256 * jj + 128:256 * jj + 256], XHt, KC),
                     reads=WSA[sf] + xall, writes=[PSA[h3]])
                sl_, sla = tmp()
                P.op("act", lambda e, sl_=sl_, h1=h1: e.activation(out=sl_[:, 0:256], in_=PSH(h1), func=AF.Silu),
                     reads=[PSA[h1]], writes=[sla])
                P.op("dve", lambda e, sl_=sl_, h3=h3, j=j: e.tensor_tensor(out=HHv[:, j, :], in0=sl_[:, 0:256], in1=PSH(h3),
                                                                          op=ALU.mult),
                     reads=[sla, PSA[h3]], writes=scr_at(HHO + j * 256, 256))
        hha_all = scr_at(HHO, 5632)
        for c in range(KC):
            so = load_w(l, f"ffo{c}")
            w22 = WS[so][:, 0:2816].rearrange("p (j n) -> p j n", n=128)
            hz = next_hb()
            P.op("pe", mm_group(hz, lambda k, w22=w22: w22[:, k, :], lambda k: HHv[:, k, :], NJ),
                 reads=WSA[so] + hha_all, writes=[PSA[hz]])
            residual_and_stats(i, c, hz, Z2O)
        layer_norm_tile(l, i, Z2O, 80, 88, is_last_out, s)

    import os
    PH = os.environ.get("KDEBUG", "xksdc")
    NSQ = int(os.environ.get("KNSEQ", NSEQ))
    convert_blocks(0, 0, BLOB_LEN)
    per_tile = (BLOB_LEN + NTL - 1) // NTL
    for s in range(NSQ):
        if "x" in PH:
            load_x(s)
        for l in range(n_layers):
            if "k" in PH:
                layer_consts(l)
            if "s" in PH:
                for kv in range(int(os.environ.get("KNSWA", 4))):
                    phase_a_swa(l, kv)
            if "d" in PH:
                for g in range(int(os.environ.get("KNDIL", 8))):
                    phase_a_dil(l, g)
            if KDUMP and s == 0 and l == 0:
                P.dma("sp", lambda e: e.dma_start(out=dbg_yb[:, :, :], in_=YB[:, :, :]), "dbg",
                      reads=ya(YBA, range(KC), 0, T), writes=[])
                P.dma("sp", lambda e: e.dma_start(out=dbg_yc[:, :, :], in_=YC[:, :, :]), "dbg",
                      reads=ya(YCA, range(KC), 0, T), writes=[])
                P.dma("sp", lambda e: e.dma_start(out=dbg_xh[:, :, :], in_=XH[:, :, :]), "dbg",
                      reads=xa(range(KC), 0, T), writes=[])
            if "c" in PH:
                for i in range(int(os.environ.get("KNTL", NTL))):
                    if s == 0 and l + 1 < n_layers:
                        convert_blocks(l + 1, i * per_tile, min(BLOB_LEN, (i + 1) * per_tile))
                    phase_c_tile(l, i, last_flags[l], s)
                if KDUMP and s == 0 and l == 0:
                    P.dma("sp", lambda e: e.dma_start(out=dbg_x1[:, :, :], in_=XH[:, :, :]), "dbg%d" % next(_dbgc),
                          reads=xa(range(KC), 0, T), writes=[])

    total_const = P.dma_count.get("const", 0)
    for e_ in P.ENGS:
        for op in P.ops[e_]:
            if op.isdma and op.semkey == "const":
                op.semval = total_const
    for e_ in P.ENGS:
        cnt = 0
        for op in P.ops[e_]:
            if op.sig and not op.isdma:
                cnt += 1
                op.sigval = cnt
    sem_names = ["e_" + e_ for e_ in P.ENGS] + ["d_" + k for k in sorted(P.dma_count)]
    sems = {n: es.enter_context(nc.semaphore(n)) for n in sem_names}
    handles = {"pe": nc.tensor, "act": nc.scalar, "dve": nc.vector, "pool": nc.gpsimd, "sp": nc.sync}

    def emit(eng, e):
        waited = {}
        for op in P.ops[eng]:
            need = {}
            for d in op.deps:
                if d.isdma:
                    key, val = "d_" + d.semkey, d.semval
                else:
                    key, val = "e_" + d.eng, d.sigval
                if need.get(key, 0) < val:
                    need[key] = val
            for key, val in need.items():
                if waited.get(key, 0) < val:
                    e.wait_ge(sems[key], val)
                    waited[key] = val
            inst = op.fn(e)
            if op.isdma:
                inst.then_inc(sems["d_" + op.semkey], 16)
            elif op.sig:
                inst.then_inc(sems["e_" + eng], 1)
        if eng == "sp":
            for k_, v_ in sorted(P.dma_count.items()):
                e.wait_ge(sems["d_" + k_], v_)

    block = es.enter_context(nc.Block())

    @block.tensor
    def _(e):
        emit("pe", e)

    @block.scalar
    def _(e):
        emit("act", e)

    @block.vector
    def _(e):
        emit("dve", e)

    @block.gpsimd
    def _(e):
        emit("pool", e)

    @block.sync
    def _(e):
        emit("sp", e)

    es.close()
    stats = {k: len(v) for k, v in P.ops.items()}
    return nc, stats


_CACHE = {}


def _get_prog(n_layers, last_flags):
    key = (n_layers, tuple(last_flags))
    if key not in _CACHE:
        _CACHE[key] = build_program(n_layers, list(last_flags))
    return _CACHE[key][0]


def kernel(x, w_in, conv_w, conv_b, w_rg, b_rg, w_ig, b_ig, lru_lambda, sinks, w_branch, w_out,
           ln1_g, ln1_b, w_ffn_in, w_ffn_out, ln2_g, ln2_b):
    f = lambda a: np.asarray(a, dtype=np.float32)
    x = f(x); w_in = f(w_in); conv_w = f(conv_w); conv_b = f(conv_b); w_rg = f(w_rg); b_rg = f(b_rg)
    w_ig = f(w_ig); b_ig = f(b_ig); lru_lambda = f(lru_lambda); sinks = f(sinks); w_branch = f(w_branch)
    w_out = f(w_out); ln1_g = f(ln1_g); ln1_b = f(ln1_b); w_ffn_in = f(w_ffn_in); w_ffn_out = f(w_ffn_out)
    ln2_g = f(ln2_g); ln2_b = f(ln2_b)
    maskd, masks = _build_masks()
    blobs = [_build_blob(l, w_in, w_rg, w_ig, w_branch, w_out, w_ffn_in, w_ffn_out) for l in range(DEPTH)]
    pcols = [_build_pcol(l, conv_w, conv_b, b_rg, b_ig, lru_lambda, sinks, ln1_g, ln1_b, ln2_g, ln2_b)
             for l in range(DEPTH)]
    xT = np.ascontiguousarray(x.transpose(0, 2, 1))
    cur = [np.ascontiguousarray(xT[NSEQ * c:NSEQ * (c + 1)]) for c in range(NCORES)]
    if FUSED:
        nc = _get_prog(DEPTH, [False] * (DEPTH - 1) + [True])
        blob = np.ascontiguousarray(np.stack(blobs, 0))
        pc = np.ascontiguousarray(np.concatenate(pcols, axis=1))
        in_maps = [{"xT": cur[c], "blob": blob, "pcol": pc, "maskd": maskd, "masks": masks} for c in range(NCORES)]
        res = run_bass_kernel_spmd(nc, in_maps, core_ids=list(range(NCORES)))
        cur = [np.asarray(res.results[c]["oT"]) for c in range(NCORES)]
    else:
        nc = _get_prog(1, [True])
        for l in range(DEPTH):
            blob = np.ascontiguousarray(blobs[l][None])
            in_maps = [{"xT": cur[c], "blob": blob, "pcol": pcols[l], "maskd": maskd, "masks": masks}
                       for c in range(NCORES)]
            res = run_bass_kernel_spmd(nc, in_maps, core_ids=list(range(NCORES)))
            cur = [np.asarray(res.results[c]["oT"]) for c in range(NCORES)]
    outT = np.concatenate(cur, axis=0)
    return np.ascontiguousarray(outT.transpose(0, 2, 1)).astype(np.float32)
```

```python
import numpy as np
import concourse.bass as bass
import concourse.mybir as mybir
from concourse.bass_utils import run_bass_kernel_spmd

F32 = mybir.dt.float32
BF16 = mybir.dt.bfloat16
AF = mybir.ActivationFunctionType
ALU = mybir.AluOpType

D = 1024
T = 2048
DEPTH = 4
KC = 8
TT = 256
NTL = T // TT
FFH = 2816
NJ = FFH // 128
NSEQ = 2
NCORES = 8
ALPHA = float((2.0 * DEPTH) ** 0.25)
LN_EPS = 1e-5
NS = 3
NTMP = 12
TMPW = 264
NPC = 120

FUSED = True


def _chunked(wsub):
    n = wsub.shape[1]
    return np.ascontiguousarray(wsub.reshape(KC, 128, n).transpose(1, 0, 2).reshape(128, KC * n))


def _tile_table():
    names = []
    for kv in range(4):
        names.append((f"swa{kv}", 8 * 448))
    for g in range(8):
        names.append((f"dil{g}", 8 * 384))
    names.append(("bd", 2048))
    for cp in range(4):
        names.append((f"lru{cp}", 4096))
    for c in range(8):
        names.append((f"gate{c}", 3072))
        names.append((f"br{c}", 3072))
    for o in range(2):
        names.append((f"out{o}", 4096))
    for jp in range(11):
        names.append((f"ffi{jp}", 4096))
    for c in range(8):
        names.append((f"ffo{c}", 2816))
    table = {}
    off = 0
    for n, s in names:
        nb = (s + 511) // 512
        table[n] = (off, s, nb)
        off += nb
    return table, off


TILES, BLOB_LEN = _tile_table()


def _build_blob(l, w_in, w_rg, w_ig, w_branch, w_out, w_ffn_in, w_ffn_out):
    blob = np.zeros((BLOB_LEN, 128, 512), np.float32)
    wi = w_in[l]

    def put(name, arr):
        off, size, nb = TILES[name]
        assert arr.shape == (128, size), (name, arr.shape, size)
        for b in range(nb):
            w = min(512, size - 512 * b)
            blob[off + b, :, 0:w] = arr[:, 512 * b:512 * b + w]

    for kv in range(4):
        cols = np.concatenate([
            np.arange(2048 + 256 * kv, 2048 + 256 * kv + 256),
            np.arange(3072 + 64 * kv, 3072 + 64 * kv + 64),
            np.arange(3072 + 64 * kv, 3072 + 64 * kv + 64),
            np.arange(3328 + 64 * kv, 3328 + 64 * kv + 64)])
        put(f"swa{kv}", _chunked(wi[:, cols]))
    for g in range(8):
        cols = np.concatenate([np.arange(3584 + 128 * g, 3584 + 128 * g + 128),
                               np.arange(4608 + 128 * g, 4608 + 128 * g + 128),
                               np.arange(5632 + 128 * g, 5632 + 128 * g + 128)])
        put(f"dil{g}", _chunked(wi[:, cols]))
    bd = np.zeros((128, 2, 8, 128), np.float32)
    for gi, wg in enumerate((w_rg[l], w_ig[l])):
        for c in range(8):
            for b in range(2):
                bd[64 * b:64 * b + 64, gi, c, 64 * b:64 * b + 64] = wg[2 * c + b]
    put("bd", bd.reshape(128, 2048))
    for cp in range(4):
        cols = []
        for c in (2 * cp, 2 * cp + 1):
            cols.append(np.arange(128 * c, 128 * c + 128))
            cols.append(np.arange(1024 + 128 * c, 1024 + 128 * c + 128))
        put(f"lru{cp}", _chunked(wi[:, np.concatenate(cols)]))
    for c in range(8):
        cols = np.concatenate([np.arange(6656 + 1024 * n + 128 * c, 6656 + 1024 * n + 128 * c + 128) for n in range(3)])
        put(f"gate{c}", _chunked(wi[:, cols]))
        br = np.concatenate([w_branch[l, n][:, 128 * c:128 * c + 128] for n in range(3)], axis=1)
        put(f"br{c}", _chunked(br))
    for o in range(2):
        put(f"out{o}", _chunked(w_out[l][:, 512 * o:512 * o + 512]))
    wf = w_ffn_in[l]
    for jp in range(11):
        cols = []
        for j in (2 * jp, 2 * jp + 1):
            cols.append(np.arange(128 * j, 128 * j + 128))
            cols.append(np.arange(FFH + 128 * j, FFH + 128 * j + 128))
        put(f"ffi{jp}", _chunked(wf[:, np.concatenate(cols)]))
    wo = w_ffn_out[l]
    for c in range(8):
        sub = wo[:, 128 * c:128 * c + 128].reshape(NJ, 128, 128).transpose(1, 0, 2).reshape(128, NJ * 128)
        put(f"ffo{c}", np.ascontiguousarray(sub))
    return blob


def _build_pcol(l, conv_w, conv_b, b_rg, b_ig, lru_lambda, sinks, ln1_g, ln1_b, ln2_g, ln2_b):
    pc = np.zeros((128, NPC), np.float32)

    def col(v):
        return v.reshape(8, 128).T

    for j in range(4):
        pc[:, 8 * j:8 * j + 8] = col(conv_w[l, j])
    pc[:, 32:40] = col(conv_b[l])
    pc[:, 40:48] = col(b_rg[l])
    pc[:, 48:56] = col(b_ig[l])
    pc[:, 56:64] = col(lru_lambda[l])
    pc[:, 64:72] = col(ln1_g[l])
    pc[:, 72:80] = col(ln1_b[l])
    pc[:, 80:88] = col(ln2_g[l])
    pc[:, 88:96] = col(ln2_b[l])
    pc[:, 104:120] = np.broadcast_to(sinks[l][None, :], (128, 16))
    return pc


def _build_masks():
    ki = np.arange(128)[:, None]
    j = np.arange(19 * 128)[None, :]
    dist = 128 * (j // 128 - 3) + (j % 128) - ki
    m = ((dist >= 0) & (dist <= 128)).astype(np.float32)
    m += ((dist >= 0) & (dist % 4 == 0) & (dist <= 512)).astype(np.float32)
    m += ((dist >= 0) & (dist % 16 == 0) & (dist <= 2048)).astype(np.float32)
    j2 = np.arange(8 * 128)[None, :]
    dist2 = 128 * (j2 // 128 - 3) + (j2 % 128) - ki
    m2 = ((dist2 >= 0) & (dist2 <= 128)).astype(np.float32)
    mp = np.zeros((128, 2560), np.float32)
    mp[:, :19 * 128] = m
    mb = np.ascontiguousarray(mp.reshape(128, 5, 512).transpose(1, 0, 2))
    m2b = np.ascontiguousarray(m2.reshape(128, 2, 512).transpose(1, 0, 2))
    return mb, m2b


class Atom:
    __slots__ = ("w", "r")

    def __init__(self):
        self.w = None
        self.r = []


class Op:
    __slots__ = ("eng", "fn", "deps", "sig", "sigval", "isdma", "semkey", "semval")


class Prog:
    ENGS = ("pe", "act", "dve", "pool", "sp")

    def __init__(self):
        self.ops = {e: [] for e in self.ENGS}
        self.dma_count = {}

    def _add(self, eng, fn, reads, writes, isdma=False, semkey=None):
        op = Op()
        op.eng = eng
        op.fn = fn
        op.sig = False
        op.sigval = 0
        op.isdma = isdma
        op.semkey = semkey
        op.semval = 0
        deps = {}
        for a in reads:
            w = a.w
            if w is None:
                continue
            if w.isdma or isdma or w.eng != eng or eng != "pe":
                deps[id(w)] = w
        for a in writes:
            w = a.w
            if w is not None and (w.isdma or isdma or w.eng != eng):
                deps[id(w)] = w
            for r in a.r:
                if r.isdma or isdma or r.eng != eng:
                    deps[id(r)] = r
        op.deps = list(deps.values())
        for d in op.deps:
            if not d.isdma:
                d.sig = True
        for a in reads:
            a.r.append(op)
        for a in writes:
            a.w = op
            a.r = []
        if isdma:
            n = self.dma_count.get(semkey, 0) + 16
            self.dma_count[semkey] = n
            op.semval = n
        self.ops[eng].append(op)
        return op

    def op(self, eng, fn, reads=(), writes=()):
        return self._add(eng, fn, reads, writes)

    def dma(self, queue, fn, semkey, reads=(), writes=()):
        return self._add(queue, fn, reads, writes, isdma=True, semkey=semkey)


def _flat(*items):
    out = []
    for it in items:
        if isinstance(it, Atom):
            out.append(it)
        else:
            out.extend(_flat(*it))
    return out


def build_program(n_layers, last_flags):
    nc = bass.Bass("TRN2", target_bir_lowering=False)
    P = Prog()

    xT = nc.dram_tensor("xT", [NSEQ, D, T], F32, kind="ExternalInput").ap()
    blob = nc.dram_tensor("blob", [n_layers, BLOB_LEN, 128, 512], F32, kind="ExternalInput").ap()
    pcol = nc.dram_tensor("pcol", [128, n_layers * NPC], F32, kind="ExternalInput").ap()
    maskd_d = nc.dram_tensor("maskd", [5, 128, 512], F32, kind="ExternalInput").ap()
    masks_d = nc.dram_tensor("masks", [2, 128, 512], F32, kind="ExternalInput").ap()
    oT = nc.dram_tensor("oT", [NSEQ, D, T], F32, kind="ExternalOutput").ap()
    wbf = nc.dram_tensor("wbf", [n_layers, BLOB_LEN, 128, 512], BF16, kind="Internal").ap()
    import os as _os2
    KDUMP = _os2.environ.get("KDUMP", "") == "1"
    import itertools as _it
    _dbgc = _it.count()
    if KDUMP:
        dbg_yb = nc.dram_tensor("dbg_yb", [128, KC, T], BF16, kind="ExternalOutput").ap()
        dbg_yc = nc.dram_tensor("dbg_yc", [128, KC, T], BF16, kind="ExternalOutput").ap()
        dbg_ya = nc.dram_tensor("dbg_ya", [128, KC, T], BF16, kind="ExternalOutput").ap()
        dbg_mg = nc.dram_tensor("dbg_mg", [128, KC, T], BF16, kind="ExternalOutput").ap()
        dbg_xh = nc.dram_tensor("dbg_xh", [128, KC, T], BF16, kind="ExternalOutput").ap()
        dbg_x1 = nc.dram_tensor("dbg_x1", [128, KC, T], BF16, kind="ExternalOutput").ap()

    from contextlib import ExitStack
    es = ExitStack()

    def sb(name, shape, dt):
        return es.enter_context(nc.sbuf_tensor(name, shape, dt))

    XH = sb("XH", [128, KC, T], BF16)
    XL = sb("XL", [128, KC, T], BF16)
    YB = sb("YB", [128, KC, T], BF16)
    YC = sb("YC", [128, KC, T], BF16)
    WS = [sb(f"WS{i}", [128, 4096], BF16) for i in range(NS)]
    MASKD = sb("MASKD", [128, 2560], BF16)
    MASKS = sb("MASKS", [128, 8 * 128], BF16)
    BD = sb("BD", [128, 2048], BF16)
    SCR = sb("SCR", [128, 10240], BF16)
    TMP = sb("TMP", [128, NTMP * TMPW], F32)
    TB = sb("TB", [128, 4 * 256], BF16)
    HALF = sb("HALF", [128, 256], F32)
    NHALF = sb("NHALF", [128, 256], F32)
    PCOL = sb("PCOL", [128, n_layers * NPC], F32)
    DER = sb("DER", [128, 64], F32)
    ONES32 = sb("ONES32", [128, 128], F32)
    CARRY = sb("CARRY", [128, 8, 3], F32)
    HC = sb("HC", [128, 8], F32)
    PS = [es.enter_context(nc.psum_tensor(f"PS{i}", [128, 512], F32)) for i in range(8)]


    def scr_bf(off_el, n_el):
        return SCR[:, off_el:off_el + n_el]

    scr_atoms = [Atom() for _ in range(20)]

    def scr_at(off_el, n_el):
        return scr_atoms[off_el // 512:(off_el + n_el + 511) // 512]

    XA = [[Atom() for _ in range(NTL)] for _ in range(KC)]
    YBA = [[Atom() for _ in range(NTL)] for _ in range(KC)]
    YCA = [[Atom() for _ in range(NTL)] for _ in range(KC)]
    WSB = [[Atom() for _ in range(8)] for _ in range(NS)]
    WSA = WSB
    TMPA = [Atom() for _ in range(NTMP)]
    TBA = [Atom() for _ in range(4)]
    PSA = [Atom() for _ in range(8)]
    CONSTA = Atom()
    MASKA = [Atom() for _ in range(5)]
    MASKA2 = [Atom() for _ in range(2)]
    BDB = [Atom() for _ in range(4)]
    BDA = BDB
    DERA = Atom()
    CARRYA = [Atom() for _ in range(8)]
    HCA = [Atom() for _ in range(8)]

    def xa(cs, t0, t1):
        return [XA[c][i] for c in cs for i in range(t0 // TT, (t1 + TT - 1) // TT)]

    def ya(A, cs, t0, t1):
        return [A[c][i] for c in cs for i in range(t0 // TT, (t1 + TT - 1) // TT)]

    def bank_atoms(b):
        return [PSA[b]]

    def PSH(hb):
        return PS[hb][:, 0:256]

    st = {"w": 0, "tmp": 0, "tb": 0, "hb": 0}

    def tmp():
        i = st["tmp"] % (NTMP - 2)
        st["tmp"] += 1
        return TMP[:, i * TMPW:(i + 1) * TMPW], TMPA[i]

    def tmp_fixed(i):
        return TMP[:, i * TMPW:(i + 1) * TMPW], TMPA[i]

    def tmpb():
        i = st["tb"] % 4
        st["tb"] += 1
        return TB[:, i * 256:(i + 1) * 256], TBA[i]

    def next_hb():
        i = st["hb"] % 6
        st["hb"] += 1
        return i

    def atmp(a):
        return TMP[:, 2 * a * TMPW:2 * a * TMPW + 512], [TMPA[2 * a], TMPA[2 * a + 1]]

    def pcv(l, k):
        return PCOL[:, l * NPC + k:l * NPC + k + 1]

    def cast_dma(dst2d, src_blk, nblk, key, atoms):
        grp = []
        for b in range(nblk):
            d_ = dst2d[:, 512 * b:512 * b + 512]
            s_ = src_blk(b)
            grp.append(P.dma("pool", lambda e, d_=d_, s_=s_: e.dma_start(out=d_, in_=s_), key, reads=[],
                             writes=([atoms[b % len(atoms)]] if atoms else [])))
        tot = P.dma_count[key]
        for o_ in grp:
            o_.semval = tot

    CVB = 32
    cv = {"n": 0, "batch_last": {}, "key_last": {}, "layer_last": {}}

    def convert_blocks(l, b0, b1):
        for b in range(b0, b1):
            n = cv["n"]
            bi = n // CVB
            key = f"cv{bi % 3}"
            o_ = P.dma("pool", lambda e, b=b: e.dma_start(out=wbf[l, b, :, :], in_=blob[l, b, :, :]), key, reads=[], writes=[])
            if n % CVB == 0 and (bi - 2) in cv["batch_last"]:
                o_.deps.append(cv["batch_last"][bi - 2])
            cv["batch_last"][bi] = o_
            cv["key_last"][key] = o_
            cv["n"] = n + 1
            if b == BLOB_LEN - 1:
                cv["layer_last"][l] = list(cv["key_last"].values())

    def load_w(l, name):
        s = st["w"] % NS
        st["w"] += 1
        off, size, nb = TILES[name]
        src = wbf[l, off:off + nb, :, :].rearrange("b p n -> p b n")
        dst = WS[s][:, 0:nb * 512].rearrange("p (b n) -> p b n", n=512)
        o_ = P.dma("sp", lambda e, src=src, dst=dst: e.dma_start(out=dst, in_=src), f"ws{s}", reads=[], writes=WSB[s])
        o_.deps.extend(cv["layer_last"][l])
        return s

    import os as _os
    SK = _os.environ.get("KSKIP", "")
    if "m" not in SK:
      cast_dma(MASKD, lambda b: maskd_d[b, :, :], 5, "const", MASKA)
    if "n" not in SK:
      cast_dma(MASKS, lambda b: masks_d[b, :, :], 2, "const", MASKA2)
    if "p" not in SK:
      P.dma("sp", lambda e: e.dma_start(out=PCOL[:], in_=pcol[:, :]), "consth", writes=[CONSTA])
    if "z" not in SK:
      P.op("dve", lambda e: e.memset(ONES32[:], 1.0), writes=[CONSTA])
      P.op("dve", lambda e: e.memset(HALF[:], 0.5), writes=[CONSTA])
      P.op("dve", lambda e: e.memset(NHALF[:], -0.5), writes=[CONSTA])

    xs_ctr = [0]

    def load_x(s):
        for c in range(KC):
            for q in range(4):
                a = xs_ctr[0] % 2
                xs_ctr[0] += 1
                stg, stga = atmp(a)
                src = xT[s, c * 128:(c + 1) * 128, q * 512:(q + 1) * 512]
                P.dma("sp", lambda e, stg=stg, src=src: e.dma_start(out=stg, in_=src), f"xs{a}",
                      reads=[], writes=stga)
                hi = XH[:, c, q * 512:(q + 1) * 512]
                lo = XL[:, c, q * 512:(q + 1) * 512]
                xat = xa([c], q * 512, (q + 1) * 512)
                P.op("act", lambda e, hi=hi, stg=stg: e.activation(out=hi, in_=stg, func=AF.Copy),
                     reads=stga, writes=xat)
                P.op("dve", lambda e, lo=lo, hi=hi, stg=stg: e.tensor_tensor(out=lo, in0=stg, in1=hi, op=ALU.subtract),
                     reads=stga + xat, writes=xat)

    def layer_consts(l):
        lam = PCOL[:, l * NPC + 56:l * NPC + 64]
        P.op("act", lambda e: e.activation(out=DER[:, 0:8], in_=lam, func=AF.Exp, scale=-1.0),
             reads=[CONSTA], writes=[DERA])
        P.op("act", lambda e: e.activation(out=DER[:, 8:16], in_=DER[:, 0:8], func=AF.Ln, bias=1.0),
             reads=[DERA], writes=[DERA])
        P.op("act", lambda e: e.activation(out=DER[:, 48:64], in_=PCOL[:, l * NPC + 104:l * NPC + 120], func=AF.Exp),
             reads=[CONSTA], writes=[DERA])
        P.op("dve", lambda e: e.tensor_scalar(out=DER[:, 16:24], in0=DER[:, 8:16], scalar1=-8.0, scalar2=None, op0=ALU.mult),
             reads=[DERA], writes=[DERA])
        P.op("dve", lambda e: e.tensor_scalar(out=DER[:, 24:32], in0=DER[:, 8:16], scalar1=-4.0, scalar2=None, op0=ALU.mult),
             reads=[DERA], writes=[DERA])
        P.op("dve", lambda e: e.tensor_scalar(out=DER[:, 32:48], in0=PCOL[:, l * NPC + 40:l * NPC + 56], scalar1=0.5,
                                              scalar2=None, op0=ALU.mult),
             reads=[DERA, CONSTA], writes=[DERA])
        off, size, nb = TILES["bd"]
        src_bd = wbf[l, off:off + nb, :, :].rearrange("b p n -> p b n")
        dst_bd = BD[:, :].rearrange("p (b n) -> p b n", n=512)
        o_ = P.dma("sp", lambda e: e.dma_start(out=dst_bd, in_=src_bd), "bd", reads=[], writes=BDB)
        o_.deps.extend(cv["layer_last"][l])

    SBANKS = [0, 1, 2, 3]
    OBANKS = [4, 5]
    PBANKS = [6, 7]
    pa = {"s": 0, "o": 0, "p": 0, "pt": 0, "at": 0}

    def attention_units(heads, mask, mask_w, kb_lo_fn, ptile_off):
        units = []
        for h in heads:
            for Qi in range(4):
                lo = kb_lo_fn(Qi)
                hi = 4 * Qi + 3
                for kb in range(lo, hi + 1):
                    units.append((h, Qi, kb, kb == lo, kb == hi))
        n = len(units)
        LA = 3
        info = [None] * n
        for i in range(n + LA):
            if i < n:
                h, Qi, kb, first, last = units[i]
                sbk = SBANKS[pa["s"] % 4]
                pa["s"] += 1
                pt = pa["pt"] % 4
                pa["pt"] += 1
                Pt = scr_bf(ptile_off + pt * 512, 512)
                Pa = scr_at(ptile_off + pt * 512, 512)
                kT = h["kT"](kb)
                qT = h["qT"](Qi)
                P.op("pe", lambda e, sbk=sbk, kT=kT, qT=qT: e.matmul(PS[sbk][:, :], lhsT=kT, rhs=qT, start=True, stop=True),
                     reads=h["k_atoms"](kb) + h["q_atoms"](Qi), writes=bank_atoms(sbk))
                P.op("act", lambda e, sbk=sbk, Pt=Pt: e.activation(out=Pt, in_=PS[sbk][:, :], func=AF.Exp),
                     reads=bank_atoms(sbk), writes=Pa)
                j0 = (4 * Qi - kb + 3) * 128
                assert 0 <= j0 and j0 + 512 <= mask_w
                mk = mask[:, j0:j0 + 512]
                meng = "dve"
                P.op(meng, lambda e, Pt=Pt, mk=mk: e.tensor_tensor(out=Pt, in0=Pt, in1=mk, op=ALU.mult),
                     reads=Pa + MASKA + MASKA2, writes=Pa)
                info[i] = (Pt, Pa)
            k = i - LA
            if k >= 0:
                h, Qi, kb, first, last = units[k]
                Pt, Pa = info[k]
                if first:
                    pa["o"] += 1
                ob = OBANKS[pa["o"] % 2]
                v1 = h["v1"](kb)
                P.op("pe", lambda e, ob=ob, v1=v1, Pt=Pt, first=first, last=last:
                     e.matmul(PS[ob][:, :], lhsT=v1, rhs=Pt, start=first, stop=last),
                     reads=Pa + h["v_atoms"](kb), writes=bank_atoms(ob))
                if last:
                    a = pa["at"] % 2
                    pa["at"] += 1
                    ta, taa = atmp(a)
                    sink = h["sink"]
                    if sink is not None:
                        P.op("act", lambda e, ob=ob, ta=ta, sink=sink: e.activation(
                            out=ta[64:128, :], in_=PS[ob][64:128, :], func=AF.Ln, bias=sink),
                            reads=bank_atoms(ob) + [DERA], writes=taa)
                    else:
                        P.op("act", lambda e, ob=ob, ta=ta: e.activation(
                            out=ta[64:128, :], in_=PS[ob][64:128, :], func=AF.Ln),
                            reads=bank_atoms(ob), writes=taa)
                    P.op("act", lambda e, ta=ta: e.activation(out=ta[64:128, :], in_=ta[64:128, :], func=AF.Exp, scale=-1.0),
                         reads=taa, writes=taa)
                    yo = h["out"](Qi)
                    P.op("dve", lambda e, yo=yo, ob=ob, ta=ta: e.tensor_tensor(
                        out=yo, in0=PS[ob][0:64, :], in1=ta[64:128, :], op=ALU.mult),
                        reads=bank_atoms(ob) + taa, writes=h["out_atoms"](Qi))

    def proj_fm(s_slot, wv, ncols0, dst_fn, dst_atoms_fn, scale, eng):
        for Qi in range(4):
            bk = PBANKS[pa["p"] % 2]
            pa["p"] += 1

            def mm(e, bk=bk, Qi=Qi):
                r = None
                for kc in range(KC):
                    r = e.matmul(PS[bk][:, :], lhsT=wv[:, kc, ncols0:ncols0 + 128], rhs=XH[:, kc, Qi * 512:(Qi + 1) * 512],
                                 start=(kc == 0), stop=(kc == KC - 1))
                return r
            P.op("pe", mm, reads=WSA[s_slot] + xa(range(KC), Qi * 512, (Qi + 1) * 512), writes=bank_atoms(bk))
            dst = dst_fn(Qi)
            if eng == "act":
                P.op("act", lambda e, dst=dst, bk=bk: e.activation(out=dst, in_=PS[bk][:, :], func=AF.Copy, scale=scale),
                     reads=bank_atoms(bk), writes=dst_atoms_fn(Qi))
            else:
                P.op("dve", lambda e, dst=dst, bk=bk: e.tensor_copy(out=dst, in_=PS[bk][:, :]),
                     reads=bank_atoms(bk), writes=dst_atoms_fn(Qi))

    def phase_a_dil(l, g):
        s = load_w(l, f"dil{g}")
        wv = WS[s][:, 0:3072].rearrange("p (k n) -> p k n", n=384)
        QO, KO, VO, PO = 0, 2048, 4096, 8192
        proj_fm(s, wv, 0, lambda Qi: scr_bf(QO + Qi * 512, 512), lambda Qi: scr_at(QO + Qi * 512, 512), 0.125, "act")
        proj_fm(s, wv, 128, lambda Qi: scr_bf(KO + Qi * 512, 512), lambda Qi: scr_at(KO + Qi * 512, 512), 1.0, "dve")
        V1 = scr_bf(VO, 4096).rearrange("p (t h d) -> p t h d", h=2, d=128)
        V1a = scr_at(VO, 4096)
        P.op("dve", lambda e: e.memset(V1[:, :, :, 64:128], 1.0), writes=V1a)
        for tg in range(4):
            bk = PBANKS[pa["p"] % 2]
            pa["p"] += 1

            def mm(e, bk=bk, tg=tg):
                r = None
                for tb in range(4 * tg, 4 * tg + 4):
                    for kc in range(KC):
                        r = e.matmul(PS[bk][:, (tb % 4) * 128:(tb % 4) * 128 + 128], lhsT=XH[:, kc, tb * 128:(tb + 1) * 128],
                                     rhs=wv[:, kc, 256:384], start=(kc == 0), stop=(kc == KC - 1))
                return r
            P.op("pe", mm, reads=WSA[s] + xa(range(KC), tg * 512, (tg + 1) * 512), writes=bank_atoms(bk))
            dst = V1[:, 4 * tg:4 * tg + 4, :, 0:64]
            src = PS[bk][:, :].rearrange("p (t h d) -> p t h d", h=2, d=64)
            P.op("act", lambda e, dst=dst, src=src: e.activation(out=dst, in_=src, func=AF.Copy),
                 reads=bank_atoms(bk), writes=scr_at(VO + tg * 1024, 1024))
        heads = []
        for j in range(2):
            pb = 64 * j
            heads.append(dict(
                qT=lambda Qi, pb=pb: SCR[pb:pb + 64, QO + Qi * 512:QO + (Qi + 1) * 512],
                q_atoms=lambda Qi: scr_at(QO + Qi * 512, 512),
                kT=lambda kb, pb=pb: SCR[pb:pb + 64, KO + kb * 128:KO + (kb + 1) * 128],
                k_atoms=lambda kb: scr_at(KO + kb * 128, 128),
                v1=lambda kb, j=j: V1[:, kb, j, :],
                v_atoms=lambda kb: scr_at(VO + kb * 256, 256),
                sink=None,
                out=lambda Qi, pb=pb: YC[pb:pb + 64, g, Qi * 512:(Qi + 1) * 512],
                out_atoms=lambda Qi: ya(YCA, [g], Qi * 512, (Qi + 1) * 512)))
        attention_units(heads, MASKD, 19 * 128, lambda Qi: 0, PO)

    def phase_a_swa(l, kv):
        s = load_w(l, f"swa{kv}")
        wv = WS[s][:, 0:3584].rearrange("p (k n) -> p k n", n=448)
        QO, KO, VO, PO = 0, 4096, 6144, 8192
        for ch in range(2):
            proj_fm(s, wv, 128 * ch, lambda Qi, ch=ch: scr_bf(QO + ch * 2048 + Qi * 512, 512),
                    lambda Qi, ch=ch: scr_at(QO + ch * 2048 + Qi * 512, 512), 0.125, "act")
        proj_fm(s, wv, 256, lambda Qi: scr_bf(KO + Qi * 512, 512), lambda Qi: scr_at(KO + Qi * 512, 512), 1.0, "dve")
        V1 = scr_bf(VO, 2048).rearrange("p (t d) -> p t d", d=128)
        V1a = scr_at(VO, 2048)
        P.op("dve", lambda e: e.memset(V1[:, :, 64:128], 1.0), writes=V1a)
        for tg in range(2):
            bk = PBANKS[pa["p"] % 2]
            pa["p"] += 1

            def mm(e, bk=bk, tg=tg):
                r = None
                for tb in range(8 * tg, 8 * tg + 8):
                    for kc in range(KC):
                        r = e.matmul(PS[bk][:, (tb % 8) * 64:(tb % 8) * 64 + 64], lhsT=XH[:, kc, tb * 128:(tb + 1) * 128],
                                     rhs=wv[:, kc, 384:448], start=(kc == 0), stop=(kc == KC - 1))
                return r
            P.op("pe", mm, reads=WSA[s] + xa(range(KC), tg * 1024, (tg + 1) * 1024), writes=bank_atoms(bk))
            dst = V1[:, 8 * tg:8 * tg + 8, 0:64]
            src = PS[bk][:, :].rearrange("p (t d) -> p t d", d=64)
            P.op("act", lambda e, dst=dst, src=src: e.activation(out=dst, in_=src, func=AF.Copy),
                 reads=bank_atoms(bk), writes=scr_at(VO + tg * 1024, 1024))
        heads = []
        for j in range(4):
            pb = 64 * (j % 2)
            ch = j // 2
            hidx = 4 * kv + j
            cidx = 2 * kv + ch
            heads.append(dict(
                qT=lambda Qi, pb=pb, ch=ch: SCR[pb:pb + 64, QO + ch * 2048 + Qi * 512:QO + ch * 2048 + (Qi + 1) * 512],
                q_atoms=lambda Qi, ch=ch: scr_at(QO + ch * 2048 + Qi * 512, 512),
                kT=lambda kb, pb=pb: SCR[pb:pb + 64, KO + kb * 128:KO + (kb + 1) * 128],
                k_atoms=lambda kb: scr_at(KO + kb * 128, 128),
                v1=lambda kb: V1[:, kb, :],
                v_atoms=lambda kb: scr_at(VO + kb * 128, 128),
                sink=DER[64:128, 48 + hidx:48 + hidx + 1],
                out=lambda Qi, pb=pb, cidx=cidx: YB[pb:pb + 64, cidx, Qi * 512:(Qi + 1) * 512],
                out_atoms=lambda Qi, cidx=cidx: ya(YBA, [cidx], Qi * 512, (Qi + 1) * 512)))
        attention_units(heads, MASKS, 8 * 128, lambda Qi: max(0, 4 * Qi - 1), PO)

    HHO, YAO, MGO, OSO = 0, 5632, 7680, 9728
    Z1O, Z2O = 0, 5632

    def scr_f32(off_el, ncol):
        return SCR[:, off_el:off_el + 2 * ncol].bitcast(F32)

    SBANK1, SBANK2 = 6, 7

    def mm_group(hb, lhs_fn, rhs_fn, nk):
        def mm(e):
            r = None
            for k in range(nk):
                r = e.matmul(PSH(hb), lhsT=lhs_fn(k), rhs=rhs_fn(k), start=(k == 0), stop=(k == nk - 1))
            return r
        return mm

    def layer_norm_tile(l, i, zoff, gk, bk_, is_last_out, s):
        t0 = i * TT
        za = scr_at(zoff, 4096)
        mean, meana = tmp_fixed(NTMP - 2)
        P.op("act", lambda e: e.activation(out=mean[:, 0:256], in_=PS[SBANK1][:, 0:256], func=AF.Copy, scale=1.0 / D),
             reads=bank_atoms(SBANK1), writes=[meana])
        msq, msqa = tmp()
        P.op("dve", lambda e: e.scalar_tensor_tensor(out=msq[:, 0:256], in0=mean[:, 0:256], scalar=-1.0, in1=mean[:, 0:256],
                                                     op0=ALU.mult, op1=ALU.mult),
             reads=[meana], writes=[msqa])
        var, vara = tmp()
        P.op("dve", lambda e: e.scalar_tensor_tensor(out=var[:, 0:256], in0=PS[SBANK2][:, 0:256], scalar=1.0 / D,
                                                     in1=msq[:, 0:256], op0=ALU.mult, op1=ALU.add),
             reads=bank_atoms(SBANK2) + [msqa], writes=[vara])
        P.op("pool", lambda e: e.tensor_scalar(out=var[:, 0:256], in0=var[:, 0:256], scalar1=LN_EPS, scalar2=None, op0=ALU.add),
             reads=[vara], writes=[vara])
        rstd, rstda = tmp_fixed(NTMP - 1)
        P.op("pool", lambda e: e.tensor_tensor(out=rstd[:, 0:256], in0=var[:, 0:256], in1=NHALF[:, :], op=ALU.pow),
             reads=[vara, CONSTA], writes=[rstda])
        for c in range(KC):
            z = scr_f32(zoff + c * 512, 256)
            zc = scr_at(zoff + c * 512, 512)
            u, ua = tmp()
            P.op("dve", lambda e, u=u, z=z: e.tensor_tensor(out=u[:, 0:256], in0=z, in1=mean[:, 0:256], op=ALU.subtract),
                 reads=zc + [meana], writes=[ua])
            P.op("dve", lambda e, u=u: e.tensor_tensor(out=u[:, 0:256], in0=u[:, 0:256], in1=rstd[:, 0:256], op=ALU.mult),
                 reads=[ua, rstda], writes=[ua])
            gcol = pcv(l, gk + c)
            bcol = pcv(l, bk_ + c)
            if is_last_out:
                v = scr_f32(OSO, 256)
                va = scr_at(OSO, 512)
            else:
                v, va1 = tmp()
                v = v[:, 0:256]
                va = [va1]
            P.op("act", lambda e, v=v, u=u, gcol=gcol, bcol=bcol: e.activation(out=v, in_=u[:, 0:256], func=AF.Identity,
                                                                              scale=gcol, bias=bcol),
                 reads=[ua, CONSTA], writes=va)
            if is_last_out:
                dst = oT[s, c * 128:(c + 1) * 128, t0:t0 + TT]
                P.dma("sp", lambda e, v=v, dst=dst: e.dma_start(out=dst, in_=v), "os0", reads=va, writes=[])
            else:
                hi = XH[:, c, t0:t0 + TT]
                lo = XL[:, c, t0:t0 + TT]
                xat = xa([c], t0, t0 + TT)
                P.op("act", lambda e, hi=hi, v=v: e.activation(out=hi, in_=v, func=AF.Copy), reads=va, writes=xat)
                P.op("pool", lambda e, lo=lo, hi=hi, v=v: e.tensor_tensor(out=lo, in0=v, in1=hi, op=ALU.subtract),
                     reads=va + xat, writes=xat)

    def residual_and_stats(i, c, hz, zoff):
        t0 = i * TT
        xf, xfa = tmp()
        xat = xa([c], t0, t0 + TT)
        P.op("pool", lambda e, xf=xf: e.tensor_tensor(out=xf[:, 0:256], in0=XH[:, c, t0:t0 + TT], in1=XL[:, c, t0:t0 + TT],
                                                      op=ALU.add),
             reads=xat, writes=[xfa])
        z = scr_f32(zoff + c * 512, 256)
        zc = scr_at(zoff + c * 512, 512)
        P.op("dve", lambda e, xf=xf, z=z: e.scalar_tensor_tensor(out=z, in0=xf[:, 0:256], scalar=ALPHA, in1=PSH(hz),
                                                                 op0=ALU.mult, op1=ALU.add),
             reads=[xfa, PSA[hz]], writes=zc)
        sq, sqa = tmp()
        P.op("act", lambda e, sq=sq, z=z: e.activation(out=sq[:, 0:256], in_=z, func=AF.Square), reads=zc, writes=[sqa])
        P.op("pe", lambda e, z=z: e.matmul(PS[SBANK1][:, 0:256], lhsT=ONES32[:, :], rhs=z, start=(c == 0), stop=(c == KC - 1)),
             reads=zc + [CONSTA], writes=bank_atoms(SBANK1))
        P.op("pe", lambda e, sq=sq: e.matmul(PS[SBANK2][:, 0:256], lhsT=ONES32[:, :], rhs=sq[:, 0:256], start=(c == 0),
                                             stop=(c == KC - 1)),
             reads=[sqa, CONSTA], writes=bank_atoms(SBANK2))

    def phase_c_tile(l, i, is_last_out, s):
        t0 = i * TT
        XHt = lambda kc: XH[:, kc, t0:t0 + TT]
        xall = xa(range(KC), t0, t0 + TT)
        YAv = scr_bf(YAO, 2048).rearrange("p (c t) -> p c t", t=256)
        MGv = scr_bf(MGO, 2048).rearrange("p (c t) -> p c t", t=256)
        HHv = scr_bf(HHO, 5632).rearrange("p (j t) -> p j t", t=256)
        BDv = BD[:, :].rearrange("p (g c n) -> p g c n", g=2, c=8)
        lru_proj = {}

        def issue_lru_proj(c):
            if c % 2 == 0:
                lru_proj["sl"] = load_w(l, f"lru{c // 2}")
            sl = lru_proj["sl"]
            w4 = WS[sl][:, 0:4096].rearrange("p (k n) -> p k n", n=512)
            base = (c % 2) * 256
            hx = next_hb()
            P.op("pe", mm_group(hx, lambda k, w4=w4, base=base: w4[:, k, base:base + 128], XHt, KC),
                 reads=WSA[sl] + xall, writes=[PSA[hx]])
            hg = next_hb()
            P.op("pe", mm_group(hg, lambda k, w4=w4, base=base: w4[:, k, base + 128:base + 256], XHt, KC),
                 reads=WSA[sl] + xall, writes=[PSA[hg]])
            lru_proj[c] = (hx, hg)

        issue_lru_proj(0)
        for c in range(KC):
            hx, hg = lru_proj[c]
            lx, lxa = tmp()
            if i == 0:
                P.op("dve", lambda e, lx=lx: e.memset(lx[:, 0:3], 0.0), writes=[lxa])
            else:
                P.op("act", lambda e, lx=lx, c=c: e.activation(out=lx[:, 0:3], in_=CARRY[:, c, :], func=AF.Copy),
                     reads=[CARRYA[c]], writes=[lxa])
            P.op("act", lambda e, lx=lx, hx=hx: e.activation(out=lx[:, 3:259], in_=PSH(hx), func=AF.Copy),
                 reads=[PSA[hx]], writes=[lxa])
            P.op("act", lambda e, lx=lx, c=c: e.activation(out=CARRY[:, c, :], in_=lx[:, 256:259], func=AF.Copy),
                 reads=[lxa], writes=[CARRYA[c]])
            xg, xga = tmp()
            P.op("act", lambda e, xg=xg, hg=hg: e.activation(out=xg[:, 0:256], in_=PSH(hg), func=AF.Copy),
                 reads=[PSA[hg]], writes=[xga])
            xc, xca = tmp()
            P.op("dve", lambda e, xc=xc, lx=lx, c=c: e.tensor_scalar(out=xc[:, 0:256], in0=lx[:, 3:259], scalar1=pcv(l, 24 + c),
                                                                     scalar2=pcv(l, 32 + c), op0=ALU.mult, op1=ALU.add),
                 reads=[lxa, CONSTA], writes=[xca])
            for jj in (2, 1, 0):
                P.op("dve", lambda e, xc=xc, lx=lx, c=c, jj=jj: e.scalar_tensor_tensor(
                    out=xc[:, 0:256], in0=lx[:, jj:jj + 256], scalar=pcv(l, 8 * jj + c), in1=xc[:, 0:256],
                    op0=ALU.mult, op1=ALU.add),
                    reads=[lxa, xca, CONSTA], writes=[xca])
            xcb, xcba = tmpb()
            P.op("act", lambda e, xcb=xcb, xc=xc: e.activation(out=xcb, in_=xc[:, 0:256], func=AF.Copy),
                 reads=[xca], writes=[xcba])
            if c + 1 < KC:
                issue_lru_proj(c + 1)
            hr = next_hb()
            P.op("pe", lambda e, hr=hr, c=c, xcb=xcb: e.matmul(PSH(hr), lhsT=BDv[:, 0, c, :], rhs=xcb, start=True, stop=True),
                 reads=BDA + [xcba], writes=[PSA[hr]])
            hi_ = next_hb()
            P.op("pe", lambda e, hi_=hi_, c=c, xcb=xcb: e.matmul(PSH(hi_), lhsT=BDv[:, 1, c, :], rhs=xcb, start=True, stop=True),
                 reads=BDA + [xcba], writes=[PSA[hi_]])
            tr, tra = tmp()
            P.op("act", lambda e, tr=tr, hr=hr, c=c: e.activation(out=tr[:, 0:256], in_=PSH(hr), func=AF.Tanh, scale=0.5,
                                                                 bias=DER[:, 32 + c:33 + c]),
                 reads=[PSA[hr], DERA], writes=[tra])
            ti, tia = tmp()
            P.op("act", lambda e, ti=ti, hi_=hi_, c=c: e.activation(out=ti[:, 0:256], in_=PSH(hi_), func=AF.Tanh, scale=0.5,
                                                                   bias=DER[:, 40 + c:41 + c]),
                 reads=[PSA[hi_], DERA], writes=[tia])
            av, ava = tmp()
            P.op("act", lambda e, av=av, tr=tr, c=c: e.activation(out=av[:, 0:256], in_=tr[:, 0:256], func=AF.Exp,
                                                                 scale=DER[:, 24 + c:25 + c], bias=DER[:, 24 + c:25 + c]),
                 reads=[tra, DERA], writes=[ava])
            a2, a2a = tmp()
            P.op("act", lambda e, a2=a2, tr=tr, c=c: e.activation(out=a2[:, 0:256], in_=tr[:, 0:256], func=AF.Exp,
                                                                 scale=DER[:, 16 + c:17 + c], bias=DER[:, 16 + c:17 + c]),
                 reads=[tra, DERA], writes=[a2a])
            P.op("pool", lambda e, a2=a2: e.tensor_scalar(out=a2[:, 0:256], in0=a2[:, 0:256], scalar1=-1.0, scalar2=1.0,
                                                          op0=ALU.mult, op1=ALU.add),
                 reads=[a2a], writes=[a2a])
            P.op("pool", lambda e, a2=a2: e.tensor_tensor(out=a2[:, 0:256], in0=a2[:, 0:256], in1=HALF[:, :], op=ALU.pow),
                 reads=[a2a, CONSTA], writes=[a2a])
            P.op("dve", lambda e, ti=ti, xc=xc: e.scalar_tensor_tensor(out=ti[:, 0:256], in0=ti[:, 0:256], scalar=1.0,
                                                                       in1=xc[:, 0:256], op0=ALU.add, op1=ALU.mult),
                 reads=[tia, xca], writes=[tia])
            P.op("dve", lambda e, ti=ti, a2=a2: e.scalar_tensor_tensor(out=ti[:, 0:256], in0=ti[:, 0:256], scalar=0.5,
                                                                       in1=a2[:, 0:256], op0=ALU.mult, op1=ALU.mult),
                 reads=[tia, a2a], writes=[tia])
            hh, hha = tmp()
            if i == 0:
                P.op("dve", lambda e, hh=hh, av=av, ti=ti: e.tensor_tensor_scan(
                    out=hh[:, 0:256], data0=av[:, 0:256], data1=ti[:, 0:256], initial=0.0, op0=ALU.mult, op1=ALU.add),
                    reads=[ava, tia], writes=[hha])
            else:
                P.op("dve", lambda e, hh=hh, av=av, ti=ti, c=c: e.tensor_tensor_scan(
                    out=hh[:, 0:256], data0=av[:, 0:256], data1=ti[:, 0:256], initial=HC[:, c:c + 1], op0=ALU.mult,
                    op1=ALU.add),
                    reads=[ava, tia, HCA[c]], writes=[hha])
            P.op("act", lambda e, hh=hh, c=c: e.activation(out=HC[:, c:c + 1], in_=hh[:, 255:256], func=AF.Copy),
                 reads=[hha], writes=[HCA[c]])
            x2, x2a = tmp()
            P.op("pool", lambda e, x2=x2, xg=xg: e.tensor_tensor(out=x2[:, 0:256], in0=xg[:, 0:256], in1=xg[:, 0:256], op=ALU.mult),
                 reads=[xga], writes=[x2a])
            P.op("pool", lambda e, x2=x2: e.tensor_scalar(out=x2[:, 0:256], in0=x2[:, 0:256], scalar1=0.044715, scalar2=1.0,
                                                          op0=ALU.mult, op1=ALU.add),
                 reads=[x2a], writes=[x2a])
            P.op("pool", lambda e, x2=x2, xg=xg: e.tensor_tensor(out=x2[:, 0:256], in0=x2[:, 0:256], in1=xg[:, 0:256], op=ALU.mult),
                 reads=[x2a, xga], writes=[x2a])
            P.op("act", lambda e, x2=x2: e.activation(out=x2[:, 0:256], in_=x2[:, 0:256], func=AF.Tanh, scale=0.7978845608028654),
                 reads=[x2a], writes=[x2a])
            P.op("dve", lambda e, x2=x2, xg=xg: e.scalar_tensor_tensor(out=x2[:, 0:256], in0=x2[:, 0:256], scalar=1.0,
                                                                       in1=xg[:, 0:256], op0=ALU.add, op1=ALU.mult),
                 reads=[x2a, xga], writes=[x2a])
            P.op("dve", lambda e, x2=x2, hh=hh, c=c: e.scalar_tensor_tensor(out=YAv[:, c, :], in0=x2[:, 0:256], scalar=0.5,
                                                                            in1=hh[:, 0:256], op0=ALU.mult, op1=ALU.mult),
                 reads=[x2a, hha], writes=scr_at(YAO + c * 256, 256))
        Ys = [lambda k: YAv[:, k, :], lambda k: YB[:, k, t0:t0 + TT], lambda k: YC[:, k, t0:t0 + TT]]
        Yat = [scr_at(YAO, 2048), ya(YBA, range(KC), t0, t0 + TT), ya(YCA, range(KC), t0, t0 + TT)]
        for c in range(KC):
            sg = load_w(l, f"gate{c}")
            g3 = WS[sg][:, 0:3072].rearrange("p (k n) -> p k n", n=384)
            sbr = load_w(l, f"br{c}")
            b3 = WS[sbr][:, 0:3072].rearrange("p (k n) -> p k n", n=384)
            acc = None
            for n in range(3):
                hgn = next_hb()
                P.op("pe", mm_group(hgn, lambda k, g3=g3, n=n: g3[:, k, 128 * n:128 * n + 128], XHt, KC),
                     reads=WSA[sg] + xall, writes=[PSA[hgn]])
                hbn = next_hb()
                P.op("pe", mm_group(hbn, lambda k, b3=b3, n=n: b3[:, k, 128 * n:128 * n + 128], Ys[n], KC),
                     reads=WSA[sbr] + Yat[n], writes=[PSA[hbn]])
                sgm, sga = tmp()
                P.op("act", lambda e, sgm=sgm, hgn=hgn: e.activation(out=sgm[:, 0:256], in_=PSH(hgn), func=AF.Sigmoid),
                     reads=[PSA[hgn]], writes=[sga])
                P.op("dve", lambda e, sgm=sgm, hbn=hbn: e.tensor_tensor(out=sgm[:, 0:256], in0=sgm[:, 0:256], in1=PSH(hbn),
                                                                        op=ALU.mult),
                     reads=[sga, PSA[hbn]], writes=[sga])
                if n == 0:
                    acc, acca = sgm, sga
                elif n == 1:
                    P.op("pool", lambda e, acc=acc, sgm=sgm: e.tensor_tensor(out=acc[:, 0:256], in0=acc[:, 0:256],
                                                                             in1=sgm[:, 0:256], op=ALU.add),
                         reads=[acca, sga], writes=[acca])
                else:
                    P.op("pool", lambda e, acc=acc, sgm=sgm, c=c: e.tensor_tensor(out=MGv[:, c, :], in0=acc[:, 0:256],
                                                                                  in1=sgm[:, 0:256], op=ALU.add),
                         reads=[acca, sga], writes=scr_at(MGO + c * 256, 256))
        if KDUMP and s == 0 and l == 0:
            P.dma("sp", lambda e: e.dma_start(out=dbg_ya[:, :, t0:t0 + TT], in_=YAv), "dbg%d" % next(_dbgc), reads=scr_at(YAO, 2048), writes=[])
            P.dma("sp", lambda e: e.dma_start(out=dbg_mg[:, :, t0:t0 + TT], in_=MGv), "dbg%d" % next(_dbgc), reads=scr_at(MGO, 2048), writes=[])
        mga = scr_at(MGO, 2048)
        for o in range(2):
            so = load_w(l, f"out{o}")
            w4 = WS[so][:, 0:4096].rearrange("p (k n) -> p k n", n=512)
            for cc in range(4):
                c = 4 * o + cc
                hz = next_hb()
                P.op("pe", mm_group(hz, lambda k, w4=w4, cc=cc: w4[:, k, 128 * cc:128 * cc + 128], lambda k: MGv[:, k, :], KC),
                     reads=WSA[so] + mga, writes=[PSA[hz]])
                residual_and_stats(i, c, hz, Z1O)
        layer_norm_tile(l, i, Z1O, 64, 72, False, s)
        for jp in range(11):
            sf = load_w(l, f"ffi{jp}")
            w4 = WS[sf][:, 0:4096].rearrange("p (k n) -> p k n", n=512)
            for jj in range(2):
                j = 2 * jp + jj
                h1 = next_hb()
                P.op("pe", mm_group(h1, lambda k, w4=w4, jj=jj: w4[:, k, 256 * jj:256 * jj + 128], XHt, KC),
                     reads=WSA[sf] + xall, writes=[PSA[h1]])
                h3 = next_hb()
                P.op("pe", mm_group(h3, lambda k, w4=w4, jj=jj: w4[:, k, 256 * jj + 128:256 * jj + 256], XHt, KC),
                     reads=WSA[sf] + xall, writes=[PSA[h3]])
                sl_, sla = tmp()
                P.op("act", lambda e, sl_=sl_, h1=h1: e.activation(out=sl_[:, 0:256], in_=PSH(h1), func=AF.Silu),
                     reads=[PSA[h1]], writes=[sla])
                P.op("dve", lambda e, sl_=sl_, h3=h3, j=j: e.tensor_tensor(out=HHv[:, j, :], in0=sl_[:, 0:256], in1=PSH(h3),
                                                                          op=ALU.mult),
                     reads=[sla, PSA[h3]], writes=scr_at(HHO + j * 256, 256))
        hha_all = scr_at(HHO, 5632)
        for c in range(KC):
            so = load_w(l, f"ffo{c}")
            w22 = WS[so][:, 0:2816].rearrange("p (j n) -> p j n", n=128)
            hz = next_hb()
            P.op("pe", mm_group(hz, lambda k, w22=w22: w22[:, k, :], lambda k: HHv[:, k, :], NJ),
                 reads=WSA[so] + hha_all, writes=[PSA[hz]])
            residual_and_stats(i, c, hz, Z2O)
        layer_norm_tile(l, i, Z2O, 80, 88, is_last_out, s)

    import os
    PH = os.environ.get("KDEBUG", "xksdc")
    NSQ = int(os.environ.get("KNSEQ", NSEQ))
    convert_blocks(0, 0, BLOB_LEN)
    per_tile = (BLOB_LEN + NTL - 1) // NTL
    for s in range(NSQ):
        if "x" in PH:
            load_x(s)
        for l in range(n_layers):
            if "k" in PH:
                layer_consts(l)
            if "s" in PH:
                for kv in range(int(os.environ.get("KNSWA", 4))):
                    phase_a_swa(l, kv)
            if "d" in PH:
                for g in range(int(os.environ.get("KNDIL", 8))):
                    phase_a_dil(l, g)
            if KDUMP and s == 0 and l == 0:
                P.dma("sp", lambda e: e.dma_start(out=dbg_yb[:, :, :], in_=YB[:, :, :]), "dbg",
                      reads=ya(YBA, range(KC), 0, T), writes=[])
                P.dma("sp", lambda e: e.dma_start(out=dbg_yc[:, :, :], in_=YC[:, :, :]), "dbg",
                      reads=ya(YCA, range(KC), 0, T), writes=[])
                P.dma("sp", lambda e: e.dma_start(out=dbg_xh[:, :, :], in_=XH[:, :, :]), "dbg",
                      reads=xa(range(KC), 0, T), writes=[])
            if "c" in PH:
                for i in range(int(os.environ.get("KNTL", NTL))):
                    if s == 0 and l + 1 < n_layers:
                        convert_blocks(l + 1, i * per_tile, min(BLOB_LEN, (i + 1) * per_tile))
                    phase_c_tile(l, i, last_flags[l], s)
                if KDUMP and s == 0 and l == 0:
                    P.dma("sp", lambda e: e.dma_start(out=dbg_x1[:, :, :], in_=XH[:, :, :]), "dbg%d" % next(_dbgc),
                          reads=xa(range(KC), 0, T), writes=[])

    total_const = P.dma_count.get("const", 0)
    for e_ in P.ENGS:
        for op in P.ops[e_]:
            if op.isdma and op.semkey == "const":
                op.semval = total_const
    for e_ in P.ENGS:
        cnt = 0
        for op in P.ops[e_]:
            if op.sig and not op.isdma:
                cnt += 1
                op.sigval = cnt
    sem_names = ["e_" + e_ for e_ in P.ENGS] + ["d_" + k for k in sorted(P.dma_count)]
    sems = {n: es.enter_context(nc.semaphore(n)) for n in sem_names}
    handles = {"pe": nc.tensor, "act": nc.scalar, "dve": nc.vector, "pool": nc.gpsimd, "sp": nc.sync}

    def emit(eng, e):
        waited = {}
        for op in P.ops[eng]:
            need = {}
            for d in op.deps:
                if d.isdma:
                    key, val = "d_" + d.semkey, d.semval
                else:
                    key, val = "e_" + d.eng, d.sigval
                if need.get(key, 0) < val:
                    need[key] = val
            for key, val in need.items():
                if waited.get(key, 0) < val:
                    e.wait_ge(sems[key], val)
                    waited[key] = val
            inst = op.fn(e)
            if op.isdma:
                inst.then_inc(sems["d_" + op.semkey], 16)
            elif op.sig:
                inst.then_inc(sems["e_" + eng], 1)
        if eng == "sp":
            for k_, v_ in sorted(P.dma_count.items()):
                e.wait_ge(sems["d_" + k_], v_)

    block = es.enter_context(nc.Block())

    @block.tensor
    def _(e):
        emit("pe", e)

    @block.scalar
    def _(e):
        emit("act", e)

    @block.vector
    def _(e):
        emit("dve", e)

    @block.gpsimd
    def _(e):
        emit("pool", e)

    @block.sync
    def _(e):
        emit("sp", e)

    es.close()
    stats = {k: len(v) for k, v in P.ops.items()}
    return nc, stats


_CACHE = {}


def _get_prog(n_layers, last_flags):
    key = (n_layers, tuple(last_flags))
    if key not in _CACHE:
        _CACHE[key] = build_program(n_layers, list(last_flags))
    return _CACHE[key][0]


def kernel(x, w_in, conv_w, conv_b, w_rg, b_rg, w_ig, b_ig, lru_lambda, sinks, w_branch, w_out,
           ln1_g, ln1_b, w_ffn_in, w_ffn_out, ln2_g, ln2_b):
    f = lambda a: np.asarray(a, dtype=np.float32)
    x = f(x); w_in = f(w_in); conv_w = f(conv_w); conv_b = f(conv_b); w_rg = f(w_rg); b_rg = f(b_rg)
    w_ig = f(w_ig); b_ig = f(b_ig); lru_lambda = f(lru_lambda); sinks = f(sinks); w_branch = f(w_branch)
    w_out = f(w_out); ln1_g = f(ln1_g); ln1_b = f(ln1_b); w_ffn_in = f(w_ffn_in); w_ffn_out = f(w_ffn_out)
    ln2_g = f(ln2_g); ln2_b = f(ln2_b)
    maskd, masks = _build_masks()
    blobs = [_build_blob(l, w_in, w_rg, w_ig, w_branch, w_out, w_ffn_in, w_ffn_out) for l in range(DEPTH)]
    pcols = [_build_pcol(l, conv_w, conv_b, b_rg, b_ig, lru_lambda, sinks, ln1_g, ln1_b, ln2_g, ln2_b)
             for l in range(DEPTH)]
    xT = np.ascontiguousarray(x.transpose(0, 2, 1))
    cur = [np.ascontiguousarray(xT[NSEQ * c:NSEQ * (c + 1)]) for c in range(NCORES)]
    if FUSED:
        nc = _get_prog(DEPTH, [False] * (DEPTH - 1) + [True])
        blob = np.ascontiguousarray(np.stack(blobs, 0))
        pc = np.ascontiguousarray(np.concatenate(pcols, axis=1))
        in_maps = [{"xT": cur[c], "blob": blob, "pcol": pc, "maskd": maskd, "masks": masks} for c in range(NCORES)]
        res = run_bass_kernel_spmd(nc, in_maps, core_ids=list(range(NCORES)))
        cur = [np.asarray(res.results[c]["oT"]) for c in range(NCORES)]
    else:
        nc = _get_prog(1, [True])
        for l in range(DEPTH):
            blob = np.ascontiguousarray(blobs[l][None])
            in_maps = [{"xT": cur[c], "blob": blob, "pcol": pcols[l], "maskd": maskd, "masks": masks}
                       for c in range(NCORES)]
            res = run_bass_kernel_spmd(nc, in_maps, core_ids=list(range(NCORES)))
            cur = [np.asarray(res.results[c]["oT"]) for c in range(NCORES)]
    outT = np.concatenate(cur, axis=0)
    return np.ascontiguousarray(outT.transpose(0, 2, 1)).astype(np.float32)
```

```python
import numpy as np
import concourse.bass as bass
import concourse.mybir as mybir
from concourse.bass_utils import run_bass_kernel_spmd

F32 = mybir.dt.float32
BF16 = mybir.dt.bfloat16
AF = mybir.ActivationFunctionType
ALU = mybir.AluOpType

D = 1024
T = 2048
DEPTH = 4
KC = 8
TT = 256
NTL = T // TT
FFH = 2816
NJ = FFH // 128
NSEQ = 2
NCORES = 8
ALPHA = float((2.0 * DEPTH) ** 0.25)
LN_EPS = 1e-5
NS = 3
NTMP = 12
TMPW = 264
NPC = 120

FUSED = True


def _chunked(wsub):
    n = wsub.shape[1]
    return np.ascontiguousarray(wsub.reshape(KC, 128, n).transpose(1, 0, 2).reshape(128, KC * n))


def _tile_table():
    names = []
    for kv in range(4):
        names.append((f"swa{kv}", 8 * 448))
    for g in range(8):
        names.append((f"dil{g}", 8 * 384))
    names.append(("bd", 2048))
    for cp in range(4):
        names.append((f"lru{cp}", 4096))
    for c in range(8):
        names.append((f"gate{c}", 3072))
        names.append((f"br{c}", 3072))
    for o in range(2):
        names.append((f"out{o}", 4096))
    for jp in range(11):
        names.append((f"ffi{jp}", 4096))
    for c in range(8):
        names.append((f"ffo{c}", 2816))
    table = {}
    off = 0
    for n, s in names:
        nb = (s + 511) // 512
        table[n] = (off, s, nb)
        off += nb
    return table, off


TILES, BLOB_LEN = _tile_table()


def _build_blob(l, w_in, w_rg, w_ig, w_branch, w_out, w_ffn_in, w_ffn_out):
    blob = np.zeros((BLOB_LEN, 128, 512), np.float32)
    wi = w_in[l]

    def put(name, arr):
        off, size, nb = TILES[name]
        assert arr.shape == (128, size), (name, arr.shape, size)
        for b in range(nb):
            w = min(512, size - 512 * b)
            blob[off + b, :, 0:w] = arr[:, 512 * b:512 * b + w]

    for kv in range(4):
        cols = np.concatenate([
            np.arange(2048 + 256 * kv, 2048 + 256 * kv + 256),
            np.arange(3072 + 64 * kv, 3072 + 64 * kv + 64),
            np.arange(3072 + 64 * kv, 3072 + 64 * kv + 64),
            np.arange(3328 + 64 * kv, 3328 + 64 * kv + 64)])
        put(f"swa{kv}", _chunked(wi[:, cols]))
    for g in range(8):
        cols = np.concatenate([np.arange(3584 + 128 * g, 3584 + 128 * g + 128),
                               np.arange(4608 + 128 * g, 4608 + 128 * g + 128),
                               np.arange(5632 + 128 * g, 5632 + 128 * g + 128)])
        put(f"dil{g}", _chunked(wi[:, cols]))
    bd = np.zeros((128, 2, 8, 128), np.float32)
    for gi, wg in enumerate((w_rg[l], w_ig[l])):
        for c in range(8):
            for b in range(2):
                bd[64 * b:64 * b + 64, gi, c, 64 * b:64 * b + 64] = wg[2 * c + b]
    put("bd", bd.reshape(128, 2048))
    for cp in range(4):
        cols = []
        for c in (2 * cp, 2 * cp + 1):
            cols.append(np.arange(128 * c, 128 * c + 128))
            cols.append(np.arange(1024 + 128 * c, 1024 + 128 * c + 128))
        put(f"lru{cp}", _chunked(wi[:, np.concatenate(cols)]))
    for c in range(8):
        cols = np.concatenate([np.arange(6656 + 1024 * n + 128 * c, 6656 + 1024 * n + 128 * c + 128) for n in range(3)])
        put(f"gate{c}", _chunked(wi[:, cols]))
        br = np.concatenate([w_branch[l, n][:, 128 * c:128 * c + 128] for n in range(3)], axis=1)
        put(f"br{c}", _chunked(br))
    for o in range(2):
        put(f"out{o}", _chunked(w_out[l][:, 512 * o:512 * o + 512]))
    wf = w_ffn_in[l]
    for jp in range(11):
        cols = []
        for j in (2 * jp, 2 * jp + 1):
            cols.append(np.arange(128 * j, 128 * j + 128))
            cols.append(np.arange(FFH + 128 * j, FFH + 128 * j + 128))
        put(f"ffi{jp}", _chunked(wf[:, np.concatenate(cols)]))
    wo = w_ffn_out[l]
    for c in range(8):
        sub = wo[:, 128 * c:128 * c + 128].reshape(NJ, 128, 128).transpose(1, 0, 2).reshape(128, NJ * 128)
        put(f"ffo{c}", np.ascontiguousarray(sub))
    return blob


def _build_pcol(l, conv_w, conv_b, b_rg, b_ig, lru_lambda, sinks, ln1_g, ln1_b, ln2_g, ln2_b):
    pc = np.zeros((128, NPC), np.float32)

    def col(v):
        return v.reshape(8, 128).T

    for j in range(4):
        pc[:, 8 * j:8 * j + 8] = col(conv_w[l, j])
    pc[:, 32:40] = col(conv_b[l])
    pc[:, 40:48] = col(b_rg[l])
    pc[:, 48:56] = col(b_ig[l])
    pc[:, 56:64] = col(lru_lambda[l])
    pc[:, 64:72] = col(ln1_g[l])
    pc[:, 72:80] = col(ln1_b[l])
    pc[:, 80:88] = col(ln2_g[l])
    pc[:, 88:96] = col(ln2_b[l])
    pc[:, 104:120] = np.broadcast_to(sinks[l][None, :], (128, 16))
    return pc


def _build_masks():
    ki = np.arange(128)[:, None]
    j = np.arange(19 * 128)[None, :]
    dist = 128 * (j // 128 - 3) + (j % 128) - ki
    m = ((dist >= 0) & (dist <= 128)).astype(np.float32)
    m += ((dist >= 0) & (dist % 4 == 0) & (dist <= 512)).astype(np.float32)
    m += ((dist >= 0) & (dist % 16 == 0) & (dist <= 2048)).astype(np.float32)
    j2 = np.arange(8 * 128)[None, :]
    dist2 = 128 * (j2 // 128 - 3) + (j2 % 128) - ki
    m2 = ((dist2 >= 0) & (dist2 <= 128)).astype(np.float32)
    mp = np.zeros((128, 2560), np.float32)
    mp[:, :19 * 128] = m
    mb = np.ascontiguousarray(mp.reshape(128, 5, 512).transpose(1, 0, 2))
    m2b = np.ascontiguousarray(m2.reshape(128, 2, 512).transpose(1, 0, 2))
    return mb, m2b


class Atom:
    __slots__ = ("w", "r")

    def __init__(self):
        self.w = None
        self.r = []


class Op:
    __slots__ = ("eng", "fn", "deps", "sig", "sigval", "isdma", "semkey", "semval")


class Prog:
    ENGS = ("pe", "act", "dve", "pool", "sp")

    def __init__(self):
        self.ops = {e: [] for e in self.ENGS}
        self.dma_count = {}

    def _add(self, eng, fn, reads, writes, isdma=False, semkey=None):
        op = Op()
        op.eng = eng
        op.fn = fn
        op.sig = False
        op.sigval = 0
        op.isdma = isdma
        op.semkey = semkey
        op.semval = 0
        deps = {}
        for a in reads:
            w = a.w
            if w is None:
                continue
            if w.isdma or isdma or w.eng != eng or eng != "pe":
                deps[id(w)] = w
        for a in writes:
            w = a.w
            if w is not None and (w.isdma or isdma or w.eng != eng):
                deps[id(w)] = w
            for r in a.r:
                if r.isdma or isdma or r.eng != eng:
                    deps[id(r)] = r
        op.deps = list(deps.values())
        for d in op.deps:
            if not d.isdma:
                d.sig = True
        for a in reads:
            a.r.append(op)
        for a in writes:
            a.w = op
            a.r = []
        if isdma:
            n = self.dma_count.get(semkey, 0) + 16
            self.dma_count[semkey] = n
            op.semval = n
        self.ops[eng].append(op)
        return op

    def op(self, eng, fn, reads=(), writes=()):
        return self._add(eng, fn, reads, writes)

    def dma(self, queue, fn, semkey, reads=(), writes=()):
        return self._add(queue, fn, reads, writes, isdma=True, semkey=semkey)


def _flat(*items):
    out = []
    for it in items:
        if isinstance(it, Atom):
            out.append(it)
        else:
            out.extend(_flat(*it))
    return out


def build_program(n_layers, last_flags):
    nc = bass.Bass("TRN2", target_bir_lowering=False)
    P = Prog()

    xT = nc.dram_tensor("xT", [NSEQ, D, T], F32, kind="ExternalInput").ap()
    blob = nc.dram_tensor("blob", [n_layers, BLOB_LEN, 128, 512], F32, kind="ExternalInput").ap()
    pcol = nc.dram_tensor("pcol", [128, n_layers * NPC], F32, kind="ExternalInput").ap()
    maskd_d = nc.dram_tensor("maskd", [5, 128, 512], F32, kind="ExternalInput").ap()
    masks_d = nc.dram_tensor("masks", [2, 128, 512], F32, kind="ExternalInput").ap()
    oT = nc.dram_tensor("oT", [NSEQ, D, T], F32, kind="ExternalOutput").ap()
    wbf = nc.dram_tensor("wbf", [n_layers, BLOB_LEN, 128, 512], BF16, kind="Internal").ap()
    import os as _os2
    KDUMP = _os2.environ.get("KDUMP", "") == "1"
    import itertools as _it
    _dbgc = _it.count()
    if KDUMP:
        dbg_yb = nc.dram_tensor("dbg_yb", [128, KC, T], BF16, kind="ExternalOutput").ap()
        dbg_yc = nc.dram_tensor("dbg_yc", [128, KC, T], BF16, kind="ExternalOutput").ap()
        dbg_ya = nc.dram_tensor("dbg_ya", [128, KC, T], BF16, kind="ExternalOutput").ap()
        dbg_mg = nc.dram_tensor("dbg_mg", [128, KC, T], BF16, kind="ExternalOutput").ap()
        dbg_xh = nc.dram_tensor("dbg_xh", [128, KC, T], BF16, kind="ExternalOutput").ap()
        dbg_x1 = nc.dram_tensor("dbg_x1", [128, KC, T], BF16, kind="ExternalOutput").ap()

    from contextlib import ExitStack
    es = ExitStack()

    def sb(name, shape, dt):
        return es.enter_context(nc.sbuf_tensor(name, shape, dt))

    XH = sb("XH", [128, KC, T], BF16)
    XL = sb("XL", [128, KC, T], BF16)
    YB = sb("YB", [128, KC, T], BF16)
    YC = sb("YC", [128, KC, T], BF16)
    WS = [sb(f"WS{i}", [128, 4096], BF16) for i in range(NS)]
    MASKD = sb("MASKD", [128, 2560], BF16)
    MASKS = sb("MASKS", [128, 8 * 128], BF16)
    BD = sb("BD", [128, 2048], BF16)
    SCR = sb("SCR", [128, 10240], BF16)
    TMP = sb("TMP", [128, NTMP * TMPW], F32)
    TB = sb("TB", [128, 4 * 256], BF16)
    HALF = sb("HALF", [128, 256], F32)
    NHALF = sb("NHALF", [128, 256], F32)
    PCOL = sb("PCOL", [128, n_layers * NPC], F32)
    DER = sb("DER", [128, 64], F32)
    ONES32 = sb("ONES32", [128, 128], F32)
    CARRY = sb("CARRY", [128, 8, 3], F32)
    HC = sb("HC", [128, 8], F32)
    PS = [es.enter_context(nc.psum_tensor(f"PS{i}", [128, 512], F32)) for i in range(8)]


    def scr_bf(off_el, n_el):
        return SCR[:, off_el:off_el + n_el]

    scr_atoms = [Atom() for _ in range(20)]

    def scr_at(off_el, n_el):
        return scr_atoms[off_el // 512:(off_el + n_el + 511) // 512]

    XA = [[Atom() for _ in range(NTL)] for _ in range(KC)]
    YBA = [[Atom() for _ in range(NTL)] for _ in range(KC)]
    YCA = [[Atom() for _ in range(NTL)] for _ in range(KC)]
    WSB = [[Atom() for _ in range(8)] for _ in range(NS)]
    WSA = WSB
    TMPA = [Atom() for _ in range(NTMP)]
    TBA = [Atom() for _ in range(4)]
    PSA = [Atom() for _ in range(8)]
    CONSTA = Atom()
    MASKA = [Atom() for _ in range(5)]
    MASKA2 = [Atom() for _ in range(2)]
    BDB = [Atom() for _ in range(4)]
    BDA = BDB
    DERA = Atom()
    CARRYA = [Atom() for _ in range(8)]
    HCA = [Atom() for _ in range(8)]

    def xa(cs, t0, t1):
        return [XA[c][i] for c in cs for i in range(t0 // TT, (t1 + TT - 1) // TT)]

    def ya(A, cs, t0, t1):
        return [A[c][i] for c in cs for i in range(t0 // TT, (t1 + TT - 1) // TT)]

    def bank_atoms(b):
        return [PSA[b]]

    def PSH(hb):
        return PS[hb][:, 0:256]

    st = {"w": 0, "tmp": 0, "tb": 0, "hb": 0}

    def tmp():
        i = st["tmp"] % (NTMP - 2)
        st["tmp"] += 1
        return TMP[:, i * TMPW:(i + 1) * TMPW], TMPA[i]

    def tmp_fixed(i):
        return TMP[:, i * TMPW:(i + 1) * TMPW], TMPA[i]

    def tmpb():
        i = st["tb"] % 4
        st["tb"] += 1
        return TB[:, i * 256:(i + 1) * 256], TBA[i]

    def next_hb():
        i = st["hb"] % 6
        st["hb"] += 1
        return i

    def atmp(a):
        return TMP[:, 2 * a * TMPW:2 * a * TMPW + 512], [TMPA[2 * a], TMPA[2 * a + 1]]

    def pcv(l, k):
        return PCOL[:, l * NPC + k:l * NPC + k + 1]

    def cast_dma(dst2d, src_blk, nblk, key, atoms):
        grp = []
        for b in range(nblk):
            d_ = dst2d[:, 512 * b:512 * b + 512]
            s_ = src_blk(b)
            grp.append(P.dma("pool", lambda e, d_=d_, s_=s_: e.dma_start(out=d_, in_=s_), key, reads=[],
                             writes=([atoms[b % len(atoms)]] if atoms else [])))
        tot = P.dma_count[key]
        for o_ in grp:
            o_.semval = tot

    CVB = 32
    cv = {"n": 0, "batch_last": {}, "key_last": {}, "layer_last": {}}

    def convert_blocks(l, b0, b1):
        for b in range(b0, b1):
            n = cv["n"]
            bi = n // CVB
            key = f"cv{bi % 3}"
            o_ = P.dma("pool", lambda e, b=b: e.dma_start(out=wbf[l, b, :, :], in_=blob[l, b, :, :]), key, reads=[], writes=[])
            if n % CVB == 0 and (bi - 2) in cv["batch_last"]:
                o_.deps.append(cv["batch_last"][bi - 2])
            cv["batch_last"][bi] = o_
            cv["key_last"][key] = o_
            cv["n"] = n + 1
            if b == BLOB_LEN - 1:
                cv["layer_last"][l] = list(cv["key_last"].values())

    def load_w(l, name):
        s = st["w"] % NS
        st["w"] += 1
        off, size, nb = TILES[name]
        src = wbf[l, off:off + nb, :, :].rearrange("b p n -> p b n")
        dst = WS[s][:, 0:nb * 512].rearrange("p (b n) -> p b n", n=512)
        o_ = P.dma("sp", lambda e, src=src, dst=dst: e.dma_start(out=dst, in_=src), f"ws{s}", reads=[], writes=WSB[s])
        o_.deps.extend(cv["layer_last"][l])
        return s

    import os as _os
    SK = _os.environ.get("KSKIP", "")
    if "m" not in SK:
      cast_dma(MASKD, lambda b: maskd_d[b, :, :], 5, "const", MASKA)
    if "n" not in SK:
      cast_dma(MASKS, lambda b: masks_d[b, :, :], 2, "const", MASKA2)
    if "p" not in SK:
      P.dma("sp", lambda e: e.dma_start(out=PCOL[:], in_=pcol[:, :]), "consth", writes=[CONSTA])
    if "z" not in SK:
      P.op("dve", lambda e: e.memset(ONES32[:], 1.0), writes=[CONSTA])
      P.op("dve", lambda e: e.memset(HALF[:], 0.5), writes=[CONSTA])
      P.op("dve", lambda e: e.memset(NHALF[:], -0.5), writes=[CONSTA])

    xs_ctr = [0]

    def load_x(s):
        for c in range(KC):
            for q in range(4):
                a = xs_ctr[0] % 2
                xs_ctr[0] += 1
                stg, stga = atmp(a)
                src = xT[s, c * 128:(c + 1) * 128, q * 512:(q + 1) * 512]
                P.dma("sp", lambda e, stg=stg, src=src: e.dma_start(out=stg, in_=src), f"xs{a}",
                      reads=[], writes=stga)
                hi = XH[:, c, q * 512:(q + 1) * 512]
                lo = XL[:, c, q * 512:(q + 1) * 512]
                xat = xa([c], q * 512, (q + 1) * 512)
                P.op("act", lambda e, hi=hi, stg=stg: e.activation(out=hi, in_=stg, func=AF.Copy),
                     reads=stga, writes=xat)
                P.op("dve", lambda e, lo=lo, hi=hi, stg=stg: e.tensor_tensor(out=lo, in0=stg, in1=hi, op=ALU.subtract),
                     reads=stga + xat, writes=xat)

    def layer_consts(l):
        lam = PCOL[:, l * NPC + 56:l * NPC + 64]
        P.op("act", lambda e: e.activation(out=DER[:, 0:8], in_=lam, func=AF.Exp, scale=-1.0),
             reads=[CONSTA], writes=[DERA])
        P.op("act", lambda e: e.activation(out=DER[:, 8:16], in_=DER[:, 0:8], func=AF.Ln, bias=1.0),
             reads=[DERA], writes=[DERA])
        P.op("act", lambda e: e.activation(out=DER[:, 48:64], in_=PCOL[:, l * NPC + 104:l * NPC + 120], func=AF.Exp),
             reads=[CONSTA], writes=[DERA])
        P.op("dve", lambda e: e.tensor_scalar(out=DER[:, 16:24], in0=DER[:, 8:16], scalar1=-8.0, scalar2=None, op0=ALU.mult),
             reads=[DERA], writes=[DERA])
        P.op("dve", lambda e: e.tensor_scalar(out=DER[:, 24:32], in0=DER[:, 8:16], scalar1=-4.0, scalar2=None, op0=ALU.mult),
             reads=[DERA], writes=[DERA])
        P.op("dve", lambda e: e.tensor_scalar(out=DER[:, 32:48], in0=PCOL[:, l * NPC + 40:l * NPC + 56], scalar1=0.5,
                                              scalar2=None, op0=ALU.mult),
             reads=[DERA, CONSTA], writes=[DERA])
        off, size, nb = TILES["bd"]
        src_bd = wbf[l, off:off + nb, :, :].rearrange("b p n -> p b n")
        dst_bd = BD[:, :].rearrange("p (b n) -> p b n", n=512)
        o_ = P.dma("sp", lambda e: e.dma_start(out=dst_bd, in_=src_bd), "bd", reads=[], writes=BDB)
        o_.deps.extend(cv["layer_last"][l])

    SBANKS = [0, 1, 2, 3]
    OBANKS = [4, 5]
    PBANKS = [6, 7]
    pa = {"s": 0, "o": 0, "p": 0, "pt": 0, "at": 0}

    def attention_units(heads, mask, mask_w, kb_lo_fn, ptile_off):
        units = []
        for h in heads:
            for Qi in range(4):
                lo = kb_lo_fn(Qi)
                hi = 4 * Qi + 3
                for kb in range(lo, hi + 1):
                    units.append((h, Qi, kb, kb == lo, kb == hi))
        n = len(units)
        LA = 3
        info = [None] * n
        for i in range(n + LA):
            if i < n:
                h, Qi, kb, first, last = units[i]
                sbk = SBANKS[pa["s"] % 4]
                pa["s"] += 1
                pt = pa["pt"] % 4
                pa["pt"] += 1
                Pt = scr_bf(ptile_off + pt * 512, 512)
                Pa = scr_at(ptile_off + pt * 512, 512)
                kT = h["kT"](kb)
                qT = h["qT"](Qi)
                P.op("pe", lambda e, sbk=sbk, kT=kT, qT=qT: e.matmul(PS[sbk][:, :], lhsT=kT, rhs=qT, start=True, stop=True),
                     reads=h["k_atoms"](kb) + h["q_atoms"](Qi), writes=bank_atoms(sbk))
                P.op("act", lambda e, sbk=sbk, Pt=Pt: e.activation(out=Pt, in_=PS[sbk][:, :], func=AF.Exp),
                     reads=bank_atoms(sbk), writes=Pa)
                j0 = (4 * Qi - kb + 3) * 128
                assert 0 <= j0 and j0 + 512 <= mask_w
                mk = mask[:, j0:j0 + 512]
                meng = "dve"
                P.op(meng, lambda e, Pt=Pt, mk=mk: e.tensor_tensor(out=Pt, in0=Pt, in1=mk, op=ALU.mult),
                     reads=Pa + MASKA + MASKA2, writes=Pa)
                info[i] = (Pt, Pa)
            k = i - LA
            if k >= 0:
                h, Qi, kb, first, last = units[k]
                Pt, Pa = info[k]
                if first:
                    pa["o"] += 1
                ob = OBANKS[pa["o"] % 2]
                v1 = h["v1"](kb)
                P.op("pe", lambda e, ob=ob, v1=v1, Pt=Pt, first=first, last=last:
                     e.matmul(PS[ob][:, :], lhsT=v1, rhs=Pt, start=first, stop=last),
                     reads=Pa + h["v_atoms"](kb), writes=bank_atoms(ob))
                if last:
                    a = pa["at"] % 2
                    pa["at"] += 1
                    ta, taa = atmp(a)
                    sink = h["sink"]
                    if sink is not None:
                        P.op("act", lambda e, ob=ob, ta=ta, sink=sink: e.activation(
                            out=ta[64:128, :], in_=PS[ob][64:128, :], func=AF.Ln, bias=sink),
                            reads=bank_atoms(ob) + [DERA], writes=taa)
                    else:
                        P.op("act", lambda e, ob=ob, ta=ta: e.activation(
                            out=ta[64:128, :], in_=PS[ob][64:128, :], func=AF.Ln),
                            reads=bank_atoms(ob), writes=taa)
                    P.op("act", lambda e, ta=ta: e.activation(out=ta[64:128, :], in_=ta[64:128, :], func=AF.Exp, scale=-1.0),
                         reads=taa, writes=taa)
                    yo = h["out"](Qi)
                    P.op("dve", lambda e, yo=yo, ob=ob, ta=ta: e.tensor_tensor(
                        out=yo, in0=PS[ob][0:64, :], in1=ta[64:128, :], op=ALU.mult),
                        reads=bank_atoms(ob) + taa, writes=h["out_atoms"](Qi))

    def proj_fm(s_slot, wv, ncols0, dst_fn, dst_atoms_fn, scale, eng):
        for Qi in range(4):
            bk = PBANKS[pa["p"] % 2]
            pa["p"] += 1

            def mm(e, bk=bk, Qi=Qi):
                r = None
                for kc in range(KC):
                    r = e.matmul(PS[bk][:, :], lhsT=wv[:, kc, ncols0:ncols0 + 128], rhs=XH[:, kc, Qi * 512:(Qi + 1) * 512],
                                 start=(kc == 0), stop=(kc == KC - 1))
                return r
            P.op("pe", mm, reads=WSA[s_slot] + xa(range(KC), Qi * 512, (Qi + 1) * 512), writes=bank_atoms(bk))
            dst = dst_fn(Qi)
            if eng == "act":
                P.op("act", lambda e, dst=dst, bk=bk: e.activation(out=dst, in_=PS[bk][:, :], func=AF.Copy, scale=scale),
                     reads=bank_atoms(bk), writes=dst_atoms_fn(Qi))
            else:
                P.op("dve", lambda e, dst=dst, bk=bk: e.tensor_copy(out=dst, in_=PS[bk][:, :]),
                     reads=bank_atoms(bk), writes=dst_atoms_fn(Qi))

    def phase_a_dil(l, g):
        s = load_w(l, f"dil{g}")
        wv = WS[s][:, 0:3072].rearrange("p (k n) -> p k n", n=384)
        QO, KO, VO, PO = 0, 2048, 4096, 8192
        proj_fm(s, wv, 0, lambda Qi: scr_bf(QO + Qi * 512, 512), lambda Qi: scr_at(QO + Qi * 512, 512), 0.125, "act")
        proj_fm(s, wv, 128, lambda Qi: scr_bf(KO + Qi * 512, 512), lambda Qi: scr_at(KO + Qi * 512, 512), 1.0, "dve")
        V1 = scr_bf(VO, 4096).rearrange("p (t h d) -> p t h d", h=2, d=128)
        V1a = scr_at(VO, 4096)
        P.op("dve", lambda e: e.memset(V1[:, :, :, 64:128], 1.0), writes=V1a)
        for tg in range(4):
            bk = PBANKS[pa["p"] % 2]
            pa["p"] += 1

            def mm(e, bk=bk, tg=tg):
                r = None
                for tb in range(4 * tg, 4 * tg + 4):
                    for kc in range(KC):
                        r = e.matmul(PS[bk][:, (tb % 4) * 128:(tb % 4) * 128 + 128], lhsT=XH[:, kc, tb * 128:(tb + 1) * 128],
                                     rhs=wv[:, kc, 256:384], start=(kc == 0), stop=(kc == KC - 1))
                return r
            P.op("pe", mm, reads=WSA[s] + xa(range(KC), tg * 512, (tg + 1) * 512), writes=bank_atoms(bk))
            dst = V1[:, 4 * tg:4 * tg + 4, :, 0:64]
            src = PS[bk][:, :].rearrange("p (t h d) -> p t h d", h=2, d=64)
            P.op("act", lambda e, dst=dst, src=src: e.activation(out=dst, in_=src, func=AF.Copy),
                 reads=bank_atoms(bk), writes=scr_at(VO + tg * 1024, 1024))
        heads = []
        for j in range(2):
            pb = 64 * j
            heads.append(dict(
                qT=lambda Qi, pb=pb: SCR[pb:pb + 64, QO + Qi * 512:QO + (Qi + 1) * 512],
                q_atoms=lambda Qi: scr_at(QO + Qi * 512, 512),
                kT=lambda kb, pb=pb: SCR[pb:pb + 64, KO + kb * 128:KO + (kb + 1) * 128],
                k_atoms=lambda kb: scr_at(KO + kb * 128, 128),
                v1=lambda kb, j=j: V1[:, kb, j, :],
                v_atoms=lambda kb: scr_at(VO + kb * 256, 256),
                sink=None,
                out=lambda Qi, pb=pb: YC[pb:pb + 64, g, Qi * 512:(Qi + 1) * 512],
                out_atoms=lambda Qi: ya(YCA, [g], Qi * 512, (Qi + 1) * 512)))
        attention_units(heads, MASKD, 19 * 128, lambda Qi: 0, PO)

    def phase_a_swa(l, kv):
        s = load_w(l, f"swa{kv}")
        wv = WS[s][:, 0:3584].rearrange("p (k n) -> p k n", n=448)
        QO, KO, VO, PO = 0, 4096, 6144, 8192
        for ch in range(2):
            proj_fm(s, wv, 128 * ch, lambda Qi, ch=ch: scr_bf(QO + ch * 2048 + Qi * 512, 512),
                    lambda Qi, ch=ch: scr_at(QO + ch * 2048 + Qi * 512, 512), 0.125, "act")
        proj_fm(s, wv, 256, lambda Qi: scr_bf(KO + Qi * 512, 512), lambda Qi: scr_at(KO + Qi * 512, 512), 1.0, "dve")
        V1 = scr_bf(VO, 2048).rearrange("p (t d) -> p t d", d=128)
        V1a = scr_at(VO, 2048)
        P.op("dve", lambda e: e.memset(V1[:, :, 64:128], 1.0), writes=V1a)
        for tg in range(2):
            bk = PBANKS[pa["p"] % 2]
            pa["p"] += 1

            def mm(e, bk=bk, tg=tg):
                r = None
                for tb in range(8 * tg, 8 * tg + 8):
                    for kc in range(KC):
                        r = e.matmul(PS[bk][:, (tb % 8) * 64:(tb % 8) * 64 + 64], lhsT=XH[:, kc, tb * 128:(tb + 1) * 128],
                                     rhs=wv[:, kc, 384:448], start=(kc == 0), stop=(kc == KC - 1))
                return r
            P.op("pe", mm, reads=WSA[s] + xa(range(KC), tg * 1024, (tg + 1) * 1024), writes=bank_atoms(bk))
            dst = V1[:, 8 * tg:8 * tg + 8, 0:64]
            src = PS[bk][:, :].rearrange("p (t d) -> p t d", d=64)
            P.op("act", lambda e, dst=dst, src=src: e.activation(out=dst, in_=src, func=AF.Copy),
                 reads=bank_atoms(bk), writes=scr_at(VO + tg * 1024, 1024))
        heads = []
        for j in range(4):
            pb = 64 * (j % 2)
            ch = j // 2
            hidx = 4 * kv + j
            cidx = 2 * kv + ch
            heads.append(dict(
                qT=lambda Qi, pb=pb, ch=ch: SCR[pb:pb + 64, QO + ch * 2048 + Qi * 512:QO + ch * 2048 + (Qi + 1) * 512],
                q_atoms=lambda Qi, ch=ch: scr_at(QO + ch * 2048 + Qi * 512, 512),
                kT=lambda kb, pb=pb: SCR[pb:pb + 64, KO + kb * 128:KO + (kb + 1) * 128],
                k_atoms=lambda kb: scr_at(KO + kb * 128, 128),
                v1=lambda kb: V1[:, kb, :],
                v_atoms=lambda kb: scr_at(VO + kb * 128, 128),
                sink=DER[64:128, 48 + hidx:48 + hidx + 1],
                out=lambda Qi, pb=pb, cidx=cidx: YB[pb:pb + 64, cidx, Qi * 512:(Qi + 1) * 512],
                out_atoms=lambda Qi, cidx=cidx: ya(YBA, [cidx], Qi * 512, (Qi + 1) * 512)))
        attention_units(heads, MASKS, 8 * 128, lambda Qi: max(0, 4 * Qi - 1), PO)

    HHO, YAO, MGO, OSO = 0, 5632, 7680, 9728
    Z1O, Z2O = 0, 5632

    def scr_f32(off_el, ncol):
        return SCR[:, off_el:off_el + 2 * ncol].bitcast(F32)

    SBANK1, SBANK2 = 6, 7

    def mm_group(hb, lhs_fn, rhs_fn, nk):
        def mm(e):
            r = None
            for k in range(nk):
                r = e.matmul(PSH(hb), lhsT=lhs_fn(k), rhs=rhs_fn(k), start=(k == 0), stop=(k == nk - 1))
            return r
        return mm

    def layer_norm_tile(l, i, zoff, gk, bk_, is_last_out, s):
        t0 = i * TT
        za = scr_at(zoff, 4096)
        mean, meana = tmp_fixed(NTMP - 2)
        P.op("act", lambda e: e.activation(out=mean[:, 0:256], in_=PS[SBANK1][:, 0:256], func=AF.Copy, scale=1.0 / D),
             reads=bank_atoms(SBANK1), writes=[meana])
        msq, msqa = tmp()
        P.op("dve", lambda e: e.scalar_tensor_tensor(out=msq[:, 0:256], in0=mean[:, 0:256], scalar=-1.0, in1=mean[:, 0:256],
                                                     op0=ALU.mult, op1=ALU.mult),
             reads=[meana], writes=[msqa])
        var, vara = tmp()
        P.op("dve", lambda e: e.scalar_tensor_tensor(out=var[:, 0:256], in0=PS[SBANK2][:, 0:256], scalar=1.0 / D,
                                                     in1=msq[:, 0:256], op0=ALU.mult, op1=ALU.add),
             reads=bank_atoms(SBANK2) + [msqa], writes=[vara])
        P.op("pool", lambda e: e.tensor_scalar(out=var[:, 0:256], in0=var[:, 0:256], scalar1=LN_EPS, scalar2=None, op0=ALU.add),
             reads=[vara], writes=[vara])
        rstd, rstda = tmp_fixed(NTMP - 1)
        P.op("pool", lambda e: e.tensor_tensor(out=rstd[:, 0:256], in0=var[:, 0:256], in1=NHALF[:, :], op=ALU.pow),
             reads=[vara, CONSTA], writes=[rstda])
        for c in range(KC):
            z = scr_f32(zoff + c * 512, 256)
            zc = scr_at(zoff + c * 512, 512)
            u, ua = tmp()
            P.op("dve", lambda e, u=u, z=z: e.tensor_tensor(out=u[:, 0:256], in0=z, in1=mean[:, 0:256], op=ALU.subtract),
                 reads=zc + [meana], writes=[ua])
            P.op("dve", lambda e, u=u: e.tensor_tensor(out=u[:, 0:256], in0=u[:, 0:256], in1=rstd[:, 0:256], op=ALU.mult),
                 reads=[ua, rstda], writes=[ua])
            gcol = pcv(l, gk + c)
            bcol = pcv(l, bk_ + c)
            if is_last_out:
                v = scr_f32(OSO, 256)
                va = scr_at(OSO, 512)
            else:
                v, va1 = tmp()
                v = v[:, 0:256]
                va = [va1]
            P.op("act", lambda e, v=v, u=u, gcol=gcol, bcol=bcol: e.activation(out=v, in_=u[:, 0:256], func=AF.Identity,
                                                                              scale=gcol, bias=bcol),
                 reads=[ua, CONSTA], writes=va)
            if is_last_out:
                dst = oT[s, c * 128:(c + 1) * 128, t0:t0 + TT]
                P.dma("sp", lambda e, v=v, dst=dst: e.dma_start(out=dst, in_=v), "os0", reads=va, writes=[])
            else:
                hi = XH[:, c, t0:t0 + TT]
                lo = XL[:, c, t0:t0 + TT]
                xat = xa([c], t0, t0 + TT)
                P.op("act", lambda e, hi=hi, v=v: e.activation(out=hi, in_=v, func=AF.Copy), reads=va, writes=xat)
                P.op("pool", lambda e, lo=lo, hi=hi, v=v: e.tensor_tensor(out=lo, in0=v, in1=hi, op=ALU.subtract),
                     reads=va + xat, writes=xat)

    def residual_and_stats(i, c, hz, zoff):
        t0 = i * TT
        xf, xfa = tmp()
        xat = xa([c], t0, t0 + TT)
        P.op("pool", lambda e, xf=xf: e.tensor_tensor(out=xf[:, 0:256], in0=XH[:, c, t0:t0 + TT], in1=XL[:, c, t0:t0 + TT],
                                                      op=ALU.add),
             reads=xat, writes=[xfa])
        z = scr_f32(zoff + c * 512, 256)
        zc = scr_at(zoff + c * 512, 512)
        P.op("dve", lambda e, xf=xf, z=z: e.scalar_tensor_tensor(out=z, in0=xf[:, 0:256], scalar=ALPHA, in1=PSH(hz),
                                                                 op0=ALU.mult, op1=ALU.add),
             reads=[xfa, PSA[hz]], writes=zc)
        sq, sqa = tmp()
        P.op("act", lambda e, sq=sq, z=z: e.activation(out=sq[:, 0:256], in_=z, func=AF.Square), reads=zc, writes=[sqa])
        P.op("pe", lambda e, z=z: e.matmul(PS[SBANK1][:, 0:256], lhsT=ONES32[:, :], rhs=z, start=(c == 0), stop=(c == KC - 1)),
             reads=zc + [CONSTA], writes=bank_atoms(SBANK1))
        P.op("pe", lambda e, sq=sq: e.matmul(PS[SBANK2][:, 0:256], lhsT=ONES32[:, :], rhs=sq[:, 0:256], start=(c == 0),
                                             stop=(c == KC - 1)),
             reads=[sqa, CONSTA], writes=bank_atoms(SBANK2))

    def phase_c_tile(l, i, is_last_out, s):
        t0 = i * TT
        XHt = lambda kc: XH[:, kc, t0:t0 + TT]
        xall = xa(range(KC), t0, t0 + TT)
        YAv = scr_bf(YAO, 2048).rearrange("p (c t) -> p c t", t=256)
        MGv = scr_bf(MGO, 2048).rearrange("p (c t) -> p c t", t=256)
        HHv = scr_bf(HHO, 5632).rearrange("p (j t) -> p j t", t=256)
        BDv = BD[:, :].rearrange("p (g c n) -> p g c n", g=2, c=8)
        lru_proj = {}

        def issue_lru_proj(c):
            if c % 2 == 0:
                lru_proj["sl"] = load_w(l, f"lru{c // 2}")
            sl = lru_proj["sl"]
            w4 = WS[sl][:, 0:4096].rearrange("p (k n) -> p k n", n=512)
            base = (c % 2) * 256
            hx = next_hb()
            P.op("pe", mm_group(hx, lambda k, w4=w4, base=base: w4[:, k, base:base + 128], XHt, KC),
                 reads=WSA[sl] + xall, writes=[PSA[hx]])
            hg = next_hb()
            P.op("pe", mm_group(hg, lambda k, w4=w4, base=base: w4[:, k, base + 128:base + 256], XHt, KC),
                 reads=WSA[sl] + xall, writes=[PSA[hg]])
            lru_proj[c] = (hx, hg)

        issue_lru_proj(0)
        for c in range(KC):
            hx, hg = lru_proj[c]
            lx, lxa = tmp()
            if i == 0:
                P.op("dve", lambda e, lx=lx: e.memset(lx[:, 0:3], 0.0), writes=[lxa])
            else:
                P.op("act", lambda e, lx=lx, c=c: e.activation(out=lx[:, 0:3], in_=CARRY[:, c, :], func=AF.Copy),
                     reads=[CARRYA[c]], writes=[lxa])
            P.op("act", lambda e, lx=lx, hx=hx: e.activation(out=lx[:, 3:259], in_=PSH(hx), func=AF.Copy),
                 reads=[PSA[hx]], writes=[lxa])
            P.op("act", lambda e, lx=lx, c=c: e.activation(out=CARRY[:, c, :], in_=lx[:, 256:259], func=AF.Copy),
                 reads=[lxa], writes=[CARRYA[c]])
            xg, xga = tmp()
            P.op("act", lambda e, xg=xg, hg=hg: e.activation(out=xg[:, 0:256], in_=PSH(hg), func=AF.Copy),
                 reads=[PSA[hg]], writes=[xga])
            xc, xca = tmp()
            P.op("dve", lambda e, xc=xc, lx=lx, c=c: e.tensor_scalar(out=xc[:, 0:256], in0=lx[:, 3:259], scalar1=pcv(l, 24 + c),
                                                                     scalar2=pcv(l, 32 + c), op0=ALU.mult, op1=ALU.add),
                 reads=[lxa, CONSTA], writes=[xca])
            for jj in (2, 1, 0):
                P.op("dve", lambda e, xc=xc, lx=lx, c=c, jj=jj: e.scalar_tensor_tensor(
                    out=xc[:, 0:256], in0=lx[:, jj:jj + 256], scalar=pcv(l, 8 * jj + c), in1=xc[:, 0:256],
                    op0=ALU.mult, op1=ALU.add),
                    reads=[lxa, xca, CONSTA], writes=[xca])
            xcb, xcba = tmpb()
            P.op("act", lambda e, xcb=xcb, xc=xc: e.activation(out=xcb, in_=xc[:, 0:256], func=AF.Copy),
                 reads=[xca], writes=[xcba])
            if c + 1 < KC:
                issue_lru_proj(c + 1)
            hr = next_hb()
            P.op("pe", lambda e, hr=hr, c=c, xcb=xcb: e.matmul(PSH(hr), lhsT=BDv[:, 0, c, :], rhs=xcb, start=True, stop=True),
                 reads=BDA + [xcba], writes=[PSA[hr]])
            hi_ = next_hb()
            P.op("pe", lambda e, hi_=hi_, c=c, xcb=xcb: e.matmul(PSH(hi_), lhsT=BDv[:, 1, c, :], rhs=xcb, start=True, stop=True),
                 reads=BDA + [xcba], writes=[PSA[hi_]])
            tr, tra = tmp()
            P.op("act", lambda e, tr=tr, hr=hr, c=c: e.activation(out=tr[:, 0:256], in_=PSH(hr), func=AF.Tanh, scale=0.5,
                                                                 bias=DER[:, 32 + c:33 + c]),
                 reads=[PSA[hr], DERA], writes=[tra])
            ti, tia = tmp()
            P.op("act", lambda e, ti=ti, hi_=hi_, c=c: e.activation(out=ti[:, 0:256], in_=PSH(hi_), func=AF.Tanh, scale=0.5,
                                                                   bias=DER[:, 40 + c:41 + c]),
                 reads=[PSA[hi_], DERA], writes=[tia])
            av, ava = tmp()
            P.op("act", lambda e, av=av, tr=tr, c=c: e.activation(out=av[:, 0:256], in_=tr[:, 0:256], func=AF.Exp,
                                                                 scale=DER[:, 24 + c:25 + c], bias=DER[:, 24 + c:25 + c]),
                 reads=[tra, DERA], writes=[ava])
            a2, a2a = tmp()
            P.op("act", lambda e, a2=a2, tr=tr, c=c: e.activation(out=a2[:, 0:256], in_=tr[:, 0:256], func=AF.Exp,
                                                                 scale=DER[:, 16 + c:17 + c], bias=DER[:, 16 + c:17 + c]),
                 reads=[tra, DERA], writes=[a2a])
            P.op("pool", lambda e, a2=a2: e.tensor_scalar(out=a2[:, 0:256], in0=a2[:, 0:256], scalar1=-1.0, scalar2=1.0,
                                                          op0=ALU.mult, op1=ALU.add),
                 reads=[a2a], writes=[a2a])
            P.op("pool", lambda e, a2=a2: e.tensor_tensor(out=a2[:, 0:256], in0=a2[:, 0:256], in1=HALF[:, :], op=ALU.pow),
                 reads=[a2a, CONSTA], writes=[a2a])
            P.op("dve", lambda e, ti=ti, xc=xc: e.scalar_tensor_tensor(out=ti[:, 0:256], in0=ti[:, 0:256], scalar=1.0,
                                                                       in1=xc[:, 0:256], op0=ALU.add, op1=ALU.mult),
                 reads=[tia, xca], writes=[tia])
            P.op("dve", lambda e, ti=ti, a2=a2: e.scalar_tensor_tensor(out=ti[:, 0:256], in0=ti[:, 0:256], scalar=0.5,
                                                                       in1=a2[:, 0:256], op0=ALU.mult, op1=ALU.mult),
                 reads=[tia, a2a], writes=[tia])
            hh, hha = tmp()
            if i == 0:
                P.op("dve", lambda e, hh=hh, av=av, ti=ti: e.tensor_tensor_scan(
                    out=hh[:, 0:256], data0=av[:, 0:256], data1=ti[:, 0:256], initial=0.0, op0=ALU.mult, op1=ALU.add),
                    reads=[ava, tia], writes=[hha])
            else:
                P.op("dve", lambda e, hh=hh, av=av, ti=ti, c=c: e.tensor_tensor_scan(
                    out=hh[:, 0:256], data0=av[:, 0:256], data1=ti[:, 0:256], initial=HC[:, c:c + 1], op0=ALU.mult,
                    op1=ALU.add),
                    reads=[ava, tia, HCA[c]], writes=[hha])
            P.op("act", lambda e, hh=hh, c=c: e.activation(out=HC[:, c:c + 1], in_=hh[:, 255:256], func=AF.Copy),
                 reads=[hha], writes=[HCA[c]])
            x2, x2a = tmp()
            P.op("pool", lambda e, x2=x2, xg=xg: e.tensor_tensor(out=x2[:, 0:256], in0=xg[:, 0:256], in1=xg[:, 0:256], op=ALU.mult),
                 reads=[xga], writes=[x2a])
            P.op("pool", lambda e, x2=x2: e.tensor_scalar(out=x2[:, 0:256], in0=x2[:, 0:256], scalar1=0.044715, scalar2=1.0,
                                                          op0=ALU.mult, op1=ALU.add),
                 reads=[x2a], writes=[x2a])
            P.op("pool", lambda e, x2=x2, xg=xg: e.tensor_tensor(out=x2[:, 0:256], in0=x2[:, 0:256], in1=xg[:, 0:256], op=ALU.mult),
                 reads=[x2a, xga], writes=[x2a])
            P.op("act", lambda e, x2=x2: e.activation(out=x2[:, 0:256], in_=x2[:, 0:256], func=AF.Tanh, scale=0.7978845608028654),
                 reads=[x2a], writes=[x2a])
            P.op("dve", lambda e, x2=x2, xg=xg: e.scalar_tensor_tensor(out=x2[:, 0:256], in0=x2[:, 0:256], scalar=1.0,
                                                                       in1=xg[:, 0:256], op0=ALU.add, op1=ALU.mult),
                 reads=[x2a, xga], writes=[x2a])
            P.op("dve", lambda e, x2=x2, hh=hh, c=c: e.scalar_tensor_tensor(out=YAv[:, c, :], in0=x2[:, 0:256], scalar=0.5,
                                                                            in1=hh[:, 0:256], op0=ALU.mult, op1=ALU.mult),
                 reads=[x2a, hha], writes=scr_at(YAO + c * 256, 256))
        Ys = [lambda k: YAv[:, k, :], lambda k: YB[:, k, t0:t0 + TT], lambda k: YC[:, k, t0:t0 + TT]]
        Yat = [scr_at(YAO, 2048), ya(YBA, range(KC), t0, t0 + TT), ya(YCA, range(KC), t0, t0 + TT)]
        for c in range(KC):
            sg = load_w(l, f"gate{c}")
            g3 = WS[sg][:, 0:3072].rearrange("p (k n) -> p k n", n=384)
            sbr = load_w(l, f"br{c}")
            b3 = WS[sbr][:, 0:3072].rearrange("p (k n) -> p k n", n=384)
            acc = None
            for n in range(3):
                hgn = next_hb()
                P.op("pe", mm_group(hgn, lambda k, g3=g3, n=n: g3[:, k, 128 * n:128 * n + 128], XHt, KC),
                     reads=WSA[sg] + xall, writes=[PSA[hgn]])
                hbn = next_hb()
                P.op("pe", mm_group(hbn, lambda k, b3=b3, n=n: b3[:, k, 128 * n:128 * n + 128], Ys[n], KC),
                     reads=WSA[sbr] + Yat[n], writes=[PSA[hbn]])
                sgm, sga = tmp()
                P.op("act", lambda e, sgm=sgm, hgn=hgn: e.activation(out=sgm[:, 0:256], in_=PSH(hgn), func=AF.Sigmoid),
                     reads=[PSA[hgn]], writes=[sga])
                P.op("dve", lambda e, sgm=sgm, hbn=hbn: e.tensor_tensor(out=sgm[:, 0:256], in0=sgm[:, 0:256], in1=PSH(hbn),
                                                                        op=ALU.mult),
                     reads=[sga, PSA[hbn]], writes=[sga])
                if n == 0:
                    acc, acca = sgm, sga
                elif n == 1:
                    P.op("pool", lambda e, acc=acc, sgm=sgm: e.tensor_tensor(out=acc[:, 0:256], in0=acc[:, 0:256],
                                                                             in1=sgm[:, 0:256], op=ALU.add),
                         reads=[acca, sga], writes=[acca])
                else:
                    P.op("pool", lambda e, acc=acc, sgm=sgm, c=c: e.tensor_tensor(out=MGv[:, c, :], in0=acc[:, 0:256],
                                                                                  in1=sgm[:, 0:256], op=ALU.add),
                         reads=[acca, sga], writes=scr_at(MGO + c * 256, 256))
        if KDUMP and s == 0 and l == 0:
            P.dma("sp", lambda e: e.dma_start(out=dbg_ya[:, :, t0:t0 + TT], in_=YAv), "dbg%d" % next(_dbgc), reads=scr_at(YAO, 2048), writes=[])
            P.dma("sp", lambda e: e.dma_start(out=dbg_mg[:, :, t0:t0 + TT], in_=MGv), "dbg%d" % next(_dbgc), reads=scr_at(MGO, 2048), writes=[])
        mga = scr_at(MGO, 2048)
        for o in range(2):
            so = load_w(l, f"out{o}")
            w4 = WS[so][:, 0:4096].rearrange("p (k n) -> p k n", n=512)
            for cc in range(4):
                c = 4 * o + cc
                hz = next_hb()
                P.op("pe", mm_group(hz, lambda k, w4=w4, cc=cc: w4[:, k, 128 * cc:128 * cc + 128], lambda k: MGv[:, k, :], KC),
                     reads=WSA[so] + mga, writes=[PSA[hz]])
                residual_and_stats(i, c, hz, Z1O)
        layer_norm_tile(l, i, Z1O, 64, 72, False, s)
        for jp in range(11):
            sf = load_w(l, f"ffi{jp}")
            w4 = WS[sf][:, 0:4096].rearrange("p (k n) -> p k n", n=512)
            for jj in range(2):
                j = 2 * jp + jj
                h1 = next_hb()
                P.op("pe", mm_group(h1, lambda k, w4=w4, jj=jj: w4[:, k, 256 * jj:256 * jj + 128], XHt, KC),
                     reads=WSA[sf] + xall, writes=[PSA[h1]])
                h3 = next_hb()
                P.op("pe", mm_group(h3, lambda k, w4=w4, jj=jj: w4[:, k, 256 * jj + 128:256 * jj + 256], XHt, KC),
                     reads=WSA[sf] + xall, writes=[PSA[h3]])
                sl_, sla = tmp()
                P.op("act", lambda e, sl_=sl_, h1=h1: e.activation(out=sl_[:, 0:256], in_=PSH(h1), func=AF.Silu),
                     reads=[PSA[h1]], writes=[sla])
                P.op("dve", lambda e, sl_=sl_, h3=h3, j=j: e.tensor_tensor(out=HHv[:, j, :], in0=sl_[:, 0:256], in1=PSH(h3),
                                                                          op=ALU.mult),
                     reads=[sla, PSA[h3]], writes=scr_at(HHO + j * 256, 256))
        hha_all = scr_at(HHO, 5632)
        for c in range(KC):
            so = load_w(l, f"ffo{c}")
            w22 = WS[so][:, 0:2816].rearrange("p (j n) -> p j n", n=128)
            hz = next_hb()
            P.op("pe", mm_group(hz, lambda k, w22=w22: w22[:, k, :], lambda k: HHv[:, k, :], NJ),
                 reads=WSA[so] + hha_all, writes=[PSA[hz]])
            residual_and_stats(i, c, hz, Z2O)
        layer_norm_tile(l, i, Z2O, 80, 88, is_last_out, s)

    import os
    PH = os.environ.get("KDEBUG", "xksdc")
    NSQ = int(os.environ.get("KNSEQ", NSEQ))
    convert_blocks(0, 0, BLOB_LEN)
    per_tile = (BLOB_LEN + NTL - 1) // NTL
    for s in range(NSQ):
        if "x" in PH:
            load_x(s)
        for l in range(n_layers):
            if "k" in PH:
                layer_consts(l)
            if s == 0 and l + 1 < n_layers:
                convert_blocks(l + 1, 0, BLOB_LEN)
            if "s" in PH:
                for kv in range(int(os.environ.get("KNSWA", 4))):
                    phase_a_swa(l, kv)
            if "d" in PH:
                for g in range(int(os.environ.get("KNDIL", 8))):
                    phase_a_dil(l, g)
            if KDUMP and s == 0 and l == 0:
                P.dma("sp", lambda e: e.dma_start(out=dbg_yb[:, :, :], in_=YB[:, :, :]), "dbg",
                      reads=ya(YBA, range(KC), 0, T), writes=[])
                P.dma("sp", lambda e: e.dma_start(out=dbg_yc[:, :, :], in_=YC[:, :, :]), "dbg",
                      reads=ya(YCA, range(KC), 0, T), writes=[])
                P.dma("sp", lambda e: e.dma_start(out=dbg_xh[:, :, :], in_=XH[:, :, :]), "dbg",
                      reads=xa(range(KC), 0, T), writes=[])
            if "c" in PH:
                for i in range(int(os.environ.get("KNTL", NTL))):
                    phase_c_tile(l, i, last_flags[l], s)
                if KDUMP and s == 0 and l == 0:
                    P.dma("sp", lambda e: e.dma_start(out=dbg_x1[:, :, :], in_=XH[:, :, :]), "dbg%d" % next(_dbgc),
                          reads=xa(range(KC), 0, T), writes=[])

    total_const = P.dma_count.get("const", 0)
    for e_ in P.ENGS:
        for op in P.ops[e_]:
            if op.isdma and op.semkey == "const":
                op.semval = total_const
    for e_ in P.ENGS:
        cnt = 0
        for op in P.ops[e_]:
            if op.sig and not op.isdma:
                cnt += 1
                op.sigval = cnt
    sem_names = ["e_" + e_ for e_ in P.ENGS] + ["d_" + k for k in sorted(P.dma_count)]
    sems = {n: es.enter_context(nc.semaphore(n)) for n in sem_names}
    handles = {"pe": nc.tensor, "act": nc.scalar, "dve": nc.vector, "pool": nc.gpsimd, "sp": nc.sync}

    def emit(eng, e):
        waited = {}
        for op in P.ops[eng]:
            need = {}
            for d in op.deps:
                if d.isdma:
                    key, val = "d_" + d.semkey, d.semval
                else:
                    key, val = "e_" + d.eng, d.sigval
                if need.get(key, 0) < val:
                    need[key] = val
            for key, val in need.items():
                if waited.get(key, 0) < val:
                    e.wait_ge(sems[key], val)
                    waited[key] = val
            inst = op.fn(e)
            if op.isdma:
                inst.then_inc(sems["d_" + op.semkey], 16)
            elif op.sig:
                inst.then_inc(sems["e_" + eng], 1)
        if eng == "sp":
            for k_, v_ in sorted(P.dma_count.items()):
                e.wait_ge(sems["d_" + k_], v_)

    block = es.enter_context(nc.Block())

    @block.tensor
    def _(e):
        emit("pe", e)

    @block.scalar
    def _(e):
        emit("act", e)

    @block.vector
    def _(e):
        emit("dve", e)

    @block.gpsimd
    def _(e):
        emit("pool", e)

    @block.sync
    def _(e):
        emit("sp", e)

    es.close()
    stats = {k: len(v) for k, v in P.ops.items()}
    return nc, stats


_CACHE = {}


def _get_prog(n_layers, last_flags):
    key = (n_layers, tuple(last_flags))
    if key not in _CACHE:
        _CACHE[key] = build_program(n_layers, list(last_flags))
    return _CACHE[key][0]


def kernel(x, w_in, conv_w, conv_b, w_rg, b_rg, w_ig, b_ig, lru_lambda, sinks, w_branch, w_out,
           ln1_g, ln1_b, w_ffn_in, w_ffn_out, ln2_g, ln2_b):
    f = lambda a: np.asarray(a, dtype=np.float32)
    x = f(x); w_in = f(w_in); conv_w = f(conv_w); conv_b = f(conv_b); w_rg = f(w_rg); b_rg = f(b_rg)
    w_ig = f(w_ig); b_ig = f(b_ig); lru_lambda = f(lru_lambda); sinks = f(sinks); w_branch = f(w_branch)
    w_out = f(w_out); ln1_g = f(ln1_g); ln1_b = f(ln1_b); w_ffn_in = f(w_ffn_in); w_ffn_out = f(w_ffn_out)
    ln2_g = f(ln2_g); ln2_b = f(ln2_b)
    maskd, masks = _build_masks()
    blobs = [_build_blob(l, w_in, w_rg, w_ig, w_branch, w_out, w_ffn_in, w_ffn_out) for l in range(DEPTH)]
    pcols = [_build_pcol(l, conv_w, conv_b, b_rg, b_ig, lru_lambda, sinks, ln1_g, ln1_b, ln2_g, ln2_b)
             for l in range(DEPTH)]
    xT = np.ascontiguousarray(x.transpose(0, 2, 1))
    cur = [np.ascontiguousarray(xT[NSEQ * c:NSEQ * (c + 1)]) for c in range(NCORES)]
    if FUSED:
        nc = _get_prog(DEPTH, [False] * (DEPTH - 1) + [True])
        blob = np.ascontiguousarray(np.stack(blobs, 0))
        pc = np.ascontiguousarray(np.concatenate(pcols, axis=1))
        in_maps = [{"xT": cur[c], "blob": blob, "pcol": pc, "maskd": maskd, "masks": masks} for c in range(NCORES)]
        res = run_bass_kernel_spmd(nc, in_maps, core_ids=list(range(NCORES)))
        cur = [np.asarray(res.results[c]["oT"]) for c in range(NCORES)]
    else:
        nc = _get_prog(1, [True])
        for l in range(DEPTH):
            blob = np.ascontiguousarray(blobs[l][None])
            in_maps = [{"xT": cur[c], "blob": blob, "pcol": pcols[l], "maskd": maskd, "masks": masks}
                       for c in range(NCORES)]
            res = run_bass_kernel_spmd(nc, in_maps, core_ids=list(range(NCORES)))
            cur = [np.asarray(res.results[c]["oT"]) for c in range(NCORES)]
    outT = np.concatenate(cur, axis=0)
    return np.ascontiguousarray(outT.transpose(0, 2, 1)).astype(np.float32)
```
